# Optimizing a Trainium2 kernel written in Bass

```python
import math, functools
import jax, jax.numpy as jnp
from jax import lax
import numpy as np

D_MODEL = 2048
BATCH = 2
SEQ = 8192
DEPTH = 4

MLA_HEADS = 8
MLA_Q_RANK = 768
MLA_KV_RANK = 512
MLA_NOPE = 128
MLA_ROPE = 64
MLA_V = 128
ROPE_THETA = 10000.0
ATTN_BLOCK = 128
GDN_HEADS = 8
GDN_DK = 128
GDN_DV = 128
GDN_CONV = 4
GDN_CHUNK = 64
RWKV_HEADS = 16
RWKV_HEAD = 64
RWKV_W_RANK = 64
RWKV_A_RANK = 64
RWKV_G_RANK = 128
RWKV_LN_EPS = 64e-5
D_FF = -(-8 * D_MODEL // (3 * 256)) * 256
NORM_EPS = 1e-6
MAX_POS_OFFSET = 4096

MLA_IN = MLA_Q_RANK + MLA_KV_RANK + MLA_ROPE
GDN_QKV = GDN_HEADS * (2 * GDN_DK + GDN_DV)
GDN_IN = GDN_QKV + GDN_HEADS * GDN_DV + 2 * GDN_HEADS
RWKV_WIDTH = RWKV_HEADS * RWKV_HEAD
RWKV_IN = 3 * RWKV_WIDTH + RWKV_W_RANK + RWKV_A_RANK + RWKV_G_RANK
GATE_IN = 3 * D_MODEL
IN_WIDTH = MLA_IN + GDN_IN + RWKV_IN + GATE_IN
MLA_OUT = MLA_HEADS * MLA_V
GDN_OUT = GDN_HEADS * GDN_DV
RWKV_OUT = RWKV_WIDTH
MIX_WIDTH = MLA_OUT + GDN_OUT + RWKV_OUT

kernel_name = 'hybrid_mla_gdn_rwkv7_adaln_block'


def rmsnorm(x, w, eps=NORM_EPS):
    xf = x.astype(jnp.float32)
    y = xf * lax.rsqrt(jnp.mean(xf * xf, axis=-1, keepdims=True) + eps)
    return (y * w.astype(jnp.float32)).astype(x.dtype)


def l2norm(x, eps=1e-6):
    xf = x.astype(jnp.float32)
    return xf * lax.rsqrt(jnp.sum(xf * xf, axis=-1, keepdims=True) + eps)


def modulate(x, norm_w, shift, scale):
    return rmsnorm(x, norm_w) * (1.0 + scale) + shift


def rope_tables(positions):
    inv = 1.0 / (ROPE_THETA ** (jnp.arange(0, MLA_ROPE, 2, dtype=jnp.float32) / MLA_ROPE))
    ang = positions.astype(jnp.float32)[..., None] * inv
    return jnp.cos(ang), jnp.sin(ang)


def apply_rope(x, cos, sin):
    x1, x2 = jnp.split(x.astype(jnp.float32), 2, axis=-1)
    return jnp.concatenate([x1 * cos - x2 * sin, x1 * sin + x2 * cos], axis=-1).astype(x.dtype)


def mla_mixer(p_in, cos, sin, q_norm_w, w_uq, kv_norm_w, w_ukv):
    B, S, _ = p_in.shape
    H = MLA_HEADS
    c_q, c_kv, k_r = jnp.split(p_in, [MLA_Q_RANK, MLA_Q_RANK + MLA_KV_RANK], axis=-1)
    q = (rmsnorm(c_q, q_norm_w) @ w_uq).reshape(B, S, H, MLA_NOPE + MLA_ROPE)
    q_nope = q[..., :MLA_NOPE]
    q_rope = apply_rope(q[..., MLA_NOPE:], cos[:, :, None, :], sin[:, :, None, :])
    kv = (rmsnorm(c_kv, kv_norm_w) @ w_ukv).reshape(B, S, H, MLA_NOPE + MLA_V)
    k_nope, v = kv[..., :MLA_NOPE], kv[..., MLA_NOPE:]
    k_rope = apply_rope(k_r, cos, sin)
    scale = (MLA_NOPE + MLA_ROPE) ** -0.5
    nb = S // ATTN_BLOCK
    qn_b = q_nope.reshape(B, nb, ATTN_BLOCK, H, MLA_NOPE).transpose(1, 0, 2, 3, 4)
    qr_b = q_rope.reshape(B, nb, ATTN_BLOCK, H, MLA_ROPE).transpose(1, 0, 2, 3, 4)
    k_idx = jnp.arange(S)

    def block(args):
        qn, qr, start = args
        s = (jnp.einsum('bqhd,bkhd->bhqk', qn, k_nope)
             + jnp.einsum('bqhr,bkr->bhqk', qr, k_rope)).astype(jnp.float32) * scale
        q_idx = start + jnp.arange(ATTN_BLOCK)
        s = jnp.where(q_idx[:, None] >= k_idx[None, :], s, -jnp.inf)
        pr = jax.nn.softmax(s, axis=-1).astype(v.dtype)
        return jnp.einsum('bhqk,bkhd->bqhd', pr, v)

    o = lax.map(block, (qn_b, qr_b, jnp.arange(nb) * ATTN_BLOCK))
    return o.transpose(1, 0, 2, 3, 4).reshape(B, S, H * MLA_V)


def causal_dwconv(x, w):
    return lax.conv_general_dilated(x, w[:, None, :], window_strides=(1,),
                                    padding=((GDN_CONV - 1, 0),),
                                    dimension_numbers=('NWC', 'WIO', 'NWC'),
                                    feature_group_count=x.shape[-1])


def gdn_mixer(p_in, conv_w, a_log, dt_bias, norm_w):
    B, S, _ = p_in.shape
    H, DK, DV, C = GDN_HEADS, GDN_DK, GDN_DV, GDN_CHUNK
    n = S // C
    f32 = jnp.float32
    qkv, z, b, a = jnp.split(p_in, [GDN_QKV, GDN_QKV + H * DV, GDN_QKV + H * DV + H], axis=-1)
    qkv = jax.nn.silu(causal_dwconv(qkv, conv_w))
    q, k, v = jnp.split(qkv, [H * DK, 2 * H * DK], axis=-1)
    q = l2norm(q.reshape(B, S, H, DK)) * (DK ** -0.5)
    k = l2norm(k.reshape(B, S, H, DK))
    v = v.reshape(B, S, H, DV).astype(f32)
    beta = jax.nn.sigmoid(b.astype(f32))
    g = -jnp.exp(a_log.astype(f32)) * jax.nn.softplus(a.astype(f32) + dt_bias.astype(f32))

    def to_chunks(t):
        return t.reshape(B, n, C, H, t.shape[-1]).transpose(0, 3, 1, 2, 4)

    q, k, v = to_chunks(q), to_chunks(k), to_chunks(v)
    beta = beta.reshape(B, n, C, H).transpose(0, 3, 1, 2)
    G = jnp.cumsum(g.reshape(B, n, C, H).transpose(0, 3, 1, 2), axis=-1)
    idx = jnp.arange(C)
    incl = idx[:, None] >= idx[None, :]
    strict = idx[:, None] > idx[None, :]
    decay = jnp.exp(jnp.where(incl, G[..., :, None] - G[..., None, :], -jnp.inf))
    kk = jnp.einsum('bhnid,bhnjd->bhnij', k, k)
    A = jnp.where(strict, kk * decay, 0.0) * beta[..., :, None]
    T = A + jnp.eye(C, dtype=f32)
    solve = functools.partial(lax.linalg.triangular_solve, left_side=True, lower=True, unit_diagonal=True)
    U = solve(T, beta[..., None] * v)
    Wk = solve(T, (beta * jnp.exp(G))[..., None] * k)
    QK = jnp.einsum('bhnid,bhnjd->bhnij', q, k) * decay
    qg = q * jnp.exp(G)[..., None]
    kg = k * jnp.exp(G[..., -1:] - G)[..., None]
    gC = jnp.exp(G[..., -1])

    def step(state, inp):
        u_c, wk_c, qk_c, qg_c, kg_c, gc_c = inp
        w_c = u_c - jnp.einsum('bhcd,bhde->bhce', wk_c, state)
        o_c = jnp.einsum('bhcd,bhde->bhce', qg_c, state) + jnp.einsum('bhcj,bhje->bhce', qk_c, w_c)
        state = gc_c[..., None, None] * state + jnp.einsum('bhcd,bhce->bhde', kg_c, w_c)
        return state, o_c

    xs = tuple(jnp.moveaxis(t, 2, 0) for t in (U, Wk, QK, qg, kg, gC))
    _, o = lax.scan(step, jnp.zeros((B, H, DK, DV), f32), xs)
    o = o.transpose(1, 0, 3, 2, 4).reshape(B, S, H, DV)
    o = rmsnorm(o, norm_w) * jax.nn.silu(z.reshape(B, S, H, DV).astype(f32))
    return o.reshape(B, S, H * DV).astype(p_in.dtype)


def rwkv_mixer(p_in, mu, w0, w_up, a0, a_up, g_up, k_k, k_a, r_k, lnx_w, lnx_b):
    B, S, _ = p_in.shape
    H, N, W = RWKV_HEADS, RWKV_HEAD, RWKV_WIDTH
    f32 = jnp.float32
    prev = jnp.concatenate([jnp.zeros_like(p_in[:, :1]), p_in[:, :-1]], axis=1)
    xs = p_in + (prev - p_in) * mu
    r, k, v, xw, xa, xg = jnp.split(
        xs, [W, 2 * W, 3 * W, 3 * W + RWKV_W_RANK, 3 * W + RWKV_W_RANK + RWKV_A_RANK], axis=-1)
    w_log = -jax.nn.softplus(-(w0 + jnp.tanh(xw) @ w_up).astype(f32)) - 0.5
    decay = jnp.exp(-jnp.exp(w_log))
    a = jax.nn.sigmoid((a0 + xa @ a_up).astype(f32))
    g = (jax.nn.sigmoid(xg) @ g_up).astype(f32)
    r, k, v = r.astype(f32), k.astype(f32), v.astype(f32)
    kk = l2norm((k * k_k.astype(f32)).reshape(B, S, H, N))
    k = k * (1.0 + (a - 1.0) * k_a.astype(f32))

    def heads(t):
        return t.reshape(B, S, H, N)

    r, k, v, decay, a = heads(r), heads(k), heads(v), heads(decay), heads(a)
    b_vec = kk * a

    def step(state, inp):
        r_t, w_t, k_t, v_t, kk_t, b_t = inp
        sa = jnp.einsum('bhvk,bhk->bhv', state, kk_t)
        state = (state * w_t[:, :, None, :] - sa[..., None] * b_t[:, :, None, :]
                 + v_t[..., None] * k_t[:, :, None, :])
        return state, jnp.einsum('bhvk,bhk->bhv', state, r_t)

    def tm(t):
        return jnp.moveaxis(t, 1, 0)

    _, y = lax.scan(step, jnp.zeros((B, H, N, N), f32),
                    (tm(r), tm(decay), tm(k), tm(v), tm(kk), tm(b_vec)))
    y = jnp.moveaxis(y, 0, 1)
    mean = jnp.mean(y, axis=-1, keepdims=True)
    var = jnp.mean(jnp.square(y - mean), axis=-1, keepdims=True)
    yn = ((y - mean) * lax.rsqrt(var + RWKV_LN_EPS)).reshape(B, S, W)
    yn = yn * lnx_w.astype(f32) + lnx_b.astype(f32)
    bonus = (jnp.sum(r * k * r_k.astype(f32), axis=-1, keepdims=True) * v).reshape(B, S, W)
    return ((yn + bonus) * g).astype(p_in.dtype)


def setup_inputs(seed: int = 0) -> dict:
    key = jax.random.key(seed)
    ks = iter(jax.random.split(key, 40))
    f32 = jnp.float32
    L, D = DEPTH, D_MODEL

    def nrm(shape, scale):
        return jax.random.normal(next(ks), shape, f32) * scale

    def gain(shape):
        return 1.0 + nrm(shape, 0.02)

    def unif(shape, lo, hi):
        return jax.random.uniform(next(ks), shape, f32, lo, hi)

    x = nrm((BATCH, SEQ, D), 1.0)
    c = nrm((BATCH, D), 1.0)
    offset = jax.random.randint(next(ks), (BATCH, 1), 0, MAX_POS_OFFSET, dtype=jnp.int32)
    positions = offset + jnp.arange(SEQ, dtype=jnp.int32)[None, :]
    dt = jnp.exp(unif((L, GDN_HEADS), math.log(1e-3), math.log(1e-1)))
    return {
        'x': x,
        'c': c,
        'positions': positions,
        'w_ada': nrm((L, D, 6 * D), 0.5 * D ** -0.5),
        'b_ada': nrm((L, 6 * D), 0.02),
        'norm1_w': gain((L, D)),
        'w_in': nrm((L, D, IN_WIDTH), D ** -0.5),
        'mla_q_norm_w': gain((L, MLA_Q_RANK)),
        'mla_w_uq': nrm((L, MLA_Q_RANK, MLA_HEADS * (MLA_NOPE + MLA_ROPE)), MLA_Q_RANK ** -0.5),
        'mla_kv_norm_w': gain((L, MLA_KV_RANK)),
        'mla_w_ukv': nrm((L, MLA_KV_RANK, MLA_HEADS * (MLA_NOPE + MLA_V)), MLA_KV_RANK ** -0.5),
        'gdn_conv_w': nrm((L, GDN_CONV, GDN_QKV), GDN_CONV ** -0.5),
        'gdn_a_log': jnp.log(unif((L, GDN_HEADS), 1.0, 16.0)),
        'gdn_dt_bias': dt + jnp.log(-jnp.expm1(-dt)),
        'gdn_norm_w': gain((L, GDN_DV)),
        'rwkv_mu': unif((L, RWKV_IN), 0.0, 1.0),
        'rwkv_w0': unif((L, RWKV_WIDTH), -6.0, -1.0),
        'rwkv_w_up': nrm((L, RWKV_W_RANK, RWKV_WIDTH), 0.5 * RWKV_W_RANK ** -0.5),
        'rwkv_a0': nrm((L, RWKV_WIDTH), 0.1),
        'rwkv_a_up': nrm((L, RWKV_A_RANK, RWKV_WIDTH), RWKV_A_RANK ** -0.5),
        'rwkv_g_up': nrm((L, RWKV_G_RANK, RWKV_WIDTH), RWKV_G_RANK ** -0.5),
        'rwkv_k_k': 0.85 + nrm((L, RWKV_WIDTH), 0.02),
        'rwkv_k_a': gain((L, RWKV_WIDTH)),
        'rwkv_r_k': nrm((L, RWKV_HEADS, RWKV_HEAD), 0.1),
        'rwkv_lnx_w': gain((L, RWKV_WIDTH)),
        'rwkv_lnx_b': nrm((L, RWKV_WIDTH), 0.02),
        'w_branch': nrm((L, MIX_WIDTH, D), (MIX_WIDTH // 3) ** -0.5),
        'w_out': nrm((L, D, D), D ** -0.5),
        'norm2_w': gain((L, D)),
        'w_gate_up': nrm((L, D, 2 * D_FF), D ** -0.5),
        'w_down': nrm((L, D_FF, D), D_FF ** -0.5),
        'final_norm_w': gain((D,)),
    }


def reference(x, c, positions, w_ada, b_ada, norm1_w, w_in, mla_q_norm_w, mla_w_uq, mla_kv_norm_w,
              mla_w_ukv, gdn_conv_w, gdn_a_log, gdn_dt_bias, gdn_norm_w, rwkv_mu, rwkv_w0, rwkv_w_up,
              rwkv_a0, rwkv_a_up, rwkv_g_up, rwkv_k_k, rwkv_k_a, rwkv_r_k, rwkv_lnx_w, rwkv_lnx_b,
              w_branch, w_out, norm2_w, w_gate_up, w_down, final_norm_w):
    cos, sin = rope_tables(positions)
    c_act = jax.nn.silu(c)
    for l in range(DEPTH):
        mod = c_act @ w_ada[l] + b_ada[l]
        sh1, sc1, gt1, sh2, sc2, gt2 = jnp.split(mod[:, None, :], 6, axis=-1)
        h = modulate(x, norm1_w[l], sh1, sc1)
        p = h @ w_in[l]
        p_a, p_b, p_c, p_g = jnp.split(p, [MLA_IN, MLA_IN + GDN_IN, MLA_IN + GDN_IN + RWKV_IN], axis=-1)
        o_a = mla_mixer(p_a, cos, sin, mla_q_norm_w[l], mla_w_uq[l], mla_kv_norm_w[l], mla_w_ukv[l])
        o_b = gdn_mixer(p_b, gdn_conv_w[l], gdn_a_log[l], gdn_dt_bias[l], gdn_norm_w[l])
        o_c = rwkv_mixer(p_c, rwkv_mu[l], rwkv_w0[l], rwkv_w_up[l], rwkv_a0[l], rwkv_a_up[l],
                         rwkv_g_up[l], rwkv_k_k[l], rwkv_k_a[l], rwkv_r_k[l], rwkv_lnx_w[l], rwkv_lnx_b[l])
        wb_a, wb_b, wb_c = jnp.split(w_branch[l], [MLA_OUT, MLA_OUT + GDN_OUT], axis=0)
        g_a, g_b, g_c = jnp.split(jax.nn.sigmoid(p_g), 3, axis=-1)
        merged = g_a * (o_a @ wb_a) + g_b * (o_b @ wb_b) + g_c * (o_c @ wb_c)
        x = x + gt1 * (merged @ w_out[l])
        h2 = modulate(x, norm2_w[l], sh2, sc2)
        gate, up = jnp.split(h2 @ w_gate_up[l], 2, axis=-1)
        x = x + gt2 * ((jax.nn.silu(gate) * up) @ w_down[l])
    return rmsnorm(x, final_norm_w)
```

```python
import contextlib
import numpy as np
import concourse.bass as bass
import concourse.mybir as mybir
from concourse.bass_utils import run_bass_kernel_spmd

F32 = mybir.dt.float32
BF16 = mybir.dt.bfloat16
I32 = mybir.dt.int32
AF = mybir.ActivationFunctionType
ALU = mybir.AluOpType

NCORES = 8
D = 2048
B = 2
S = 8192
DEPTH = 4
KC = D // 128
IN_W = 14928
DFF = 5632
EPS = 1e-6

ENGS = ['pe', 'act', 'dve', 'pool', 'sp']
EPOCH = 30000
NDMA = 12
DMA_EPOCH = 1800


class KB:
    def __init__(self):
        self.nc = bass.Bass("TRN2", target_bir_lowering=False)
        self.stack = contextlib.ExitStack()
        self.ops = {e: [] for e in ENGS}
        self.cnt = {e: 0 for e in ENGS}
        self.sem = {}
        self.nsem = 0
        for e in ENGS:
            self.sem[e] = self._newsem()
        self.dsem = [self._newsem() for _ in range(NDMA)]
        self.dcnt = [0] * NDMA
        self.drr = 0
        self.waited = {e: {} for e in ENGS}
        self.last_w = {}
        self.readers = {}
        self.dma_toks = []
        self.ntile = 0
        self.tstack = self.stack

    def _newsem(self):
        self.nsem += 1
        return self.stack.enter_context(self.nc.semaphore("s%d" % self.nsem))

    def dram(self, name, shape, dt, kind):
        return self.nc.dram_tensor(name, list(shape), dt, kind=kind).ap()

    def sb(self, shape, dt, name=None):
        self.ntile += 1
        return self.tstack.enter_context(
            self.nc.sbuf_tensor(name or ("t%d" % self.ntile), list(shape), dt))

    def ps(self, shape, dt=F32, name=None):
        self.ntile += 1
        return self.tstack.enter_context(
            self.nc.psum_tensor(name or ("p%d" % self.ntile), list(shape), dt))

    @contextlib.contextmanager
    def phase(self):
        old = self.tstack
        self.tstack = contextlib.ExitStack()
        try:
            yield
        finally:
            self.barrier()
            self.tstack.close()
            self.tstack = old

    def barrier(self):
        toks = {}
        for e in ['pe', 'act', 'dve', 'pool']:
            if self.cnt[e] > 0:
                toks[id(self.sem[e])] = (self.sem[e], self.cnt[e])
        for (_, sem, val) in self.dma_toks:
            k = id(sem)
            if k not in toks or toks[k][1] < val:
                toks[k] = (sem, val)
        self.dma_toks = [('dma', s_, v_) for (s_, v_) in toks.values()]
        for e in ENGS:
            deps = []
            for k, (sem, val) in toks.items():
                if self.waited[e].get(k, 0) >= val:
                    continue
                self.waited[e][k] = val
                deps.append((sem, val))
            if deps:
                self.ops[e].append((deps, None, None, 0))
        self.last_w = {}
        self.readers = {}

    def _deps(self, eng, reads, writes):
        deps = {}

        def add(tok):
            te, sem, val = tok
            if te == 'pe' and eng == 'pe':
                return
            k = id(sem)
            if k not in deps or deps[k][1] < val:
                deps[k] = (sem, val)

        for k in reads:
            w = self.last_w.get(k)
            if w is not None:
                add(w)
        for k in writes:
            w = self.last_w.get(k)
            if w is not None:
                add(w)
            for r in self.readers.get(k, {}).values():
                if isinstance(r, list):
                    for t in r:
                        add(t)
                else:
                    add(r)
        out = []
        wd = self.waited[eng]
        for k, (sem, val) in deps.items():
            if wd.get(k, 0) >= val:
                continue
            wd[k] = val
            out.append((sem, val))
        return out

    def _update(self, tok, reads, writes, is_dma):
        for k in writes:
            self.last_w[k] = tok
            self.readers[k] = {}
        for k in reads:
            rd = self.readers.setdefault(k, {})
            if is_dma:
                rd.setdefault('dma', []).append(tok)
            else:
                rd[tok[0]] = tok

    def op(self, eng, fn, reads=(), writes=()):
        deps = self._deps(eng, reads, writes)
        if self.cnt[eng] >= EPOCH:
            self.sem[eng] = self._newsem()
            self.cnt[eng] = 0
        self.cnt[eng] += 1
        tok = (eng, self.sem[eng], self.cnt[eng])
        self.ops[eng].append((deps, fn, self.sem[eng], 1))
        self._update(tok, reads, writes, False)

    def dma(self, out, in_, reads=(), writes=(), q='sp', slow=False):
        i = self.drr
        self.drr = (i + 1) % NDMA
        if self.dcnt[i] >= DMA_EPOCH:
            self.dsem[i] = self._newsem()
            self.dcnt[i] = 0
        deps = self._deps(q, reads, writes)
        if self.dcnt[i] > 0:
            k = id(self.dsem[i])
            v = 16 * self.dcnt[i]
            if self.waited[q].get(k, 0) < v:
                self.waited[q][k] = v
                deps.append((self.dsem[i], v))
        self.dcnt[i] += 1
        tok = ('dma', self.dsem[i], 16 * self.dcnt[i])
        if slow:
            self.ops[q].append((deps, lambda e: e.dma_start(out=out, in_=in_, allow_slow_non_contiguous=True),
                                self.dsem[i], 16))
        else:
            self.ops[q].append((deps, lambda e: e.dma_start(out=out, in_=in_), self.dsem[i], 16))
        self._update(tok, reads, writes, True)
        self.dma_toks.append(tok)

    def finish(self):
        final = {}
        for (_, sem, val) in self.dma_toks:
            k = id(sem)
            if k not in final or final[k][1] < val:
                final[k] = (sem, val)
        deps = list(final.values())
        for e in ['pe', 'act', 'dve', 'pool']:
            if self.cnt[e] > 0:
                deps.append((self.sem[e], self.cnt[e]))
        self.ops['sp'].append((deps, None, None, 0))
        nc = self.nc
        ops = self.ops

        def replay(name, e):
            for deps, fn, sem, inc in ops[name]:
                for (s, v) in deps:
                    e.wait_ge(s, v)
                if fn is not None:
                    fn(e).then_inc(sem, inc)

        with nc.Block() as block:
            @block.tensor
            def _(e):
                replay('pe', e)

            @block.scalar
            def _(e):
                replay('act', e)

            @block.vector
            def _(e):
                replay('dve', e)

            @block.gpsimd
            def _(e):
                replay('pool', e)

            @block.sync
            def _(e):
                replay('sp', e)
        self.stack.close()
        return nc

    def mm(self, out, lhsT, rhs, start, stop, reads, writes):
        self.op('pe', lambda e: e.matmul(out, lhsT, rhs, start=start, stop=stop), reads, writes)

    def tr(self, out, in_, ident, reads, writes):
        self.op('pe', lambda e: e.transpose(out, in_, ident), reads, writes)

    def act(self, out, in_, func, reads, writes, bias=None, scale=None, accum_out=None):
        kw = {}
        if bias is not None:
            kw['bias'] = bias
        if scale is not None:
            kw['scale'] = scale
        if accum_out is not None:
            kw['accum_out'] = accum_out
        self.op('act', lambda e: e.activation(out, in_, func, **kw), reads, writes)

    def ts(self, out, in0, s1, s2, op0, op1=None, reads=(), writes=(), eng='dve'):
        if op1 is None:
            self.op(eng, lambda e: e.tensor_scalar(out, in0, s1, None, op0), reads, writes)
        else:
            self.op(eng, lambda e: e.tensor_scalar(out, in0, s1, s2, op0, op1), reads, writes)

    def tt(self, out, in0, in1, op, reads, writes, eng='dve'):
        self.op(eng, lambda e: e.tensor_tensor(out, in0, in1, op), reads, writes)

    def stt(self, out, in0, scalar, in1, op0, op1, reads, writes):
        self.op('dve', lambda e: e.scalar_tensor_tensor(out, in0, scalar, in1, op0, op1), reads, writes)

    def cp(self, out, in_, reads, writes, eng='dve'):
        if eng == 'act':
            self.op('act', lambda e: e.activation(out, in_, AF.Copy), reads, writes)
        else:
            self.op(eng, lambda e: e.tensor_copy(out, in_), reads, writes)

    def memset(self, ap, val, writes, eng='dve'):
        self.op(eng, lambda e: e.memset(ap, val), (), writes)


def run(nc, in_maps):
    res = run_bass_kernel_spmd(nc, in_maps, core_ids=list(range(NCORES)))
    return res.results


def fm_vec(v):
    v = np.asarray(v)
    return np.ascontiguousarray(v.reshape(-1, 128).T)


class Dense:
    def __init__(self, kb, T, slab_w=256, slab_kc=16, nps=2):
        self.kb = kb
        self.T = T
        self.W = slab_w
        self.SK = slab_kc
        self.stage = [kb.sb([128, slab_kc, slab_w], F32) for _ in range(2)]
        self.wbf = [kb.sb([128, slab_kc, slab_w], BF16) for _ in range(2)]
        self.nsub = slab_w // 128
        self.psum = [[kb.ps([128, 512], F32) for _ in range(self.nsub)] for _ in range(nps)]
        self.nps = nps
        self.it = 0
        self.git = 0

    def run(self, w_ap, k0_rows, kc_n, n0, n_cols, act_fn, evac_fn, ntok=1):
        kb = self.kb
        T = self.T
        assert ntok == 1 or kc_n <= self.SK
        for c0 in range(0, n_cols, self.W):
            cw = min(self.W, n_cols - c0)
            nsub = (cw + 127) // 128
            gs = []
            for t in range(ntok):
                gs.append(self.git % self.nps)
                self.git += 1
            for s0 in range(0, kc_n, self.SK):
                sk = min(self.SK, kc_n - s0)
                b = self.it % 2
                self.it += 1
                src = w_ap[k0_rows + s0 * 128: k0_rows + (s0 + sk) * 128, n0 + c0: n0 + c0 + cw]
                src = src.rearrange("(kc p) n -> p kc n", p=128)
                kb.dma(self.stage[b][:, 0:sk, 0:cw], src, (), [('wst', id(self), b)])
                kb.cp(self.wbf[b][:, 0:sk, 0:cw], self.stage[b][:, 0:sk, 0:cw],
                      [('wst', id(self), b)], [('wbf', id(self), b)], eng='pool')
                for t in range(ntok):
                    g = gs[t]
                    for j in range(nsub):
                        m = min(128, cw - j * 128)
                        pk = ('dps', id(self), g, j)
                        for kc in range(sk):
                            a_ap, a_keys = act_fn(s0 + kc, t)
                            kb.mm(self.psum[g][j][0:m, 0:T], self.wbf[b][:, kc, j * 128: j * 128 + m], a_ap,
                                  start=(s0 + kc == 0), stop=(s0 + kc == kc_n - 1),
                                  reads=[('wbf', id(self), b)] + list(a_keys), writes=[pk])
                    if s0 + sk == kc_n:
                        for j in range(nsub):
                            m = min(128, cw - j * 128)
                            evac_fn(c0 + j * 128, m, self.psum[g][j][0:m, 0:T], ('dps', id(self), g, j), t)


def rstd_from_sumsq(kb, out, ps_ap, n, eps, reads, writes, tmp, tmpkey):
    kb.ts(tmp, ps_ap, 1.0 / n, eps, ALU.mult, ALU.add, reads=reads, writes=[tmpkey])
    kb.act(tmp, tmp, AF.Ln, [tmpkey], [tmpkey])
    kb.act(out, tmp, AF.Exp, [tmpkey], writes, scale=-0.5)


PREP_COLS = DEPTH * 6 * D // NCORES


def build_prep():
    kb = KB()
    cT = kb.dram("cT", [128, KC * B], F32, "ExternalInput")
    wada = kb.dram("wada", [D, PREP_COLS], F32, "ExternalInput")
    bada = kb.dram("bada", [B, PREP_COLS], F32, "ExternalInput")
    pos = kb.dram("pos", [64, S], I32, "ExternalInput")
    cst = kb.dram("cst", [64, 2], F32, "ExternalInput")
    mod = kb.dram("mod", [B, PREP_COLS], F32, "ExternalOutput")
    cc = kb.dram("cc", [64, S], F32, "ExternalOutput")
    ss = kb.dram("ss", [64, S], F32, "ExternalOutput")

    c_sb = kb.sb([128, KC * B], F32)
    b_sb = kb.sb([B, PREP_COLS], F32)
    o_sb = kb.sb([B, PREP_COLS], F32)
    kb.dma(c_sb[:], cT, (), ['c'])
    kb.dma(b_sb[:], bada, (), ['b'])
    kb.act(c_sb[:], c_sb[:], AF.Silu, ['c'], ['c'])
    wt = [kb.sb([128, KC, 512], F32) for _ in range(2)]
    pp = [kb.ps([B, 512], F32) for _ in range(2)]
    c3 = c_sb[:].rearrange("p (k b) -> p k b", b=B)
    for ci in range(PREP_COLS // 512):
        bi = ci % 2
        kb.dma(wt[bi][:], wada[:, ci * 512:(ci + 1) * 512].rearrange("(kc p) n -> p kc n", p=128),
               (), [('w', bi)])
        for k in range(KC):
            kb.mm(pp[bi][:], c3[:, k, :], wt[bi][:, k, :], start=(k == 0), stop=(k == KC - 1),
                  reads=['c', ('w', bi)], writes=[('pp', bi)])
        kb.tt(o_sb[:, ci * 512:(ci + 1) * 512], pp[bi][:], b_sb[:, ci * 512:(ci + 1) * 512], ALU.add,
              [('pp', bi), 'b'], ['o'])
    kb.dma(mod, o_sb[:], ['o'], ())

    CH = 2048
    cs = kb.sb([64, 2], F32)
    kb.dma(cs[:], cst, (), ['cs'])
    pi = kb.sb([64, CH], I32)
    ang = kb.sb([64, CH], F32)
    r = kb.sb([64, CH], F32)
    ki = kb.sb([64, CH], I32)
    kf = kb.sb([64, CH], F32)
    m = kb.sb([64, CH], F32)
    osb = {'s': kb.sb([64, CH], F32), 'c': kb.sb([64, CH], F32)}
    k_ = 'rope'
    for ci in range(S // CH):
        kb.dma(pi[:], pos[:, ci * CH:(ci + 1) * CH], (), [k_])
        kb.cp(ang[:], pi[:], [k_], [k_])
        kb.ts(ang[:], ang[:], cs[:, 0:1], None, ALU.mult, reads=[k_, 'cs'], writes=[k_])
        kb.ts(kf[:], ang[:], float(1.0 / (2 * np.pi)), None, ALU.mult, reads=[k_], writes=[k_])
        kb.cp(ki[:], kf[:], [k_], [k_])
        kb.cp(kf[:], ki[:], [k_], [k_])
        kb.stt(ang[:], kf[:], -6.28125, ang[:], ALU.mult, ALU.add, [k_], [k_])
        kb.stt(ang[:], kf[:], -0.0019353071795864769, ang[:], ALU.mult, ALU.add, [k_], [k_])
        for which, shift, dst in (('s', 0.0, ss), ('c', float(np.pi / 2), cc)):
            kb.ts(r[:], ang[:], shift, None, ALU.add, reads=[k_], writes=[k_])
            kb.ts(m[:], r[:], float(np.pi), None, ALU.is_gt, reads=[k_], writes=[k_])
            kb.stt(r[:], m[:], float(-2 * np.pi), r[:], ALU.mult, ALU.add, [k_], [k_])
            kb.ts(m[:], r[:], float(-np.pi), None, ALU.is_lt, reads=[k_], writes=[k_])
            kb.stt(r[:], m[:], float(2 * np.pi), r[:], ALU.mult, ALU.add, [k_], [k_])
            kb.ts(r[:], r[:], 3.1415925, -3.1415925, ALU.min, ALU.max, reads=[k_], writes=[k_])
            o = osb[which]
            kb.act(o[:], r[:], AF.Sin, [k_], [(k_, which)])
            if which == 's':
                kb.ts(o[:], o[:], cs[:, 1:2], None, ALU.mult, reads=[(k_, which), 'cs'], writes=[(k_, which)])
            kb.dma(dst[:, ci * CH:(ci + 1) * CH], o[:], [(k_, which)], ())
    return kb.finish()


TOK = B * S // NCORES
TT = 512


def modulate_tile(kb, xT, xkey, hT, hkey, acol, bcol, ones, scr, scrkey, ps_stat, pskey, rstd, rkey):
    for k in range(KC):
        kb.act(scr[:, k, :], xT[:, k, :], AF.Square, [xkey], [scrkey])
    for k in range(KC):
        kb.mm(ps_stat[:], ones[:], scr[:, k, :], start=(k == 0), stop=(k == KC - 1),
              reads=[scrkey, 'ones'], writes=[pskey])
    rstd_from_sumsq(kb, rstd[:], ps_stat[:], float(D), EPS, [pskey], [rkey], scr[:, 0, :], scrkey)
    for k in range(KC):
        kb.stt(scr[:, k, :], xT[:, k, :], acol[:, k:k + 1], rstd[:], ALU.mult, ALU.mult,
               [xkey, rkey, 'acol'], [scrkey])
        kb.act(hT[:, k, :], scr[:, k, :], AF.Identity, [scrkey, 'acol'], [hkey], bias=bcol[:, k:k + 1])


def build_stageA():
    kb = KB()
    xT = kb.dram("xT", [D, TOK], F32, "ExternalInput")
    nw = kb.dram("nw", [128, KC], F32, "ExternalInput")
    sc = kb.dram("sc", [128, KC], F32, "ExternalInput")
    sh = kb.dram("sh", [128, KC], F32, "ExternalInput")
    w_in = kb.dram("w_in", [D, IN_W], F32, "ExternalInput")
    pT = kb.dram("pT", [IN_W, TOK], F32, "ExternalOutput")

    ones = kb.sb([128, 128], F32)
    kb.memset(ones[:], 1.0, ['ones'])
    nw_s = kb.sb([128, KC], F32)
    sc_s = kb.sb([128, KC], F32)
    sh_s = kb.sb([128, KC], F32)
    kb.dma(nw_s[:], nw, (), ['nw'])
    kb.dma(sc_s[:], sc, (), ['sc'])
    kb.dma(sh_s[:], sh, (), ['acol'])
    kb.ts(sc_s[:], sc_s[:], 1.0, None, ALU.add, reads=['sc'], writes=['sc'])
    kb.tt(sc_s[:], sc_s[:], nw_s[:], ALU.mult, ['sc', 'nw', 'acol'], ['acol'])

    TM = 256
    hT = kb.sb([128, KC, TOK], BF16)
    xt = [kb.sb([128, KC, TM], F32) for _ in range(2)]
    scr = kb.sb([128, KC, TM], F32)
    rstd = kb.sb([128, TM], F32)
    ps_stat = kb.ps([128, TM], F32)
    xv = xT.rearrange("(kc p) t -> p kc t", p=128)
    for t in range(TOK // TM):
        bi = t % 2
        kb.dma(xt[bi][:], xv[:, :, t * TM:(t + 1) * TM], (), [('x', bi)])
        modulate_tile(kb, xt[bi], ('x', bi), hT[:, :, t * TM:(t + 1) * TM], ('h', t // 2), sc_s, sh_s, ones,
                      scr, 'scr', ps_stat, 'pstat', rstd, 'rstd')

    NT = TOK // TT
    dn = Dense(kb, TT)
    ob = [kb.sb([128, TT], F32) for _ in range(4)]
    cnt = [0]

    def act_fn(kc, t):
        return hT[:, kc, t * TT:(t + 1) * TT], [('h', t)]

    def evac(c0, m, ps_ap, pk, t):
        i = cnt[0] % 4
        cnt[0] += 1
        kb.cp(ob[i][0:m, :], ps_ap, [pk], [('ob', i)], eng=('act' if i % 2 else 'dve'))
        kb.dma(pT[c0:c0 + m, t * TT:(t + 1) * TT], ob[i][0:m, :], [('ob', i)], ())
    dn.run(w_in, 0, KC, 0, IN_W, act_fn, evac, ntok=NT)
    return kb.finish()


MLA_SCALE = float(192 ** -0.5)


def norm_tile(kb, src, nkc, T, wcol, dst, ones, scr, ps_stat, rstd, key_in, key_out, n):
    for k in range(nkc):
        kb.act(scr[:, k, 0:T], src[:, k, 0:T], AF.Square, [key_in], ['nscr'])
    for k in range(nkc):
        kb.mm(ps_stat[:, 0:T], ones[:], scr[:, k, 0:T], start=(k == 0), stop=(k == nkc - 1),
              reads=['nscr', 'ones'], writes=['nps'])
    rstd_from_sumsq(kb, rstd[:, 0:T], ps_stat[:, 0:T], float(n), EPS, ['nps'], ['nrstd'], scr[:, 0, 0:T], 'nscr')
    for k in range(nkc):
        kb.stt(dst[:, k, 0:T], src[:, k, 0:T], wcol[:, k:k + 1], rstd[:, 0:T], ALU.mult, ALU.mult,
               [key_in, 'nrstd', 'nw'], [key_out])


def load_cast(kb, dst_bf, src_ap, stage, nkc, ncol, key):
    kb.dma(stage[:, 0:nkc, 0:ncol], src_ap.rearrange("(kc p) n -> p kc n", p=128), (), ['wstage'])
    kb.cp(dst_bf[:, 0:nkc, 0:ncol], stage[:, 0:nkc, 0:ncol], ['wstage'], [key])


def mla_inputs(kb):
    d = {}
    d['cc'] = kb.dram("cc", [64, S], F32, "ExternalInput")
    d['ss'] = kb.dram("ss", [64, S], F32, "ExternalInput")
    d['qnw'] = kb.dram("qnw", [128, 6], F32, "ExternalInput")
    d['kvnw'] = kb.dram("kvnw", [128, 4], F32, "ExternalInput")
    d['wq_n'] = kb.dram("wq_n", [768, 256], F32, "ExternalInput")
    d['wq_r'] = kb.dram("wq_r", [768, 128], F32, "ExternalInput")
    d['wq_rs'] = kb.dram("wq_rs", [768, 128], F32, "ExternalInput")
    d['wk'] = kb.dram("wk", [512, 256], F32, "ExternalInput")
    d['wv'] = kb.dram("wv", [512, 256], F32, "ExternalInput")
    d['mask'] = kb.dram("mask", [128, 2048], F32, "ExternalInput")
    return d


def emit_mla(kb, pint, d, oT):
    TP = 512
    NTILE = S // TP
    cqT = pint[0:768, 3:3 + S]
    ckvT = pint[768:1280, 3:3 + S]
    krT = pint[1280:1344, 3:3 + S]
    CCd, SSd, qnw, kvnw = d['cc'], d['ss'], d['qnw'], d['kvnw']
    wq_n, wq_r, wq_rs, wk, wv, maskd = d['wq_n'], d['wq_r'], d['wq_rs'], d['wk'], d['wv'], d['mask']

    ones = kb.sb([128, 128], F32)
    kb.memset(ones[:], 1.0, ['ones'])
    ones_b = kb.sb([128, 128], BF16)
    kb.memset(ones_b[:], 1.0, ['ones_b'])
    qnw_s = kb.sb([128, 6], F32)
    kvnw_s = kb.sb([128, 4], F32)
    kb.dma(qnw_s[:], qnw, (), ['nw'])
    kb.dma(kvnw_s[:], kvnw, (), ['nw'])
    mstage = kb.sb([128, 2048], F32)
    mask_b = kb.sb([128, 2048], BF16)
    kb.dma(mstage[:], maskd, (), ['mstage'])
    kb.cp(mask_b[:], mstage[:], ['mstage'], ['mask'])

    QTn = kb.sb([128, S], BF16)
    QTr = kb.sb([64, S], BF16)
    KTn = kb.sb([128, S], BF16)
    KTr = kb.sb([64, S], BF16)
    V = kb.sb([128, S // 128, 128], BF16)
    src = [kb.sb([128, 6, TP], F32) for _ in range(2)]
    scr = kb.sb([128, 6, TP], F32)
    cn = kb.sb([128, 6, TP], BF16)
    rstd = kb.sb([128, TP], F32)
    wstage = kb.sb([128, 6, 128], F32)
    wqn_b = kb.sb([128, 6, 128], BF16)
    wqr_b = kb.sb([128, 6, 64], BF16)
    wqs_b = kb.sb([128, 6, 64], BF16)
    wk_b = kb.sb([128, 4, 128], BF16)
    wv_b = kb.sb([128, 4, 128], BF16)
    cc_s = [kb.sb([64, TP], F32) for _ in range(2)]
    ss_s = [kb.sb([64, TP], F32) for _ in range(2)]
    kr_s = [kb.sb([64, TP], F32) for _ in range(2)]
    krs_s = [kb.sb([64, TP], F32) for _ in range(2)]
    t1 = kb.sb([64, TP], F32)
    t2 = kb.sb([64, TP], F32)
    pT = [kb.sb([128, 512], BF16) for _ in range(3)]
    rec = kb.sb([128, 512], F32)
    osb = [kb.sb([128, 512], F32) for _ in range(2)]
    ps = [kb.ps([128, 512], F32) for _ in range(8)]

    def rope_combine(dst, a_ap, b_ap, bi, reads, wkey):
        kb.tt(t1[:], a_ap, cc_s[bi][:], ALU.mult, reads + [('cs', bi)], ['t1'])
        kb.tt(t2[:], b_ap, ss_s[bi][:], ALU.mult, reads + [('cs', bi)], ['t2'])
        kb.tt(dst, t1[:], t2[:], ALU.add, ['t1', 't2'], [wkey])

    for t in range(NTILE):
        bi = t % 2
        sl = slice(t * TP, (t + 1) * TP)
        kb.dma(cc_s[bi][:], CCd[:, sl], (), [('cs', bi)])
        kb.dma(ss_s[bi][:], SSd[:, sl], (), [('cs', bi)])
        kb.dma(kr_s[bi][:], krT[:, sl], (), [('kr', bi)])
        kb.dma(krs_s[bi][0:32, :], krT[32:64, sl], (), [('kr', bi)])
        kb.dma(krs_s[bi][32:64, :], krT[0:32, sl], (), [('kr', bi)])
        rope_combine(KTr[:, sl], kr_s[bi][:], krs_s[bi][:], bi, [('kr', bi)], ('KTr', t))

    cqv = cqT.rearrange("(kc p) t -> p kc t", p=128)
    ckvv = ckvT.rearrange("(kc p) t -> p kc t", p=128)
    for h in range(2):
        load_cast(kb, wqn_b, wq_n[:, h * 128:(h + 1) * 128], wstage, 6, 128, 'wqn')
        load_cast(kb, wqr_b, wq_r[:, h * 64:(h + 1) * 64], wstage, 6, 64, 'wqr')
        load_cast(kb, wqs_b, wq_rs[:, h * 64:(h + 1) * 64], wstage, 6, 64, 'wqs')
        load_cast(kb, wk_b, wk[:, h * 128:(h + 1) * 128], wstage, 4, 128, 'wk')
        load_cast(kb, wv_b, wv[:, h * 128:(h + 1) * 128], wstage, 4, 128, 'wv')
        for t in range(NTILE):
            bi = t % 2
            sl = slice(t * TP, (t + 1) * TP)
            kb.dma(src[bi][:], cqv[:, :, sl], (), [('src', bi)])
            kb.dma(cc_s[bi][:], CCd[:, sl], (), [('cs', bi)])
            kb.dma(ss_s[bi][:], SSd[:, sl], (), [('cs', bi)])
            norm_tile(kb, src[bi], 6, TP, qnw_s, cn, ones, scr, ps[0], rstd, ('src', bi), 'cn', 768)
            for k in range(6):
                kb.mm(ps[1][:], wqn_b[:, k, :], cn[:, k, :], start=(k == 0), stop=(k == 5),
                      reads=['wqn', 'cn'], writes=['ps1'])
            kb.cp(QTn[:, sl], ps[1][:], ['ps1'], [('QTn', t)], eng='act')
            for k in range(6):
                kb.mm(ps[2][0:64, :], wqr_b[:, k, :], cn[:, k, :], start=(k == 0), stop=(k == 5),
                      reads=['wqr', 'cn'], writes=['ps2'])
            for k in range(6):
                kb.mm(ps[3][0:64, :], wqs_b[:, k, :], cn[:, k, :], start=(k == 0), stop=(k == 5),
                      reads=['wqs', 'cn'], writes=['ps3'])
            rope_combine(QTr[:, sl], ps[2][0:64, :], ps[3][0:64, :], bi, ['ps2', 'ps3'], ('QTr', t))
        for t in range(NTILE):
            bi = t % 2
            sl = slice(t * TP, (t + 1) * TP)
            kb.dma(src[bi][:, 0:4, :], ckvv[:, :, sl], (), [('src', bi)])
            norm_tile(kb, src[bi], 4, TP, kvnw_s, cn, ones, scr, ps[0], rstd, ('src', bi), 'cn', 512)
            for k in range(4):
                kb.mm(ps[1][:], wk_b[:, k, :], cn[:, k, :], start=(k == 0), stop=(k == 3),
                      reads=['wk', 'cn'], writes=['ps1'])
            kb.cp(KTn[:, sl], ps[1][:], ['ps1'], [('KTn', t)], eng='act')
            for blk in range(TP // 128):
                for k in range(4):
                    kb.mm(ps[2][:, blk * 128:(blk + 1) * 128], cn[:, k, blk * 128:(blk + 1) * 128], wv_b[:, k, :],
                          start=(k == 0), stop=(k == 3), reads=['wv', 'cn'], writes=['ps2'])
            kb.cp(V[:, t * 4:(t + 1) * 4, :], ps[2][:].rearrange("p (a b) -> p a b", b=128),
                  ['ps2'], [('V', t)])
        it = 0
        for I in range(S // 512):
            qsl = slice(I * 512, (I + 1) * 512)
            nj = 4 * I + 4
            for j in range(nj):
                ksl = slice(j * 128, (j + 1) * 128)
                sp_ = ps[4 + (it % 2)]
                spk = ('sps', it % 2)
                pb = it % 3
                it += 1
                kb.mm(sp_[:], KTn[:, ksl], QTn[:, qsl], start=True, stop=False,
                      reads=[('KTn', j // 4), ('QTn', I)], writes=[spk])
                kb.mm(sp_[:], KTr[:, ksl], QTr[:, qsl], start=False, stop=True,
                      reads=[('KTr', j // 4), ('QTr', I)], writes=[spk])
                kb.act(pT[pb][:], sp_[:], AF.Exp, [spk], [('pT', pb)], scale=MLA_SCALE)
                jj = j - 4 * I
                if jj >= 0:
                    kb.tt(pT[pb][:], pT[pb][:], mask_b[:, jj * 512:(jj + 1) * 512], ALU.mult,
                          [('pT', pb), 'mask'], [('pT', pb)])
                kb.mm(ps[6][:], V[:, j, :], pT[pb][:], start=(j == 0), stop=(j == nj - 1),
                      reads=[('V', j // 4), ('pT', pb)], writes=['ops'])
                kb.mm(ps[7][:], ones_b[:], pT[pb][:], start=(j == 0), stop=(j == nj - 1),
                      reads=['ones_b', ('pT', pb)], writes=['sums'])
            kb.op('dve', lambda e: e.reciprocal(rec[:], ps[7][:]), ['sums'], ['rec'])
            ob = osb[I % 2]
            kb.tt(ob[:], ps[6][:], rec[:], ALU.mult, ['ops', 'rec'], [('ob', I % 2)])
            kb.dma(oT[h * 128:(h + 1) * 128, qsl], ob[:], [('ob', I % 2)], ())


def causal_masks():
    k = np.arange(128)[:, None]
    q = np.arange(512)[None, :]
    return np.concatenate([(q >= k + 128 * jj).astype(np.float32) for jj in range(4)], axis=1)


CH = 64
NG = 4


class Delta:
    def __init__(self, kb, dk, dv, merged, ident, ident_key, identrep):
        self.kb, self.dk, self.dv, self.merged = kb, dk, dv, merged
        self.ident, self.ident_key, self.identrep = ident, ident_key, identrep
        self.banks = [kb.ps([128, 512], F32) for _ in range(6)]
        self.bankT = kb.ps([128, 1024], BF16)
        self.bi = 0
        f3 = lambda a, b_, dt: kb.sb([a, NG, b_], dt)
        self.ApT = f3(CH, CH, F32)
        self.RpT_b = f3(CH, CH, BF16)
        self.AkT_b = f3(CH, CH, BF16)
        self.RkT_b = f3(CH, CH, BF16)
        self.P = [f3(CH, CH, F32) for _ in range(2)]
        self.Q = [f3(CH, CH, F32) for _ in range(2)]
        self.Y = f3(CH, CH, F32)
        self.Yb = f3(CH, CH, BF16)
        self.Wq_b = f3(CH, dk, BF16)
        self.AV_b = f3(CH, dv, BF16)
        self.Ul_b = f3(CH, dv, BF16)
        self.RtT_b = f3(dk, CH, BF16)
        self.N = f3(dk, dv, F32)
        self.MT = f3(dk, dk, F32)
        self.uid = 0

    def nb(self):
        i = self.bi
        self.bi = (i + 1) % len(self.banks)
        return self.banks[i], ('dbank', id(self), i)

    def new_state(self):
        kb = self.kb
        H = kb.sb([self.dk, self.dv], F32)
        Hb = kb.sb([self.dk, self.dv], BF16)
        self.uid += 1
        key = ('H', id(self), self.uid)
        kb.memset(H[:], 0.0, [key])
        kb.memset(Hb[:], 0.0, [(key, 'b')])
        return (H, Hb, key)

    def group(self, st, PTs, KTs, QR, RbT, Qb, Ph, Kh, V, FA, FR, Gam, yT_out, rkeys, ykey):
        kb, dk, dv, mg = self.kb, self.dk, self.dv, self.merged
        H, Hb, hkey = st
        me = id(self)
        K_ = lambda n: (n, me)
        rk = list(rkeys)
        v3 = lambda bank, rows, w: bank[0:rows, 0:NG * w].rearrange("p (g w) -> p g w", w=w)
        b0, k0 = self.nb()
        for c in range(NG):
            kb.mm(b0[0:CH, c * 128:(c + 1) * 128], PTs[:, c * CH:(c + 1) * CH], QR[:, c, :], True, True, rk, [k0])
        s0 = v3(b0, CH, 128)
        kb.tt(self.ApT[:], s0[:, :, 0:CH], FA, ALU.mult, [k0] + rk, [K_('ApT')])
        kb.tt(self.RpT_b[:], s0[:, :, CH:128], FR, ALU.mult, [k0] + rk, [K_('RpT')])
        if not mg:
            b1, k1 = self.nb()
            for c in range(NG):
                kb.mm(b1[0:CH, c * 128:(c + 1) * 128], KTs[:, c * CH:(c + 1) * CH], QR[:, c, :], True, True, rk, [k1])
            s1 = v3(b1, CH, 128)
            kb.tt(self.AkT_b[:], s1[:, :, 0:CH], FA, ALU.mult, [k1] + rk, [K_('AkT')])
            kb.tt(self.RkT_b[:], s1[:, :, CH:128], FR, ALU.mult, [k1] + rk, [K_('RkT')])
        b2, k2 = self.nb()
        for c in range(NG):
            kb.tr(b2[0:CH, c * CH:(c + 1) * CH], self.ApT[:, c, :], self.ident[0:CH, 0:CH],
                  [K_('ApT'), self.ident_key], [k2])
        kb.cp(self.P[0][:], v3(b2, CH, CH), [k2], [K_('P0')], eng='act')
        kb.tt(self.Y[:], self.identrep, self.ApT[:], ALU.subtract, [K_('ApT'), self.ident_key], [K_('Y')])
        Pc, Pk = self.P[0], K_('P0')
        Qc, Qk = self.ApT, K_('ApT')
        for lev in range(1, 6):
            Pn, Pnk = self.P[lev % 2], K_('P%d' % (lev % 2))
            Qn, Qnk = self.Q[lev % 2], K_('Q%d' % (lev % 2))
            bp, kp = self.nb()
            for c in range(NG):
                kb.mm(bp[0:CH, c * CH:(c + 1) * CH], Qc[:, c, :], Pc[:, c, :], True, True, [Qk, Pk], [kp])
            if lev < 5:
                bq, kq = self.nb()
                for c in range(NG):
                    kb.mm(bq[0:CH, c * CH:(c + 1) * CH], Pc[:, c, :], Qc[:, c, :], True, True, [Qk, Pk], [kq])
            kb.cp(Pn[:], v3(bp, CH, CH), [kp], [Pnk], eng='act')
            if lev < 5:
                kb.cp(Qn[:], v3(bq, CH, CH), [kq], [Qnk], eng='dve')
            by, ky = self.nb()
            for c in range(NG):
                kb.mm(by[0:CH, c * CH:(c + 1) * CH], Pn[:, c, :], self.Y[:, c, :], True, True, [Pnk, K_('Y')], [ky])
            kb.tt(self.Y[:], self.Y[:], v3(by, CH, CH), ALU.add, [ky, K_('Y')], [K_('Y')])
            Pc, Pk, Qc, Qk = Pn, Pnk, Qn, Qnk
        kb.cp(self.Yb[:], self.Y[:], [K_('Y')], [K_('Yb')], eng='act')
        bw, kw = self.nb()
        for c in range(NG):
            kb.mm(bw[0:CH, c * dk:(c + 1) * dk], self.Yb[:, c, :], Qb[:, c, :], True, True, [K_('Yb')] + rk, [kw])
        kb.cp(self.Wq_b[:], v3(bw, CH, dk), [kw], [K_('Wq')], eng='act')
        if mg:
            bu, ku = self.nb()
            for c in range(NG):
                kb.mm(bu[0:CH, c * dv:(c + 1) * dv], self.Yb[:, c, :], V[:, c, :], True, True, [K_('Yb')] + rk, [ku])
            kb.cp(self.Ul_b[:], v3(bu, CH, dv), [ku], [K_('Ul')], eng='dve')
        else:
            ba, ka = self.nb()
            for c in range(NG):
                kb.mm(ba[0:CH, c * dv:(c + 1) * dv], self.AkT_b[:, c, :], V[:, c, :], True, True, [K_('AkT')] + rk, [ka])
            kb.cp(self.AV_b[:], v3(ba, CH, dv), [ka], [K_('AV')], eng='dve')
            bu, ku = self.nb()
            for c in range(NG):
                kb.mm(bu[0:CH, c * dv:(c + 1) * dv], self.Yb[:, c, :], self.AV_b[:, c, :], True, True,
                      [K_('Yb'), K_('AV')], [ku])
            kb.ts(self.Ul_b[:], v3(bu, CH, dv), -1.0, None, ALU.mult, reads=[ku], writes=[K_('Ul')])
        br, kr_ = self.nb()
        for c in range(NG):
            kb.mm(br[0:dk, c * CH:(c + 1) * CH], self.Wq_b[:, c, :], self.RpT_b[:, c, :], True, True,
                  [K_('Wq'), K_('RpT')], [kr_])
        kb.tt(self.RtT_b[:], RbT, v3(br, dk, CH), ALU.subtract, [kr_] + rk, [K_('RtT')])
        bn, kn = self.nb()
        for c in range(NG):
            kb.mm(bn[0:dk, c * dv:(c + 1) * dv], Ph[:, c, :], self.Ul_b[:, c, :], True, mg, [K_('Ul')] + rk, [kn])
            if not mg:
                kb.mm(bn[0:dk, c * dv:(c + 1) * dv], Kh[:, c, :], V[:, c, :], False, True, rk, [kn])
        kb.cp(self.N[:], v3(bn, dk, dv), [kn], [K_('N')], eng='act')
        bm, km = self.nb()
        for c in range(NG):
            kb.mm(bm[0:dk, c * dk:(c + 1) * dk], self.Wq_b[:, c, :], Ph[:, c, :], True, True, [K_('Wq')] + rk, [km])
        for c in range(NG):
            kb.stt(self.MT[:, c, :], self.ident[0:dk, 0:dk], Gam[:, c:c + 1], bm[0:dk, c * dk:(c + 1) * dk],
                   ALU.mult, ALU.subtract, [km, self.ident_key] + rk, [K_('MT')])
        byy, kyy = self.nb()
        for c in range(NG):
            o_ = byy[0:dv, c * CH:(c + 1) * CH]
            kb.mm(o_, self.Ul_b[:, c, :], self.RpT_b[:, c, :], True, False, [K_('Ul'), K_('RpT')], [kyy])
            if not mg:
                kb.mm(o_, V[:, c, :], self.RkT_b[:, c, :], False, False, [K_('RkT')] + rk, [kyy])
            kb.mm(o_, Hb[:], self.RtT_b[:, c, :], False, True, [(hkey, 'b'), K_('RtT')], [kyy])
            bh, kh = self.nb()
            kb.mm(bh[0:dk, 0:dv], self.MT[:, c, :], H[:], True, True, [K_('MT'), hkey], [kh])
            kb.tt(H[:], bh[0:dk, 0:dv], self.N[:, c, :], ALU.add, [kh, K_('N')], [hkey])
            kb.cp(Hb[:], H[:], [hkey], [(hkey, 'b')], eng='act')
        kb.cp(yT_out, byy[0:dv, 0:NG * CH], [kyy], [ykey], eng='dve')


def delta_consts():
    s = np.arange(64)[:, None]
    t = np.arange(64)[None, :]
    c = {}
    c['ident'] = np.eye(128, dtype=np.float32)
    c['identrep'] = np.ascontiguousarray(np.tile(np.eye(64, dtype=np.float32)[:, None, :], (1, NG, 1)).reshape(64, NG * 64))
    c['maskA'] = np.ascontiguousarray(np.tile((s < t).astype(np.float32)[:, None, :], (1, NG, 1)).reshape(64, NG * 64))
    c['maskR'] = np.ascontiguousarray(np.tile((s <= t).astype(np.float32)[:, None, :], (1, NG, 1)).reshape(64, NG * 64))
    rm = np.ones((128, NG * 64), np.float32)
    rm[:, ::64] = 0.0
    c['rmask'] = rm
    sel = np.zeros((64, 64), np.float32)
    sel[63, :] = 1.0
    c['sel63'] = sel
    return c


DCONST_SHAPES = (('ident', [128, 128]), ('identrep', [64, NG * CH]), ('maskA', [64, NG * CH]),
                 ('maskR', [64, NG * CH]), ('rmask', [128, NG * CH]), ('sel63', [64, 64]))


def delta_const_inputs(kb):
    return {nm: kb.dram(nm, shp, F32, "ExternalInput") for nm, shp in DCONST_SHAPES}


class DeltaConsts:
    def __init__(self, kb, drams):
        d = {}
        for nm, shp in DCONST_SHAPES:
            t = kb.sb(shp, F32)
            kb.dma(t[:], drams[nm], (), ['dconst'])
            d[nm] = t
        self.ident = d['ident']
        self.identrep3 = d['identrep'][:].rearrange("p (g w) -> p g w", w=CH)
        self.maskA3 = d['maskA'][:].rearrange("p (g w) -> p g w", w=CH)
        self.maskR3 = d['maskR'][:].rearrange("p (g w) -> p g w", w=CH)
        self.rmask = d['rmask']
        self.sel63 = d['sel63']
        self.maskR = d['maskR']
        self.ident_b = kb.sb([128, 128], BF16)
        kb.cp(self.ident_b[:], self.ident[:], ['dconst'], ['dconst_b'])
        self.ones = kb.sb([128, 128], F32)
        kb.memset(self.ones[:], 1.0, ['dones'])


def transpose_group(kb, dl, dst3, src, rows, reads, wkey, scale_cols=None):
    bt = dl.bankT
    for c in range(NG):
        kb.tr(bt[0:CH, c * rows:(c + 1) * rows], src[:, c * CH:(c + 1) * CH], dl.cst.ident_b[0:rows, 0:rows],
              list(reads) + ['dconst_b'], ['bankT'])
    if scale_cols is None:
        kb.cp(dst3, bt[0:CH, 0:NG * rows].rearrange("p (g w) -> p g w", w=rows), ['bankT'], [wkey], eng='act')
    else:
        for c in range(NG):
            kb.ts(dst3[:, c, :], bt[0:CH, c * rows:(c + 1) * rows], scale_cols[:, c:c + 1], None, ALU.mult,
                  reads=['bankT'] + list(reads), writes=[wkey])


RW_DEC = float(np.exp(-0.5))


def rwkv_inputs(kb):
    d = {}
    d['cols'] = kb.dram("rw_cols", [64, 40], F32, "ExternalInput")
    d['mulow'] = kb.dram("rw_mulow", [128, 3], F32, "ExternalInput")
    d['w_up'] = kb.dram("rw_w_up", [64, 256], F32, "ExternalInput")
    d['a_up'] = kb.dram("rw_a_up", [64, 256], F32, "ExternalInput")
    d['g_up'] = kb.dram("rw_g_up", [128, 256], F32, "ExternalInput")
    return d


def emit_rwkv(kb, rkvT, lowT, d, ocT, cst):
    GW = NG * CH
    NGRP = S // GW
    colsd, mulow, wupd, aupd, gupd = d['cols'], d['mulow'], d['w_up'], d['a_up'], d['g_up']
    dl = Delta(kb, 64, 64, False, cst.ident, 'dconst', cst.identrep3)
    dl.cst = cst
    cols = kb.sb([64, 40], F32)
    mul = kb.sb([128, 3], F32)
    kb.dma(cols[:], colsd, (), ['cols'])
    kb.dma(mul[:], mulow, (), ['cols'])
    wst = kb.sb([128, 256], F32)
    wup_b = kb.sb([64, 256], BF16)
    aup_b = kb.sb([64, 256], BF16)
    gup_b = kb.sb([128, 256], BF16)
    for (dr, dst, rows) in ((wupd, wup_b, 64), (aupd, aup_b, 64), (gupd, gup_b, 128)):
        kb.dma(wst[0:rows, :], dr, (), ['wst'])
        kb.cp(dst[:], wst[0:rows, :], ['wst'], ['wlow'])
    col = lambda j, h: cols[:, 12 + j * 4 + h: 12 + j * 4 + h + 1]

    f = lambda r, w=GW, dt=F32: kb.sb([r, w], dt)
    lw_x = f(64, GW + 1); la_x = f(64, GW + 1); lg_x = f(128, GW + 1)
    tanh_b = f(64, GW, BF16); xa_b = f(64, GW, BF16); sg_b = f(128, GW, BF16)
    tmp = f(128); tmp2 = f(64)
    X3 = [kb.sb([64, GW + 1], F32) for _ in range(3)]
    r_s = f(64); k_s = f(64); v_s = f(64)
    lgw = f(64); a_s = f(64); g_s = f(64); kk = f(64); kp = f(64); bb = f(64)
    cum = f(64); ex = f(64); cumC = kb.sb([64, NG], F32); gam = kb.sb([64, NG], F32)
    PTs = f(64, GW, BF16); KTs = f(64, GW, BF16); QR = kb.sb([64, NG, 128], BF16)
    PhT = f(64, GW, BF16); KhT = f(64, GW, BF16); qbT = f(64, GW, BF16); vT_b = f(64, GW, BF16)
    Qb = kb.sb([64, NG, 64], BF16); Ph = kb.sb([64, NG, 64], BF16); Kh = kb.sb([64, NG, 64], BF16)
    Vt = kb.sb([64, NG, 64], BF16)
    yT = f(64); yc = f(64); sq = f(64); rs = f(64); outb = [f(64) for _ in range(2)]
    psA = kb.ps([128, 512], F32)
    states = [dl.new_state() for _ in range(4)]
    v3 = lambda t: t[:].rearrange("p (g w) -> p g w", w=CH)

    def shift(dst, X, mucol, rows, rkey, wkey):
        kb.tt(tmp[0:rows, :], X[0:rows, 0:GW], X[0:rows, 1:GW + 1], ALU.subtract, [rkey], ['tmp'])
        kb.stt(dst, tmp[0:rows, :], mucol, X[0:rows, 1:GW + 1], ALU.mult, ALU.add, ['tmp', rkey, 'cols'], [wkey])

    def rsq(out, ps_ap, eps, reads, wkey, scale=1.0):
        kb.ts(tmp2[:], ps_ap, scale, eps, ALU.mult, ALU.add, reads=reads, writes=['tmp2'])
        kb.act(tmp2[:], tmp2[:], AF.Ln, ['tmp2'], ['tmp2'])
        kb.act(out, tmp2[:], AF.Exp, ['tmp2'], [wkey], scale=-0.5)

    for gi in range(NGRP):
        c0 = gi * GW
        kb.dma(lw_x[:], lowT[0:64, c0:c0 + GW + 1], (), ['lw_x'])
        kb.dma(la_x[:], lowT[64:128, c0:c0 + GW + 1], (), ['la_x'])
        kb.dma(lg_x[:], lowT[128:256, c0:c0 + GW + 1], (), ['lg_x'])
        shift(tmp2[:], lw_x, mul[0:64, 0:1], 64, 'lw_x', 'tmp2')
        kb.act(tanh_b[:], tmp2[:], AF.Tanh, ['tmp2'], ['tanh_b'])
        shift(xa_b[:], la_x, mul[0:64, 1:2], 64, 'la_x', 'xa_b')
        shift(tmp[:], lg_x, mul[:, 2:3], 128, 'lg_x', 'tmp')
        kb.act(sg_b[:], tmp[:], AF.Sigmoid, ['tmp'], ['sg_b'])
        for h in range(4):
            hs = slice(h * 64, (h + 1) * 64)
            for part in range(3):
                r0 = (part * 4 + h) * 64
                kb.dma(X3[part][:], rkvT[r0:r0 + 64, c0:c0 + GW + 1], (), [('X3', part)])
            for part, dst, nm in ((0, r_s, 'r_s'), (1, k_s, 'k_s'), (2, v_s, 'v_s')):
                shift(dst[:], X3[part], cols[:, part * 4 + h: part * 4 + h + 1], 64, ('X3', part), nm)
            kb.mm(psA[0:64, 0:GW], wup_b[:, hs], tanh_b[:], True, True, ['wlow', 'tanh_b'], ['psA'])
            kb.act(lgw[:], psA[0:64, 0:GW], AF.Sigmoid, ['psA', 'cols'], ['lgw'], bias=col(0, h))
            kb.ts(lgw[:], lgw[:], -RW_DEC, None, ALU.mult, reads=['lgw'], writes=['lgw'])
            kb.mm(psA[0:64, 0:GW], aup_b[:, hs], xa_b[:], True, True, ['wlow', 'xa_b'], ['psA'])
            kb.act(a_s[:], psA[0:64, 0:GW], AF.Sigmoid, ['psA', 'cols'], ['a_s'], bias=col(1, h))
            kb.mm(psA[0:64, 0:GW], gup_b[:, hs], sg_b[:], True, True, ['wlow', 'sg_b'], ['psA'])
            kb.cp(g_s[:], psA[0:64, 0:GW], ['psA'], ['g_s'], eng='act')
            kb.ts(kk[:], k_s[:], col(2, h), None, ALU.mult, reads=['k_s', 'cols'], writes=['kk'])
            kb.tt(sq[:], kk[:], kk[:], ALU.mult, ['kk'], ['sq'])
            kb.mm(psA[0:64, 0:GW], cst.ones[0:64, 0:64], sq[:], True, True, ['dones', 'sq'], ['psA'])
            rsq(rs[:], psA[0:64, 0:GW], 1e-6, ['psA'], 'rs')
            kb.tt(kk[:], kk[:], rs[:], ALU.mult, ['kk', 'rs'], ['kk'])
            kb.ts(kp[:], a_s[:], -1.0, col(3, h), ALU.add, ALU.mult, reads=['a_s', 'cols'], writes=['kp'])
            kb.stt(kp[:], kp[:], 1.0, k_s[:], ALU.add, ALU.mult, ['kp', 'k_s'], ['kp'])
            kb.tt(bb[:], kk[:], a_s[:], ALU.mult, ['kk', 'a_s'], ['bb'])
            kb.op('dve', lambda e: e.tensor_tensor_scan(cum[:], cst.rmask[0:64, :], lgw[:], 0.0, ALU.mult, ALU.add),
                  ['lgw', 'dconst'], ['cum'])
            kb.cp(cumC[:], v3(cum)[:, :, CH - 1], ['cum'], ['cumC'])
            kb.act(gam[:], cumC[:], AF.Exp, ['cumC'], ['gam'])
            kb.act(ex[:], cum[:], AF.Exp, ['cum'], ['ex'])
            kb.tt(QR[:, :, CH:128], v3(r_s), v3(ex), ALU.mult, ['r_s', 'ex'], ['QR'])
            kb.tt(tmp2[:], cum[:], lgw[:], ALU.subtract, ['cum', 'lgw'], ['tmp2'])
            kb.act(ex[:], tmp2[:], AF.Exp, ['tmp2'], ['ex'])
            kb.tt(qbT[:], kk[:], ex[:], ALU.mult, ['kk', 'ex'], ['qbT'])
            kb.cp(QR[:, :, 0:CH], v3(qbT), ['qbT'], ['QR'])
            kb.act(ex[:], cum[:], AF.Exp, ['cum'], ['ex'], scale=-1.0)
            kb.tt(PTs[:], bb[:], ex[:], ALU.mult, ['bb', 'ex'], ['PTs'])
            kb.tt(KTs[:], kp[:], ex[:], ALU.mult, ['kp', 'ex'], ['KTs'])
            for c in range(NG):
                kb.ts(tmp2[:, c * CH:(c + 1) * CH], cum[:, c * CH:(c + 1) * CH], cumC[:, c:c + 1], None, ALU.subtract,
                      reads=['cum', 'cumC'], writes=['tmp2'])
            kb.act(ex[:], tmp2[:], AF.Exp, ['tmp2'], ['ex'], scale=-1.0)
            kb.tt(PhT[:], bb[:], ex[:], ALU.mult, ['bb', 'ex'], ['PhT'])
            kb.tt(KhT[:], kp[:], ex[:], ALU.mult, ['kp', 'ex'], ['KhT'])
            kb.cp(vT_b[:], v_s[:], ['v_s'], ['vT_b'])
            transpose_group(kb, dl, Qb[:], qbT, 64, ['qbT'], 'Qb')
            transpose_group(kb, dl, Ph[:], PhT, 64, ['PhT'], 'Ph')
            transpose_group(kb, dl, Kh[:], KhT, 64, ['KhT'], 'Kh')
            transpose_group(kb, dl, Vt[:], vT_b, 64, ['vT_b'], 'Vt')
            dl.group(states[h], PTs[:], KTs[:], QR, QR[:, :, CH:128], Qb, Ph, Kh, Vt, cst.maskA3, cst.maskR3, gam,
                     yT[:], ['PTs', 'KTs', 'QR', 'Qb', 'Ph', 'Kh', 'Vt', 'gam', 'dconst'], 'yT')
            kb.mm(psA[0:64, 0:GW], cst.ones[0:64, 0:64], yT[:], True, True, ['dones', 'yT'], ['psA'])
            kb.stt(yc[:], psA[0:64, 0:GW], -1.0 / 64, yT[:], ALU.mult, ALU.add, ['psA', 'yT'], ['yc'])
            kb.tt(sq[:], yc[:], yc[:], ALU.mult, ['yc'], ['sq'])
            kb.mm(psA[0:64, 0:GW], cst.ones[0:64, 0:64], sq[:], True, True, ['dones', 'sq'], ['psA'])
            rsq(rs[:], psA[0:64, 0:GW], 64e-5, ['psA'], 'rs', scale=1.0 / 64)
            kb.tt(yc[:], yc[:], rs[:], ALU.mult, ['yc', 'rs'], ['yc'])
            kb.ts(yc[:], yc[:], col(5, h), col(6, h), ALU.mult, ALU.add, reads=['yc', 'cols'], writes=['yc'])
            kb.stt(sq[:], r_s[:], col(4, h), kp[:], ALU.mult, ALU.mult, ['r_s', 'kp', 'cols'], ['sq'])
            kb.mm(psA[0:64, 0:GW], cst.ones[0:64, 0:64], sq[:], True, True, ['dones', 'sq'], ['psA'])
            kb.tt(sq[:], psA[0:64, 0:GW], v_s[:], ALU.mult, ['psA', 'v_s'], ['sq'])
            kb.tt(yc[:], yc[:], sq[:], ALU.add, ['yc', 'sq'], ['yc'])
            ob = outb[(gi * 4 + h) % 2]
            okey = ('outb', (gi * 4 + h) % 2)
            kb.tt(ob[:], yc[:], g_s[:], ALU.mult, ['yc', 'g_s'], [okey])
            kb.dma(ocT[h * 64:(h + 1) * 64, c0:c0 + GW], ob[:], [okey], ())


def gdn_inputs(kb):
    d = {}
    d['convw'] = kb.dram("gd_convw", [128, 24], F32, "ExternalInput")
    d['hcols'] = kb.dram("gd_hcols", [128, 5], F32, "ExternalInput")
    return d


def emit_gdn(kb, qkvT, zT, baT, d, obT, cst):
    GW = NG * CH
    NGRP = S // GW
    NCHK = S // CH
    convw, hcolsd = d['convw'], d['hcols']
    dl = Delta(kb, 128, 128, True, cst.ident, 'dconst', cst.identrep3)
    dl.cst = cst
    cw = kb.sb([128, 24], F32)
    hc = kb.sb([128, 5], F32)
    ab = kb.sb([64, 4 * NCHK], F32)
    kb.dma(cw[:], convw, (), ['cols'])
    kb.dma(hc[:], hcolsd, (), ['cols'])
    for hl in range(2):
        for which, row in ((0, 2 + hl), (1, hl)):
            j = 2 * hl + which
            kb.dma(ab[:, j * NCHK:(j + 1) * NCHK], baT[row, :].rearrange("(c i) -> i c", i=CH), (), ['ab'], slow=True)
    negA = kb.sb([128, 2], F32)
    for hl in range(2):
        kb.act(negA[:, hl:hl + 1], hc[:, 2 * hl:2 * hl + 1], AF.Exp, ['cols'], ['negA'])
    kb.ts(negA[:], negA[:], -1.0, None, ALU.mult, reads=['negA'], writes=['negA'])

    f = lambda r, w=GW, dt=F32: kb.sb([r, w], dt)
    Gtm = [f(64, NCHK) for _ in range(2)]
    eGtm = [f(64, NCHK) for _ in range(2)]
    coefP = [f(64, NCHK) for _ in range(2)]
    beta = [f(64, NCHK) for _ in range(2)]
    ttm = f(64, NCHK)
    psA = kb.ps([128, 512], F32)
    for hl in range(2):
        a_tm = ab[:, (2 * hl) * NCHK:(2 * hl + 1) * NCHK]
        b_tm = ab[:, (2 * hl + 1) * NCHK:(2 * hl + 2) * NCHK]
        kb.act(ttm[:], a_tm, AF.Exp, ['ab', 'cols'], ['ttm'], bias=hc[0:64, 2 * hl + 1:2 * hl + 2])
        kb.act(ttm[:], ttm[:], AF.Ln, ['ttm'], ['ttm'], bias=1.0)
        kb.ts(ttm[:], ttm[:], negA[0:64, hl:hl + 1], None, ALU.mult, reads=['ttm', 'negA'], writes=['ttm'])
        kb.mm(psA[0:64, 0:NCHK], cst.maskR[:, 0:64], ttm[:], True, True, ['dconst', 'ttm'], ['psA'])
        kb.cp(Gtm[hl][:], psA[0:64, 0:NCHK], ['psA'], [('Gtm', hl)])
        kb.act(eGtm[hl][:], Gtm[hl][:], AF.Exp, [('Gtm', hl)], [('eGtm', hl)])
        kb.mm(psA[0:64, 0:NCHK], cst.sel63[:], Gtm[hl][:], True, True, ['dconst', ('Gtm', hl)], ['psA'])
        kb.tt(ttm[:], psA[0:64, 0:NCHK], Gtm[hl][:], ALU.subtract, ['psA', ('Gtm', hl)], ['ttm'])
        kb.act(coefP[hl][:], ttm[:], AF.Exp, ['ttm'], [('coefP', hl)])
        kb.act(beta[hl][:], b_tm, AF.Sigmoid, ['ab'], [('beta', hl)])
        kb.tt(coefP[hl][:], coefP[hl][:], beta[hl][:], ALU.mult, [('coefP', hl), ('beta', hl)], [('coefP', hl)])

    X3 = [kb.sb([128, GW + 3], F32) for _ in range(3)]
    cv = [f(128) for _ in range(3)]
    sq = f(128); rs = f(128); tmp = f(128)
    abt = f(128); Gbc = f(128); eG = f(128)
    kn_b = f(128, GW, BF16); vc_b = f(128, GW, BF16)
    QR = kb.sb([128, NG, 128], BF16)
    RbT = kb.sb([128, NG, CH], BF16)
    gam = kb.sb([128, NG], F32)
    E = kb.sb([64, NG, CH], F32); FA = kb.sb([64, NG, CH], F32); FR = kb.sb([64, NG, CH], F32)
    Qb = kb.sb([64, NG, 128], BF16); Ph = kb.sb([64, NG, 128], BF16); Vt = kb.sb([64, NG, 128], BF16)
    yT = [f(128) for _ in range(2)]
    states = [dl.new_state() for _ in range(2)]
    v3 = lambda t: t[:].rearrange("p (g w) -> p g w", w=CH)

    def rsq(out, ps_ap, eps, reads, wkey):
        kb.ts(tmp[:], ps_ap, 1.0, eps, ALU.mult, ALU.add, reads=reads, writes=['tmp'])
        kb.act(tmp[:], tmp[:], AF.Ln, ['tmp'], ['tmp'])
        kb.act(out, tmp[:], AF.Exp, ['tmp'], [wkey], scale=-0.5)

    for gi in range(NGRP):
        c0 = gi * GW
        for hl in range(2):
            for part in range(3):
                r0 = (part * 2 + hl) * 128
                kb.dma(X3[part][:], qkvT[r0:r0 + 128, c0:c0 + GW + 3], (), [('X3', part)])
                wc = lambda j: cw[:, (part * 2 + hl) * 4 + j:(part * 2 + hl) * 4 + j + 1]
                kb.ts(cv[part][:], X3[part][:, 3:GW + 3], wc(3), None, ALU.mult, reads=[('X3', part), 'cols'],
                      writes=[('cv', part)])
                for j in (2, 1, 0):
                    kb.stt(cv[part][:], X3[part][:, j:GW + j], wc(j), cv[part][:], ALU.mult, ALU.add,
                           [('X3', part), 'cols', ('cv', part)], [('cv', part)])
                kb.act(cv[part][:], cv[part][:], AF.Silu, [('cv', part)], [('cv', part)])
            for part, dst3, scl in ((1, QR[:, :, 0:CH], 1.0), (0, QR[:, :, CH:128], float(128 ** -0.5))):
                kb.tt(sq[:], cv[part][:], cv[part][:], ALU.mult, [('cv', part)], ['sq'])
                kb.mm(psA[:, 0:GW], cst.ones[:], sq[:], True, True, ['dones', 'sq'], ['psA'])
                rsq(rs[:], psA[:, 0:GW], 1e-6, ['psA'], 'rs')
                kb.stt(cv[part][:], cv[part][:], scl, rs[:], ALU.mult, ALU.mult, [('cv', part), 'rs'], [('cv', part)])
                kb.cp(dst3, v3(cv[part]), [('cv', part)], ['QR'])
            kb.cp(kn_b[:], cv[1][:], [('cv', 1)], ['kn_b'])
            kb.cp(vc_b[:], cv[2][:], [('cv', 2)], ['vc_b'], eng='act')
            kb.dma(abt[:], baT[2 + hl, c0:c0 + GW].partition_broadcast(128), (), ['abt'])
            kb.act(abt[:], abt[:], AF.Exp, ['abt', 'cols'], ['abt'], bias=hc[:, 2 * hl + 1:2 * hl + 2])
            kb.act(abt[:], abt[:], AF.Ln, ['abt'], ['abt'], bias=1.0)
            kb.ts(abt[:], abt[:], negA[:, hl:hl + 1], None, ALU.mult, reads=['abt', 'negA'], writes=['abt'])
            kb.op('dve', lambda e: e.tensor_tensor_scan(Gbc[:], cst.rmask[:], abt[:], 0.0, ALU.mult, ALU.add),
                  ['abt', 'dconst'], ['Gbc'])
            kb.act(eG[:], Gbc[:], AF.Exp, ['Gbc'], ['eG'])
            kb.cp(gam[:], v3(eG)[:, :, CH - 1], ['eG'], ['gam'])
            kb.tt(RbT[:], v3(cv[0]), v3(eG), ALU.mult, [('cv', 0), 'eG'], ['RbT'])
            for c in range(NG):
                ci = gi * NG + c
                kb.ts(E[:, c, :], Gbc[0:64, c * CH:(c + 1) * CH], Gtm[hl][:, ci:ci + 1], 0.0, ALU.subtract, ALU.min,
                      reads=['Gbc', ('Gtm', hl)], writes=['E'])
            kb.act(E[:], E[:], AF.Exp, ['E'], ['E'])
            for c in range(NG):
                ci = gi * NG + c
                kb.stt(FA[:, c, :], E[:, c, :], beta[hl][:, ci:ci + 1], cst.maskA3[:, c, :], ALU.mult, ALU.mult,
                       ['E', ('beta', hl), 'dconst'], ['FA'])
                kb.stt(FR[:, c, :], E[:, c, :], beta[hl][:, ci:ci + 1], cst.maskR3[:, c, :], ALU.mult, ALU.mult,
                       ['E', ('beta', hl), 'dconst'], ['FR'])
            transpose_group(kb, dl, Qb[:], kn_b, 128, ['kn_b', ('eGtm', hl)], 'Qb',
                            scale_cols=eGtm[hl][:, gi * NG:(gi + 1) * NG])
            transpose_group(kb, dl, Ph[:], kn_b, 128, ['kn_b', ('coefP', hl)], 'Ph',
                            scale_cols=coefP[hl][:, gi * NG:(gi + 1) * NG])
            transpose_group(kb, dl, Vt[:], vc_b, 128, ['vc_b'], 'Vt')
            yt = yT[(gi * 2 + hl) % 2]
            ykey = ('yT', (gi * 2 + hl) % 2)
            dl.group(states[hl], kn_b[:], None, QR, RbT[:], Qb, Ph, None, Vt, FA[:], FR[:], gam,
                     yt[:], ['kn_b', 'QR', 'RbT', 'Qb', 'Ph', 'Vt', 'gam', 'FA', 'FR', 'dconst'], ykey)
            kb.dma(abt[:], zT[hl * 128:(hl + 1) * 128, c0:c0 + GW], (), ['abt'])
            kb.act(abt[:], abt[:], AF.Silu, ['abt'], ['abt'])
            kb.tt(sq[:], yt[:], yt[:], ALU.mult, [ykey], ['sq'])
            kb.mm(psA[:, 0:GW], cst.ones[:], sq[:], True, True, ['dones', 'sq'], ['psA'])
            kb.ts(tmp[:], psA[:, 0:GW], 1.0 / 128, EPS, ALU.mult, ALU.add, reads=['psA'], writes=['tmp'])
            kb.act(tmp[:], tmp[:], AF.Ln, ['tmp'], ['tmp'])
            kb.act(rs[:], tmp[:], AF.Exp, ['tmp'], ['rs'], scale=-0.5)
            kb.stt(yt[:], yt[:], hc[:, 4:5], rs[:], ALU.mult, ALU.mult, [ykey, 'rs', 'cols'], [ykey])
            kb.tt(yt[:], yt[:], abt[:], ALU.mult, [ykey, 'abt'], [ykey])
            kb.dma(obT[hl * 128:(hl + 1) * 128, c0:c0 + GW], yt[:], [ykey], ())


def build_stageC(final):
    kb = KB()
    T = TT
    NT = TOK // T
    xT = kb.dram("xT", [D, TOK], F32, "ExternalInput")
    oT = [kb.dram(n, [1024, TOK], F32, "ExternalInput") for n in ("oaT", "obT", "ocT")]
    colsd = kb.dram("cols", [128, 9 * KC], F32, "ExternalInput")
    w_g = kb.dram("w_g", [D, 3 * D], F32, "ExternalInput")
    w_br = kb.dram("w_branch", [3072, D], F32, "ExternalInput")
    w_out = kb.dram("w_out", [D, D], F32, "ExternalInput")
    w_gu = kb.dram("w_gate_up", [D, 2 * DFF], F32, "ExternalInput")
    w_dn = kb.dram("w_down", [DFF, D], F32, "ExternalInput")
    xo = kb.dram("xo", [D, TOK], F32, "ExternalOutput")

    ones = kb.sb([128, 128], F32)
    kb.memset(ones[:], 1.0, ['ones'])
    cols = kb.sb([128, 9 * KC], F32)
    kb.dma(cols[:], colsd, (), ['cols'])
    acol1 = kb.sb([128, KC], F32)
    kb.ts(acol1[:], cols[:, 7 * KC:8 * KC], 1.0, None, ALU.add, reads=['cols'], writes=['acol'])
    kb.tt(acol1[:], acol1[:], cols[:, 6 * KC:7 * KC], ALU.mult, ['acol', 'cols'], ['acol'])
    bcol1 = cols[:, 8 * KC:9 * KC]
    gt1 = cols[:, 0:KC]
    acol = kb.sb([128, KC], F32)
    kb.ts(acol[:], cols[:, 2 * KC:3 * KC], 1.0, None, ALU.add, reads=['cols'], writes=['acol'])
    kb.tt(acol[:], acol[:], cols[:, KC:2 * KC], ALU.mult, ['acol', 'cols'], ['acol'])
    bcol = cols[:, 3 * KC:4 * KC]
    gt2 = cols[:, 4 * KC:5 * KC]
    fnw = cols[:, 5 * KC:6 * KC]

    x = kb.sb([128, KC, T], F32)
    merged = kb.sb([128, KC, T], F32)
    mb = kb.sb([128, KC, T], BF16)
    stg = kb.sb([128, 8, T], F32)
    oi_b = kb.sb([128, 8, T], BF16)
    HK = DFF // 128 // 2
    act_b = kb.sb([128, HK, T], BF16)
    sg = [kb.sb([128, T], F32) for _ in range(2)]
    rstd = kb.sb([128, T], F32)
    ps_stat = kb.ps([128, T], F32)
    dn = Dense(kb, T)
    xv = xT.rearrange("(kc p) t -> p kc t", p=128)
    xov = xo.rearrange("(kc p) t -> p kc t", p=128)

    for t in range(NT):
        tsl = slice(t * T, (t + 1) * T)
        kb.dma(x[:], xv[:, :, tsl], (), ['x'])
        modulate_tile(kb, x, 'x', mb, 'mb', acol1, bcol1, ones, merged, 'merged', ps_stat, 'pstat', rstd, 'rstd')

        def act_h(kc, tt_):
            return mb[:, kc, :], ['mb']

        for i in range(3):
            ov = oT[i].rearrange("(kc p) t -> p kc t", p=128)
            kb.dma(stg[:], ov[:, :, tsl], (), ['stg'])
            kb.cp(oi_b[:], stg[:], ['stg'], ['oi_b'])

            def act_fn(kc, tt_):
                return oi_b[:, kc, :], ['oi_b']

            for cb in range(0, D, 256):
                def evac_g(c0, m, ps_ap, pk, tt_):
                    j = c0 // 128
                    kb.act(sg[j][:], ps_ap, AF.Sigmoid, [pk], [('sg', j)])

                def evac(c0, m, ps_ap, pk, tt_, i=i, cb=cb):
                    j = c0 // 128
                    nc_ = (cb + c0) // 128
                    if i == 0:
                        kb.tt(merged[:, nc_, :], ps_ap, sg[j][:], ALU.mult, [pk, ('sg', j)], ['merged'])
                    else:
                        kb.tt(sg[j][:], ps_ap, sg[j][:], ALU.mult, [pk, ('sg', j)], [('sg', j)])
                        kb.tt(merged[:, nc_, :], merged[:, nc_, :], sg[j][:], ALU.add, ['merged', ('sg', j)],
                              ['merged'], eng='pool')
                dn.run(w_g, 0, KC, i * D + cb, 256, act_h, evac_g)
                dn.run(w_br, i * 1024, 8, cb, 256, act_fn, evac)
        for kc in range(KC):
            kb.cp(mb[:, kc, :], merged[:, kc, :], ['merged'], ['mb'], eng='act')
        def act_fn2(kc, tt_):
            return mb[:, kc, :], ['mb']

        def evac2(c0, m, ps_ap, pk, tt_):
            nc_ = c0 // 128
            kb.stt(x[:, nc_, :], ps_ap, gt1[:, nc_:nc_ + 1], x[:, nc_, :], ALU.mult, ALU.add, [pk, 'x', 'cols'], ['x'])
        dn.run(w_out, 0, KC, 0, D, act_fn2, evac2)
        modulate_tile(kb, x, 'x', mb, 'mb', acol, bcol, ones, merged, 'merged', ps_stat, 'pstat', rstd, 'rstd')
        for half in range(2):
            def evac_g(c0, m, ps_ap, pk, tt_):
                kb.act(act_b[:, c0 // 128, :], ps_ap, AF.Silu, [pk], [('actb', c0 // 128)])

            def evac_u(c0, m, ps_ap, pk, tt_):
                j = c0 // 128
                kb.tt(act_b[:, j, :], ps_ap, act_b[:, j, :], ALU.mult, [pk, ('actb', j)], [('actb', j)])
            dn.run(w_gu, 0, KC, half * HK * 128, HK * 128, act_fn2, evac_g)
            dn.run(w_gu, 0, KC, DFF + half * HK * 128, HK * 128, act_fn2, evac_u)

            def act_fn3(kc, tt_):
                return act_b[:, kc, :], [('actb', kc)]

            def evac3(c0, m, ps_ap, pk, tt_):
                nc_ = c0 // 128
                kb.stt(x[:, nc_, :], ps_ap, gt2[:, nc_:nc_ + 1], x[:, nc_, :], ALU.mult, ALU.add, [pk, 'x', 'cols'], ['x'])
            dn.run(w_dn, half * HK * 128, HK, 0, D, act_fn3, evac3)
        if final:
            for k in range(KC):
                kb.act(merged[:, k, :], x[:, k, :], AF.Square, ['x'], ['merged'])
            for k in range(KC):
                kb.mm(ps_stat[:], ones[:], merged[:, k, :], k == 0, k == KC - 1, ['merged', 'ones'], ['pstat'])
            rstd_from_sumsq(kb, rstd[:], ps_stat[:], float(D), EPS, ['pstat'], ['rstd'], merged[:, 0, :], 'merged')
            for k in range(KC):
                kb.stt(x[:, k, :], x[:, k, :], fnw[:, k:k + 1], rstd[:], ALU.mult, ALU.mult, ['x', 'rstd', 'cols'], ['x'])
        kb.dma(xov[:, :, tsl], x[:], ['x'], ())
    return kb.finish()


NSEL = 3396


def build_AB():
    kb = KB()
    xT = kb.dram("xT", [D, S], F32, "ExternalInput")
    ncols = kb.dram("ncols", [128, 3 * KC], F32, "ExternalInput")
    wsel = kb.dram("wsel", [D, NSEL], F32, "ExternalInput")
    md = mla_inputs(kb)
    gd = gdn_inputs(kb)
    rd = rwkv_inputs(kb)
    cd = delta_const_inputs(kb)
    oaT = kb.dram("oaT", [256, S], F32, "ExternalOutput")
    obT = kb.dram("obT", [256, S], F32, "ExternalOutput")
    ocT = kb.dram("ocT", [256, S], F32, "ExternalOutput")
    pint = kb.dram("pint", [NSEL, S + 3], F32, "Internal")

    with kb.phase():
        TM = 256
        ones = kb.sb([128, 128], F32)
        kb.memset(ones[:], 1.0, ['ones'])
        zt = kb.sb([128, 3], F32)
        kb.memset(zt[:], 0.0, ['zt'])
        for r0 in range(0, NSEL, 128):
            m = min(128, NSEL - r0)
            kb.dma(pint[r0:r0 + m, 0:3], zt[0:m, :], ['zt'], ())
        cols = kb.sb([128, 3 * KC], F32)
        kb.dma(cols[:], ncols, (), ['cols'])
        acol = kb.sb([128, KC], F32)
        kb.ts(acol[:], cols[:, KC:2 * KC], 1.0, None, ALU.add, reads=['cols'], writes=['acol'])
        kb.tt(acol[:], acol[:], cols[:, 0:KC], ALU.mult, ['acol', 'cols'], ['acol'])
        bcol = cols[:, 2 * KC:3 * KC]
        wres = kb.sb([128, KC, NSEL], BF16)
        wstg = [kb.sb([128, KC, 128], F32) for _ in range(2)]
        for ci, c0 in enumerate(range(0, NSEL, 128)):
            cw = min(128, NSEL - c0)
            b_ = ci % 2
            kb.dma(wstg[b_][:, :, 0:cw], wsel[:, c0:c0 + cw].rearrange("(kc p) n -> p kc n", p=128), (), [('wstg', b_)])
            kb.cp(wres[:, :, c0:c0 + cw], wstg[b_][:, :, 0:cw], [('wstg', b_)], ['wres'], eng=('pool' if b_ else 'act'))
        xt = [kb.sb([128, KC, TM], F32) for _ in range(2)]
        scr = kb.sb([128, KC, TM], F32)
        hT = [kb.sb([128, KC, TM], BF16) for _ in range(2)]
        rstd = kb.sb([128, TM], F32)
        ps_stat = kb.ps([128, TM], F32)
        pp = [kb.ps([128, 512], F32) for _ in range(4)]
        ob = [kb.sb([128, TM], F32) for _ in range(4)]
        xv = xT.rearrange("(kc p) t -> p kc t", p=128)
        it = 0
        for t in range(S // TM):
            bi = t % 2
            kb.dma(xt[bi][:], xv[:, :, t * TM:(t + 1) * TM], (), [('x', bi)])
            modulate_tile(kb, xt[bi], ('x', bi), hT[bi], ('h', bi), acol, bcol, ones, scr, 'scr', ps_stat, 'pstat',
                          rstd, 'rstd')
            for c0 in range(0, NSEL, 128):
                m = min(128, NSEL - c0)
                j = it % 4
                it += 1
                for k in range(KC):
                    kb.mm(pp[j][0:m, 0:TM], wres[:, k, c0:c0 + m], hT[bi][:, k, :], k == 0, k == KC - 1,
                          ['wres', ('h', bi)], [('pp', j)])
                kb.cp(ob[j][0:m, :], pp[j][0:m, 0:TM], [('pp', j)], [('ob', j)], eng=('act' if j % 2 else 'dve'))
                kb.dma(pint[c0:c0 + m, 3 + t * TM:3 + (t + 1) * TM], ob[j][0:m, :], [('ob', j)], ())
    with kb.phase():
        emit_mla(kb, pint, md, oaT)
    with kb.phase():
        cst = DeltaConsts(kb, cd)
        emit_gdn(kb, pint[1344:2112, :], pint[2112:2368, 3:3 + S], pint[2368:2372, 3:3 + S], gd, obT, cst)
    with kb.phase():
        cst = DeltaConsts(kb, cd)
        emit_rwkv(kb, pint[2372:3140, 2:3 + S], pint[3140:3396, 2:3 + S], rd, ocT, cst)
    return kb.finish()


_PROGS = {}
N_LAUNCH = [0]


def prog(name, builder):
    if name not in _PROGS:
        _PROGS[name] = builder()
    return _PROGS[name]


def launch(name, builder, ims):
    N_LAUNCH[0] += 1
    return run(prog(name, builder), ims)


OFF_G = 1344
OFF_R = 1344 + 4112
OFF_PG = IN_W - 3 * D


def sel_columns(g):
    idx = list(range(0, 1344))
    hs = [2 * g, 2 * g + 1]
    for part in range(3):
        for h in hs:
            idx += list(range(OFF_G + part * 1024 + h * 128, OFF_G + part * 1024 + (h + 1) * 128))
    for h in hs:
        idx += list(range(OFF_G + 3072 + h * 128, OFF_G + 3072 + (h + 1) * 128))
    idx += [OFF_G + 4096 + h for h in hs] + [OFF_G + 4104 + h for h in hs]
    rh = [4 * g + i for i in range(4)]
    for part in range(3):
        for h in rh:
            idx += list(range(OFF_R + part * 1024 + h * 64, OFF_R + part * 1024 + (h + 1) * 64))
    idx += list(range(OFF_R + 3072, OFF_R + 3328))
    assert len(idx) == NSEL
    return np.array(idx)


def ab_inputs(inp, l, b, g, xTb, mod, cc, ss, consts):
    f32 = np.float32
    d = {}
    d['xT'] = xTb
    d['ncols'] = np.ascontiguousarray(np.concatenate(
        [fm_vec(inp['norm1_w'][l]), fm_vec(mod[b, l, 1]), fm_vec(mod[b, l, 0])], axis=1), dtype=f32)
    d['wsel'] = np.ascontiguousarray(inp['w_in'][l][:, sel_columns(g)])
    hs = [2 * g, 2 * g + 1]
    wuq = inp['mla_w_uq'][l].reshape(768, 8, 192)
    wukv = inp['mla_w_ukv'][l].reshape(512, 8, 256)
    wr = wuq[:, hs, 128:]
    wrs = np.concatenate([wr[:, :, 32:], wr[:, :, :32]], axis=2)
    d.update(cc=cc[b], ss=ss[b], qnw=fm_vec(inp['mla_q_norm_w'][l]), kvnw=fm_vec(inp['mla_kv_norm_w'][l]),
             wq_n=np.ascontiguousarray(wuq[:, hs, :128].reshape(768, 256)),
             wq_r=np.ascontiguousarray(wr.reshape(768, 128)), wq_rs=np.ascontiguousarray(wrs.reshape(768, 128)),
             wk=np.ascontiguousarray(wukv[:, hs, :128].reshape(512, 256)),
             wv=np.ascontiguousarray(wukv[:, hs, 128:].reshape(512, 256)), mask=consts['mask'])
    convw = np.zeros((128, 24), f32)
    cwl = inp['gdn_conv_w'][l]
    for part in range(3):
        for i, h in enumerate(hs):
            for j in range(4):
                convw[:, (part * 2 + i) * 4 + j] = cwl[j, part * 1024 + h * 128: part * 1024 + (h + 1) * 128]
    hcols = np.zeros((128, 5), f32)
    for i, h in enumerate(hs):
        hcols[:, 2 * i] = inp['gdn_a_log'][l][h]
        hcols[:, 2 * i + 1] = inp['gdn_dt_bias'][l][h]
    hcols[:, 4] = inp['gdn_norm_w'][l]
    d.update(gd_convw=convw, gd_hcols=hcols)
    rh = [4 * g + i for i in range(4)]
    mu = inp['rwkv_mu'][l]
    W = 1024
    cols = np.zeros((64, 40), f32)
    for part in range(3):
        for i, h in enumerate(rh):
            cols[:, part * 4 + i] = mu[part * W + h * 64: part * W + (h + 1) * 64]
    for j, nm in enumerate(['rwkv_w0', 'rwkv_a0', 'rwkv_k_k', 'rwkv_k_a', 'rwkv_r_k', 'rwkv_lnx_w', 'rwkv_lnx_b']):
        v = inp[nm][l].reshape(-1)
        for i, h in enumerate(rh):
            cols[:, 12 + j * 4 + i] = v[h * 64:(h + 1) * 64]
    mulow = np.zeros((128, 3), f32)
    mulow[:64, 0] = mu[3 * W:3 * W + 64]
    mulow[:64, 1] = mu[3 * W + 64:3 * W + 128]
    mulow[:, 2] = mu[3 * W + 128:]
    hc = np.concatenate([np.arange(h * 64, (h + 1) * 64) for h in rh])
    d.update(rw_cols=cols, rw_mulow=mulow, rw_w_up=np.ascontiguousarray(inp['rwkv_w_up'][l][:, hc]),
             rw_a_up=np.ascontiguousarray(inp['rwkv_a_up'][l][:, hc]),
             rw_g_up=np.ascontiguousarray(inp['rwkv_g_up'][l][:, hc]))
    for nm, _ in DCONST_SHAPES:
        d[nm] = consts[nm]
    return d


def run_prep(inp):
    c = inp['c']
    pos = inp['positions']
    cT = np.ascontiguousarray(c.T.reshape(KC, 128, B).transpose(1, 0, 2).reshape(128, KC * B))
    wall = np.concatenate([inp['w_ada'][l] for l in range(DEPTH)], axis=1)
    ball = np.concatenate([inp['b_ada'][l] for l in range(DEPTH)], axis=0)
    inv = (1.0 / (np.float32(10000.0) ** (np.arange(0, 64, 2, dtype=np.float32) / np.float32(64)))).astype(np.float32)
    cst = np.zeros((64, 2), np.float32)
    cst[:, 0] = np.tile(inv, 2)
    cst[:32, 1] = -1
    cst[32:, 1] = 1
    ims = []
    for i in range(NCORES):
        cs = slice(i * PREP_COLS, (i + 1) * PREP_COLS)
        ims.append(dict(cT=cT, wada=np.ascontiguousarray(wall[:, cs]),
                        bada=np.ascontiguousarray(np.broadcast_to(ball[cs], (B, PREP_COLS))),
                        pos=np.ascontiguousarray(np.broadcast_to(pos[i % B], (64, S))), cst=cst))
    res = launch('prep', build_prep, ims)
    mod = np.concatenate([r['mod'] for r in res], axis=1).reshape(B, DEPTH, 6, D)
    cc = [res[b]['cc'] for b in range(B)]
    ss = [res[b]['ss'] for b in range(B)]
    return mod, cc, ss


def run_layer(inp, l, xsh, mod, cc, ss, consts, final):
    xTb = [np.ascontiguousarray(np.concatenate(xsh[b * 4:(b + 1) * 4], axis=1)) for b in range(B)]
    ims = [ab_inputs(inp, l, i // 4, i % 4, xTb[i // 4], mod, cc, ss, consts) for i in range(NCORES)]
    res = launch('AB', build_AB, ims)
    wg = np.ascontiguousarray(inp['w_in'][l][:, OFF_PG:])
    ims = []
    for i in range(NCORES):
        b, q = i // 4, i % 4
        tsl = slice(q * TOK, (q + 1) * TOK)
        o = {nm: np.ascontiguousarray(np.concatenate([res[b * 4 + g][nm] for g in range(4)], axis=0)[:, tsl])
             for nm in ('oaT', 'obT', 'ocT')}
        cols = np.concatenate([fm_vec(mod[b, l, 2]), fm_vec(inp['norm2_w'][l]), fm_vec(mod[b, l, 4]),
                               fm_vec(mod[b, l, 3]), fm_vec(mod[b, l, 5]), fm_vec(inp['final_norm_w']),
                               fm_vec(inp['norm1_w'][l]), fm_vec(mod[b, l, 1]), fm_vec(mod[b, l, 0])], axis=1)
        ims.append(dict(xT=xsh[i], cols=np.ascontiguousarray(cols, dtype=np.float32), w_g=wg,
                        w_branch=inp['w_branch'][l], w_out=inp['w_out'][l], w_gate_up=inp['w_gate_up'][l],
                        w_down=inp['w_down'][l], **o))
    name = 'Cf' if final else 'C'
    res = launch(name, (lambda: build_stageC(True)) if final else (lambda: build_stageC(False)), ims)
    return [r['xo'] for r in res]


def make_consts():
    c = delta_consts()
    c['mask'] = causal_masks()
    return c


def kernel(**inp):
    inp = {k: np.asarray(v) for k, v in inp.items()}
    N_LAUNCH[0] = 0
    mod, cc, ss = run_prep(inp)
    consts = make_consts()
    xf = inp['x'].reshape(B * S, D)
    xsh = [np.ascontiguousarray(xf[i * TOK:(i + 1) * TOK].T) for i in range(NCORES)]
    for l in range(DEPTH):
        xsh = run_layer(inp, l, xsh, mod, cc, ss, consts, final=(l == DEPTH - 1))
    out = np.concatenate([s_.T for s_ in xsh], axis=0).reshape(B, S, D)
    return np.ascontiguousarray(out, dtype=np.float32)
```

```python
import contextlib
import numpy as np
import concourse.bass as bass
import concourse.mybir as mybir
from concourse.bass_utils import run_bass_kernel_spmd

F32 = mybir.dt.float32
BF16 = mybir.dt.bfloat16
I32 = mybir.dt.int32
AF = mybir.ActivationFunctionType
ALU = mybir.AluOpType

NCORES = 8
D = 2048
B = 2
S = 8192
DEPTH = 4
KC = D // 128
IN_W = 14928
DFF = 5632
EPS = 1e-6

ENGS = ['pe', 'act', 'dve', 'pool', 'sp']
EPOCH = 60000
NDMA = 12
DMA_EPOCH = 3900


class KB:
    def __init__(self):
        self.nc = bass.Bass("TRN2", target_bir_lowering=False)
        self.stack = contextlib.ExitStack()
        self.tstack = self.stack
        self.nsem = 0
        self.stream = 0
        self.sops = {}
        self.cnt = {}
        self.sem = {}
        self.dsem = {}
        self.dcnt = {}
        self.drr = {}
        self.waited = {}
        self.final_ops = {e: [] for e in ENGS}
        self.last_w = {}
        self.readers = {}
        self.dma_toks = []
        self.ntile = 0
        self.set_stream(0)

    def _newsem(self):
        self.nsem += 1
        return self.stack.enter_context(self.nc.semaphore("s%d" % self.nsem))

    def set_stream(self, i):
        self.stream = i
        if i not in self.sops:
            self.sops[i] = []
            self.dsem[i] = [self._newsem() for _ in range(NDMA)]
            self.dcnt[i] = [0] * NDMA
            self.drr[i] = 0
            for e in ENGS:
                self.cnt[(e, i)] = 0
                self.sem[(e, i)] = self._newsem()
                self.waited[(e, i)] = {}

    def dram(self, name, shape, dt, kind):
        return self.nc.dram_tensor(name, list(shape), dt, kind=kind).ap()

    def sb(self, shape, dt, name=None):
        self.ntile += 1
        return self.tstack.enter_context(
            self.nc.sbuf_tensor(name or ("t%d" % self.ntile), list(shape), dt))

    def ps(self, shape, dt=F32, name=None):
        self.ntile += 1
        return self.tstack.enter_context(
            self.nc.psum_tensor(name or ("p%d" % self.ntile), list(shape), dt))

    @contextlib.contextmanager
    def phase(self):
        old = self.tstack
        self.tstack = contextlib.ExitStack()
        try:
            yield
        finally:
            self.barrier()
            self.tstack.close()
            self.tstack = old

    def flush(self):
        lists = [(k, v) for k, v in sorted(self.sops.items()) if v]
        if len(lists) == 1:
            for (e, deps, fn, sem, inc) in lists[0][1]:
                self.final_ops[e].append((deps, fn, sem, inc))
        elif lists:
            pos = [0] * len(lists)
            tot = [len(v) for _, v in lists]
            n = sum(tot)
            for _ in range(n):
                best, bf = None, None
                for i in range(len(lists)):
                    if pos[i] < tot[i]:
                        fr = pos[i] / tot[i]
                        if bf is None or fr < bf:
                            best, bf = i, fr
                (e, deps, fn, sem, inc) = lists[best][1][pos[best]]
                pos[best] += 1
                self.final_ops[e].append((deps, fn, sem, inc))
        for k in self.sops:
            self.sops[k] = []

    def barrier(self):
        self.flush()
        toks = {}
        for (e, st), c in self.cnt.items():
            if c > 0 and e != 'sp':
                toks[id(self.sem[(e, st)])] = (self.sem[(e, st)], c)
        for (_, sem, val) in self.dma_toks:
            k = id(sem)
            if k not in toks or toks[k][1] < val:
                toks[k] = (sem, val)
        self.dma_toks = [('dma', s_, v_) for (s_, v_) in toks.values()]
        for e in ENGS:
            deps = list(toks.values())
            self.final_ops[e].append((deps, None, None, 0))
            for st in self.sops:
                for k, (sem, val) in toks.items():
                    if self.waited[(e, st)].get(k, 0) < val:
                        self.waited[(e, st)][k] = val
        self.last_w = {}
        self.readers = {}

    def _deps(self, eng, reads, writes):
        deps = {}

        def add(tok):
            te, sem, val = tok
            if te == 'pe' and eng == 'pe':
                return
            k = id(sem)
            if k not in deps or deps[k][1] < val:
                deps[k] = (sem, val)

        for k in reads:
            w = self.last_w.get(k)
            if w is not None:
                add(w)
        for k in writes:
            w = self.last_w.get(k)
            if w is not None:
                add(w)
            for r in self.readers.get(k, {}).values():
                if isinstance(r, list):
                    for t in r:
                        add(t)
                else:
                    add(r)
        out = []
        wd = self.waited[(eng, self.stream)]
        for k, (sem, val) in deps.items():
            if wd.get(k, 0) >= val:
                continue
            wd[k] = val
            out.append((sem, val))
        return out

    def _update(self, tok, reads, writes, is_dma):
        for k in writes:
            self.last_w[k] = tok
            self.readers[k] = {}
        for k in reads:
            rd = self.readers.setdefault(k, {})
            if is_dma:
                rd.setdefault('dma', []).append(tok)
            else:
                rd[tok[0]] = tok

    def op(self, eng, fn, reads=(), writes=()):
        st = self.stream
        reads = [(st, k) for k in reads]
        writes = [(st, k) for k in writes]
        deps = self._deps(eng, reads, writes)
        es = (eng, st)
        if self.cnt[es] >= EPOCH:
            self.sem[es] = self._newsem()
            self.cnt[es] = 0
        self.cnt[es] += 1
        tok = (eng, self.sem[es], self.cnt[es])
        self.sops[st].append((eng, deps, fn, self.sem[es], 1))
        self._update(tok, reads, writes, False)

    def dma(self, out, in_, reads=(), writes=(), q='sp', slow=False):
        st = self.stream
        reads = [(st, k) for k in reads]
        writes = [(st, k) for k in writes]
        i = self.drr[st]
        self.drr[st] = (i + 1) % NDMA
        dsem, dcnt = self.dsem[st], self.dcnt[st]
        if dcnt[i] >= DMA_EPOCH:
            dsem[i] = self._newsem()
            dcnt[i] = 0
        deps = self._deps(q, reads, writes)
        wd = self.waited[(q, st)]
        if dcnt[i] > 0:
            k = id(dsem[i])
            v = 16 * dcnt[i]
            if wd.get(k, 0) < v:
                wd[k] = v
                deps.append((dsem[i], v))
        dcnt[i] += 1
        tok = ('dma', dsem[i], 16 * dcnt[i])
        if slow:
            fn = lambda e: e.dma_start(out=out, in_=in_, allow_slow_non_contiguous=True)
        else:
            fn = lambda e: e.dma_start(out=out, in_=in_)
        self.sops[st].append((q, deps, fn, dsem[i], 16))
        self._update(tok, reads, writes, True)
        self.dma_toks.append(tok)

    def finish(self):
        self.flush()
        final = {}
        for (_, sem, val) in self.dma_toks:
            k = id(sem)
            if k not in final or final[k][1] < val:
                final[k] = (sem, val)
        for (e, st), c in self.cnt.items():
            if c > 0 and e != 'sp':
                final[id(self.sem[(e, st)])] = (self.sem[(e, st)], c)
        self.final_ops['sp'].append((list(final.values()), None, None, 0))
        nc = self.nc
        ops = self.final_ops

        def replay(name, e):
            for deps, fn, sem, inc in ops[name]:
                for (s, v) in deps:
                    e.wait_ge(s, v)
                if fn is not None:
                    fn(e).then_inc(sem, inc)

        with nc.Block() as block:
            @block.tensor
            def _(e):
                replay('pe', e)

            @block.scalar
            def _(e):
                replay('act', e)

            @block.vector
            def _(e):
                replay('dve', e)

            @block.gpsimd
            def _(e):
                replay('pool', e)

            @block.sync
            def _(e):
                replay('sp', e)
        self.stack.close()
        return nc

    def mm(self, out, lhsT, rhs, start, stop, reads, writes):
        self.op('pe', lambda e: e.matmul(out, lhsT, rhs, start=start, stop=stop), reads, writes)

    def tr(self, out, in_, ident, reads, writes):
        self.op('pe', lambda e: e.transpose(out, in_, ident), reads, writes)

    def act(self, out, in_, func, reads, writes, bias=None, scale=None, accum_out=None):
        kw = {}
        if bias is not None:
            kw['bias'] = bias
        if scale is not None:
            kw['scale'] = scale
        if accum_out is not None:
            kw['accum_out'] = accum_out
        self.op('act', lambda e: e.activation(out, in_, func, **kw), reads, writes)

    def ts(self, out, in0, s1, s2, op0, op1=None, reads=(), writes=(), eng='dve'):
        if op1 is None:
            self.op(eng, lambda e: e.tensor_scalar(out, in0, s1, None, op0), reads, writes)
        else:
            self.op(eng, lambda e: e.tensor_scalar(out, in0, s1, s2, op0, op1), reads, writes)

    def tt(self, out, in0, in1, op, reads, writes, eng='dve'):
        self.op(eng, lambda e: e.tensor_tensor(out, in0, in1, op), reads, writes)

    def stt(self, out, in0, scalar, in1, op0, op1, reads, writes):
        self.op('dve', lambda e: e.scalar_tensor_tensor(out, in0, scalar, in1, op0, op1), reads, writes)

    def cp(self, out, in_, reads, writes, eng='dve'):
        if eng == 'act':
            self.op('act', lambda e: e.activation(out, in_, AF.Copy), reads, writes)
        else:
            self.op(eng, lambda e: e.tensor_copy(out, in_), reads, writes)

    def memset(self, ap, val, writes, eng='dve'):
        self.op(eng, lambda e: e.memset(ap, val), (), writes)


def run(nc, in_maps):
    res = run_bass_kernel_spmd(nc, in_maps, core_ids=list(range(len(in_maps))))
    return res.results


def fm_vec(v):
    v = np.asarray(v)
    return np.ascontiguousarray(v.reshape(-1, 128).T)


class Dense:
    def __init__(self, kb, T, slab_w=256, slab_kc=16, nps=2):
        self.kb = kb
        self.T = T
        self.W = slab_w
        self.SK = slab_kc
        self.stage = [kb.sb([128, slab_kc, slab_w], F32) for _ in range(2)]
        self.wbf = [kb.sb([128, slab_kc, slab_w], BF16) for _ in range(2)]
        self.nsub = slab_w // 128
        self.psum = [[kb.ps([128, 512], F32) for _ in range(self.nsub)] for _ in range(nps)]
        self.nps = nps
        self.it = 0
        self.git = 0

    def run(self, w_ap, k0_rows, kc_n, n0, n_cols, act_fn, evac_fn, ntok=1):
        kb = self.kb
        T = self.T
        assert ntok == 1 or kc_n <= self.SK
        for c0 in range(0, n_cols, self.W):
            cw = min(self.W, n_cols - c0)
            nsub = (cw + 127) // 128
            gs = []
            for t in range(ntok):
                gs.append(self.git % self.nps)
                self.git += 1
            for s0 in range(0, kc_n, self.SK):
                sk = min(self.SK, kc_n - s0)
                b = self.it % 2
                self.it += 1
                src = w_ap[k0_rows + s0 * 128: k0_rows + (s0 + sk) * 128, n0 + c0: n0 + c0 + cw]
                src = src.rearrange("(kc p) n -> p kc n", p=128)
                kb.dma(self.stage[b][:, 0:sk, 0:cw], src, (), [('wst', id(self), b)])
                kb.cp(self.wbf[b][:, 0:sk, 0:cw], self.stage[b][:, 0:sk, 0:cw],
                      [('wst', id(self), b)], [('wbf', id(self), b)], eng='pool')
                for t in range(ntok):
                    g = gs[t]
                    for j in range(nsub):
                        m = min(128, cw - j * 128)
                        pk = ('dps', id(self), g, j)
                        for kc in range(sk):
                            a_ap, a_keys = act_fn(s0 + kc, t)
                            kb.mm(self.psum[g][j][0:m, 0:T], self.wbf[b][:, kc, j * 128: j * 128 + m], a_ap,
                                  start=(s0 + kc == 0), stop=(s0 + kc == kc_n - 1),
                                  reads=[('wbf', id(self), b)] + list(a_keys), writes=[pk])
                    if s0 + sk == kc_n:
                        for j in range(nsub):
                            m = min(128, cw - j * 128)
                            evac_fn(c0 + j * 128, m, self.psum[g][j][0:m, 0:T], ('dps', id(self), g, j), t)


def rstd_from_sumsq(kb, out, ps_ap, n, eps, reads, writes, tmp, tmpkey):
    kb.ts(tmp, ps_ap, 1.0 / n, eps, ALU.mult, ALU.add, reads=reads, writes=[tmpkey])
    kb.act(tmp, tmp, AF.Ln, [tmpkey], [tmpkey])
    kb.act(out, tmp, AF.Exp, [tmpkey], writes, scale=-0.5)


PREP_COLS = DEPTH * 6 * D // NCORES


def build_prep():
    kb = KB()
    cT = kb.dram("cT", [128, KC * B], F32, "ExternalInput")
    wada = kb.dram("wada", [D, PREP_COLS], F32, "ExternalInput")
    bada = kb.dram("bada", [B, PREP_COLS], F32, "ExternalInput")
    pos = kb.dram("pos", [64, S], I32, "ExternalInput")
    cst = kb.dram("cst", [64, 2], F32, "ExternalInput")
    mod = kb.dram("mod", [B, PREP_COLS], F32, "ExternalOutput")
    cc = kb.dram("cc", [64, S], F32, "ExternalOutput")
    ss = kb.dram("ss", [64, S], F32, "ExternalOutput")

    c_sb = kb.sb([128, KC * B], F32)
    b_sb = kb.sb([B, PREP_COLS], F32)
    o_sb = kb.sb([B, PREP_COLS], F32)
    kb.dma(c_sb[:], cT, (), ['c'])
    kb.dma(b_sb[:], bada, (), ['b'])
    kb.act(c_sb[:], c_sb[:], AF.Silu, ['c'], ['c'])
    wt = [kb.sb([128, KC, 512], F32) for _ in range(2)]
    pp = [kb.ps([B, 512], F32) for _ in range(2)]
    c3 = c_sb[:].rearrange("p (k b) -> p k b", b=B)
    for ci in range(PREP_COLS // 512):
        bi = ci % 2
        kb.dma(wt[bi][:], wada[:, ci * 512:(ci + 1) * 512].rearrange("(kc p) n -> p kc n", p=128),
               (), [('w', bi)])
        for k in range(KC):
            kb.mm(pp[bi][:], c3[:, k, :], wt[bi][:, k, :], start=(k == 0), stop=(k == KC - 1),
                  reads=['c', ('w', bi)], writes=[('pp', bi)])
        kb.tt(o_sb[:, ci * 512:(ci + 1) * 512], pp[bi][:], b_sb[:, ci * 512:(ci + 1) * 512], ALU.add,
              [('pp', bi), 'b'], ['o'])
    kb.dma(mod, o_sb[:], ['o'], ())

    CH = 2048
    cs = kb.sb([64, 2], F32)
    kb.dma(cs[:], cst, (), ['cs'])
    pi = kb.sb([64, CH], I32)
    ang = kb.sb([64, CH], F32)
    r = kb.sb([64, CH], F32)
    ki = kb.sb([64, CH], I32)
    kf = kb.sb([64, CH], F32)
    m = kb.sb([64, CH], F32)
    osb = {'s': kb.sb([64, CH], F32), 'c': kb.sb([64, CH], F32)}
    k_ = 'rope'
    for ci in range(S // CH):
        kb.dma(pi[:], pos[:, ci * CH:(ci + 1) * CH], (), [k_])
        kb.cp(ang[:], pi[:], [k_], [k_])
        kb.ts(ang[:], ang[:], cs[:, 0:1], None, ALU.mult, reads=[k_, 'cs'], writes=[k_])
        kb.ts(kf[:], ang[:], float(1.0 / (2 * np.pi)), None, ALU.mult, reads=[k_], writes=[k_])
        kb.cp(ki[:], kf[:], [k_], [k_])
        kb.cp(kf[:], ki[:], [k_], [k_])
        kb.stt(ang[:], kf[:], -6.28125, ang[:], ALU.mult, ALU.add, [k_], [k_])
        kb.stt(ang[:], kf[:], -0.0019353071795864769, ang[:], ALU.mult, ALU.add, [k_], [k_])
        for which, shift, dst in (('s', 0.0, ss), ('c', float(np.pi / 2), cc)):
            kb.ts(r[:], ang[:], shift, None, ALU.add, reads=[k_], writes=[k_])
            kb.ts(m[:], r[:], float(np.pi), None, ALU.is_gt, reads=[k_], writes=[k_])
            kb.stt(r[:], m[:], float(-2 * np.pi), r[:], ALU.mult, ALU.add, [k_], [k_])
            kb.ts(m[:], r[:], float(-np.pi), None, ALU.is_lt, reads=[k_], writes=[k_])
            kb.stt(r[:], m[:], float(2 * np.pi), r[:], ALU.mult, ALU.add, [k_], [k_])
            kb.ts(r[:], r[:], 3.1415925, -3.1415925, ALU.min, ALU.max, reads=[k_], writes=[k_])
            o = osb[which]
            kb.act(o[:], r[:], AF.Sin, [k_], [(k_, which)])
            if which == 's':
                kb.ts(o[:], o[:], cs[:, 1:2], None, ALU.mult, reads=[(k_, which), 'cs'], writes=[(k_, which)])
            kb.dma(dst[:, ci * CH:(ci + 1) * CH], o[:], [(k_, which)], ())
    return kb.finish()


TOK = B * S // NCORES
TT = 512


def modulate_tile(kb, xT, xkey, hT, hkey, acol, bcol, ones, scr, scrkey, ps_stat, pskey, rstd, rkey, xk=()):
    for k in range(KC):
        kb.act(scr[:, k, :], xT[:, k, :], AF.Square, [xkey], [scrkey])
    for k in range(KC):
        kb.mm(ps_stat[:], ones[:], scr[:, k, :], start=(k == 0), stop=(k == KC - 1),
              reads=[scrkey, 'ones'], writes=[pskey])
    rstd_from_sumsq(kb, rstd[:], ps_stat[:], float(D), EPS, [pskey], [rkey], scr[:, 0, :], scrkey)
    for k in range(KC):
        kb.stt(scr[:, k, :], xT[:, k, :], acol[:, k:k + 1], rstd[:], ALU.mult, ALU.mult,
               [xkey, rkey, 'acol'], [scrkey])
        kb.act(hT[:, k, :], scr[:, k, :], AF.Identity, [scrkey, 'acol'] + list(xk), [hkey], bias=bcol[:, k:k + 1])


def build_stageA():
    kb = KB()
    xT = kb.dram("xT", [D, TOK], F32, "ExternalInput")
    nw = kb.dram("nw", [128, KC], F32, "ExternalInput")
    sc = kb.dram("sc", [128, KC], F32, "ExternalInput")
    sh = kb.dram("sh", [128, KC], F32, "ExternalInput")
    w_in = kb.dram("w_in", [D, IN_W], F32, "ExternalInput")
    pT = kb.dram("pT", [IN_W, TOK], F32, "ExternalOutput")

    ones = kb.sb([128, 128], F32)
    kb.memset(ones[:], 1.0, ['ones'])
    nw_s = kb.sb([128, KC], F32)
    sc_s = kb.sb([128, KC], F32)
    sh_s = kb.sb([128, KC], F32)
    kb.dma(nw_s[:], nw, (), ['nw'])
    kb.dma(sc_s[:], sc, (), ['sc'])
    kb.dma(sh_s[:], sh, (), ['acol'])
    kb.ts(sc_s[:], sc_s[:], 1.0, None, ALU.add, reads=['sc'], writes=['sc'])
    kb.tt(sc_s[:], sc_s[:], nw_s[:], ALU.mult, ['sc', 'nw', 'acol'], ['acol'])

    TM = 256
    hT = kb.sb([128, KC, TOK], BF16)
    xt = [kb.sb([128, KC, TM], F32) for _ in range(2)]
    scr = kb.sb([128, KC, TM], F32)
    rstd = kb.sb([128, TM], F32)
    ps_stat = kb.ps([128, TM], F32)
    xv = xT.rearrange("(kc p) t -> p kc t", p=128)
    for t in range(TOK // TM):
        bi = t % 2
        kb.dma(xt[bi][:], xv[:, :, t * TM:(t + 1) * TM], (), [('x', bi)])
        modulate_tile(kb, xt[bi], ('x', bi), hT[:, :, t * TM:(t + 1) * TM], ('h', t // 2), sc_s, sh_s, ones,
                      scr, 'scr', ps_stat, 'pstat', rstd, 'rstd')

    NT = TOK // TT
    dn = Dense(kb, TT)
    ob = [kb.sb([128, TT], F32) for _ in range(4)]
    cnt = [0]

    def act_fn(kc, t):
        return hT[:, kc, t * TT:(t + 1) * TT], [('h', t)]

    def evac(c0, m, ps_ap, pk, t):
        i = cnt[0] % 4
        cnt[0] += 1
        kb.cp(ob[i][0:m, :], ps_ap, [pk], [('ob', i)], eng=('act' if i % 2 else 'dve'))
        kb.dma(pT[c0:c0 + m, t * TT:(t + 1) * TT], ob[i][0:m, :], [('ob', i)], ())
    dn.run(w_in, 0, KC, 0, IN_W, act_fn, evac, ntok=NT)
    return kb.finish()


MLA_SCALE = float(192 ** -0.5)


def norm_tile(kb, src, nkc, T, wcol, dst, ones, scr, ps_stat, rstd, key_in, key_out, n):
    for k in range(nkc):
        kb.act(scr[:, k, 0:T], src[:, k, 0:T], AF.Square, [key_in], ['nscr'])
    for k in range(nkc):
        kb.mm(ps_stat[:, 0:T], ones[:], scr[:, k, 0:T], start=(k == 0), stop=(k == nkc - 1),
              reads=['nscr', 'ones'], writes=['nps'])
    rstd_from_sumsq(kb, rstd[:, 0:T], ps_stat[:, 0:T], float(n), EPS, ['nps'], ['nrstd'], scr[:, 0, 0:T], 'nscr')
    for k in range(nkc):
        kb.stt(dst[:, k, 0:T], src[:, k, 0:T], wcol[:, k:k + 1], rstd[:, 0:T], ALU.mult, ALU.mult,
               [key_in, 'nrstd', 'nw'], [key_out])


def load_cast(kb, dst_bf, src_ap, stage, nkc, ncol, key):
    kb.dma(stage[:, 0:nkc, 0:ncol], src_ap.rearrange("(kc p) n -> p kc n", p=128), (), ['wstage'])
    kb.cp(dst_bf[:, 0:nkc, 0:ncol], stage[:, 0:nkc, 0:ncol], ['wstage'], [key])


def mla_inputs(kb, sfx='', shared=None):
    d = {}
    d['cc'] = kb.dram("cc" + sfx, [64, S], F32, "ExternalInput")
    d['ss'] = kb.dram("ss" + sfx, [64, S], F32, "ExternalInput")
    d['qnw'] = kb.dram("qnw" + sfx, [128, 6], F32, "ExternalInput")
    d['kvnw'] = kb.dram("kvnw" + sfx, [128, 4], F32, "ExternalInput")
    d['wq_n'] = kb.dram("wq_n" + sfx, [768, 256], F32, "ExternalInput")
    d['wq_r'] = kb.dram("wq_r" + sfx, [768, 128], F32, "ExternalInput")
    d['wq_rs'] = kb.dram("wq_rs" + sfx, [768, 128], F32, "ExternalInput")
    d['wk'] = kb.dram("wk" + sfx, [512, 256], F32, "ExternalInput")
    d['wv'] = kb.dram("wv" + sfx, [512, 256], F32, "ExternalInput")
    d['mask'] = kb.dram("mask" + sfx, [128, 2048], F32, "ExternalInput")
    if shared:
        d.update(shared)
    return d


def emit_mla(kb, pint, d, oT):
    TP = 512
    NTILE = S // TP
    cqT = pint[0:768, 3:3 + S]
    ckvT = pint[768:1280, 3:3 + S]
    krT = pint[1280:1344, 3:3 + S]
    CCd, SSd, qnw, kvnw = d['cc'], d['ss'], d['qnw'], d['kvnw']
    wq_n, wq_r, wq_rs, wk, wv, maskd = d['wq_n'], d['wq_r'], d['wq_rs'], d['wk'], d['wv'], d['mask']

    ones = kb.sb([128, 128], F32)
    kb.memset(ones[:], 1.0, ['ones'])
    ones_b = kb.sb([128, 128], BF16)
    kb.memset(ones_b[:], 1.0, ['ones_b'])
    qnw_s = kb.sb([128, 6], F32)
    kvnw_s = kb.sb([128, 4], F32)
    kb.dma(qnw_s[:], qnw, (), ['nw'])
    kb.dma(kvnw_s[:], kvnw, (), ['nw'])
    mstage = kb.sb([128, 2048], F32)
    mask_b = kb.sb([128, 2048], BF16)
    kb.dma(mstage[:], maskd, (), ['mstage'])
    kb.cp(mask_b[:], mstage[:], ['mstage'], ['mask'])

    QTn = kb.sb([128, S], BF16)
    QTr = kb.sb([64, S], BF16)
    KTn = kb.sb([128, S], BF16)
    KTr = kb.sb([64, S], BF16)
    V = kb.sb([128, S // 128, 128], BF16)
    src = [kb.sb([128, 6, TP], F32) for _ in range(2)]
    scr = kb.sb([128, 6, TP], F32)
    cn = kb.sb([128, 6, TP], BF16)
    rstd = kb.sb([128, TP], F32)
    wstage = kb.sb([128, 6, 128], F32)
    wqn_b = kb.sb([128, 6, 128], BF16)
    wqr_b = kb.sb([128, 6, 64], BF16)
    wqs_b = kb.sb([128, 6, 64], BF16)
    wk_b = kb.sb([128, 4, 128], BF16)
    wv_b = kb.sb([128, 4, 128], BF16)
    cc_s = [kb.sb([64, TP], F32) for _ in range(2)]
    ss_s = [kb.sb([64, TP], F32) for _ in range(2)]
    kr_s = [kb.sb([64, TP], F32) for _ in range(2)]
    krs_s = [kb.sb([64, TP], F32) for _ in range(2)]
    t1 = kb.sb([64, TP], F32)
    t2 = kb.sb([64, TP], F32)
    pT = [kb.sb([128, 512], BF16) for _ in range(3)]
    rec = kb.sb([128, 512], F32)
    osb = [kb.sb([128, 512], F32) for _ in range(2)]
    ps = [kb.ps([128, 512], F32) for _ in range(8)]

    def rope_combine(dst, a_ap, b_ap, bi, reads, wkey):
        kb.tt(t1[:], a_ap, cc_s[bi][:], ALU.mult, reads + [('cs', bi)], ['t1'])
        kb.tt(t2[:], b_ap, ss_s[bi][:], ALU.mult, reads + [('cs', bi)], ['t2'])
        kb.tt(dst, t1[:], t2[:], ALU.add, ['t1', 't2'], [wkey])

    for t in range(NTILE):
        bi = t % 2
        sl = slice(t * TP, (t + 1) * TP)
        kb.dma(cc_s[bi][:], CCd[:, sl], (), [('cs', bi)])
        kb.dma(ss_s[bi][:], SSd[:, sl], (), [('cs', bi)])
        kb.dma(kr_s[bi][:], krT[:, sl], (), [('kr', bi)])
        kb.dma(krs_s[bi][0:32, :], krT[32:64, sl], (), [('kr', bi)])
        kb.dma(krs_s[bi][32:64, :], krT[0:32, sl], (), [('kr', bi)])
        rope_combine(KTr[:, sl], kr_s[bi][:], krs_s[bi][:], bi, [('kr', bi)], ('KTr', t))

    cqv = cqT.rearrange("(kc p) t -> p kc t", p=128)
    ckvv = ckvT.rearrange("(kc p) t -> p kc t", p=128)
    for h in range(2):
        load_cast(kb, wqn_b, wq_n[:, h * 128:(h + 1) * 128], wstage, 6, 128, 'wqn')
        load_cast(kb, wqr_b, wq_r[:, h * 64:(h + 1) * 64], wstage, 6, 64, 'wqr')
        load_cast(kb, wqs_b, wq_rs[:, h * 64:(h + 1) * 64], wstage, 6, 64, 'wqs')
        load_cast(kb, wk_b, wk[:, h * 128:(h + 1) * 128], wstage, 4, 128, 'wk')
        load_cast(kb, wv_b, wv[:, h * 128:(h + 1) * 128], wstage, 4, 128, 'wv')
        for t in range(NTILE):
            bi = t % 2
            sl = slice(t * TP, (t + 1) * TP)
            kb.dma(src[bi][:], cqv[:, :, sl], (), [('src', bi)])
            kb.dma(cc_s[bi][:], CCd[:, sl], (), [('cs', bi)])
            kb.dma(ss_s[bi][:], SSd[:, sl], (), [('cs', bi)])
            norm_tile(kb, src[bi], 6, TP, qnw_s, cn, ones, scr, ps[0], rstd, ('src', bi), 'cn', 768)
            for k in range(6):
                kb.mm(ps[1][:], wqn_b[:, k, :], cn[:, k, :], start=(k == 0), stop=(k == 5),
                      reads=['wqn', 'cn'], writes=['ps1'])
            kb.cp(QTn[:, sl], ps[1][:], ['ps1'], [('QTn', t)], eng='act')
            for k in range(6):
                kb.mm(ps[2][0:64, :], wqr_b[:, k, :], cn[:, k, :], start=(k == 0), stop=(k == 5),
                      reads=['wqr', 'cn'], writes=['ps2'])
            for k in range(6):
                kb.mm(ps[3][0:64, :], wqs_b[:, k, :], cn[:, k, :], start=(k == 0), stop=(k == 5),
                      reads=['wqs', 'cn'], writes=['ps3'])
            rope_combine(QTr[:, sl], ps[2][0:64, :], ps[3][0:64, :], bi, ['ps2', 'ps3'], ('QTr', t))
        for t in range(NTILE):
            bi = t % 2
            sl = slice(t * TP, (t + 1) * TP)
            kb.dma(src[bi][:, 0:4, :], ckvv[:, :, sl], (), [('src', bi)])
            norm_tile(kb, src[bi], 4, TP, kvnw_s, cn, ones, scr, ps[0], rstd, ('src', bi), 'cn', 512)
            for k in range(4):
                kb.mm(ps[1][:], wk_b[:, k, :], cn[:, k, :], start=(k == 0), stop=(k == 3),
                      reads=['wk', 'cn'], writes=['ps1'])
            kb.cp(KTn[:, sl], ps[1][:], ['ps1'], [('KTn', t)], eng='act')
            for blk in range(TP // 128):
                for k in range(4):
                    kb.mm(ps[2][:, blk * 128:(blk + 1) * 128], cn[:, k, blk * 128:(blk + 1) * 128], wv_b[:, k, :],
                          start=(k == 0), stop=(k == 3), reads=['wv', 'cn'], writes=['ps2'])
            kb.cp(V[:, t * 4:(t + 1) * 4, :], ps[2][:].rearrange("p (a b) -> p a b", b=128),
                  ['ps2'], [('V', t)])
        it = 0
        for I in range(S // 512):
            qsl = slice(I * 512, (I + 1) * 512)
            nj = 4 * I + 4
            for j in range(nj):
                ksl = slice(j * 128, (j + 1) * 128)
                sp_ = ps[4 + (it % 2)]
                spk = ('sps', it % 2)
                pb = it % 3
                it += 1
                kb.mm(sp_[:], KTn[:, ksl], QTn[:, qsl], start=True, stop=False,
                      reads=[('KTn', j // 4), ('QTn', I)], writes=[spk])
                kb.mm(sp_[:], KTr[:, ksl], QTr[:, qsl], start=False, stop=True,
                      reads=[('KTr', j // 4), ('QTr', I)], writes=[spk])
                kb.act(pT[pb][:], sp_[:], AF.Exp, [spk], [('pT', pb)], scale=MLA_SCALE)
                jj = j - 4 * I
                if jj >= 0:
                    kb.tt(pT[pb][:], pT[pb][:], mask_b[:, jj * 512:(jj + 1) * 512], ALU.mult,
                          [('pT', pb), 'mask'], [('pT', pb)])
                kb.mm(ps[6][:], V[:, j, :], pT[pb][:], start=(j == 0), stop=(j == nj - 1),
                      reads=[('V', j // 4), ('pT', pb)], writes=['ops'])
                kb.mm(ps[7][:], ones_b[:], pT[pb][:], start=(j == 0), stop=(j == nj - 1),
                      reads=['ones_b', ('pT', pb)], writes=['sums'])
            kb.op('dve', lambda e: e.reciprocal(rec[:], ps[7][:]), ['sums'], ['rec'])
            ob = osb[I % 2]
            kb.tt(ob[:], ps[6][:], rec[:], ALU.mult, ['ops', 'rec'], [('ob', I % 2)])
            kb.dma(oT[h * 128:(h + 1) * 128, qsl], ob[:], [('ob', I % 2)], ())


def causal_masks():
    k = np.arange(128)[:, None]
    q = np.arange(512)[None, :]
    return np.concatenate([(q >= k + 128 * jj).astype(np.float32) for jj in range(4)], axis=1)


CH = 64
NG = 4


def delta_psum(kb, nbanks=3):
    return [kb.ps([128, 512], F32) for _ in range(nbanks)], kb.ps([128, 1024], BF16)


class Delta:
    def __init__(self, kb, dk, dv, merged, ident, ident_key, identrep, psum=None):
        self.kb, self.dk, self.dv, self.merged = kb, dk, dv, merged
        self.ident, self.ident_key, self.identrep = ident, ident_key, identrep
        if psum is None:
            psum = delta_psum(kb)
        allb, self.bankT = psum
        self.banks = allb[:-1]
        self.ybank = allb[-1]
        self.bi = 0
        f3 = lambda a, b_, dt: kb.sb([a, NG, b_], dt)
        self.ApT = f3(CH, CH, F32)
        self.RpT_b = f3(CH, CH, BF16)
        self.AkT_b = f3(CH, CH, BF16)
        self.RkT_b = f3(CH, CH, BF16)
        self.P = [f3(CH, CH, F32) for _ in range(2)]
        self.Q = [f3(CH, CH, F32) for _ in range(2)]
        self.Y = f3(CH, CH, F32)
        self.Yb = f3(CH, CH, BF16)
        self.Wq_b = f3(CH, dk, BF16)
        self.AV_b = f3(CH, dv, BF16)
        self.Ul_b = f3(CH, dv, BF16)
        self.RtT_b = f3(dk, CH, BF16)
        self.N = f3(dk, dv, F32)
        self.MT = f3(dk, dk, F32)
        self.uid = 0

    def nb(self):
        i = self.bi
        self.bi = (i + 1) % len(self.banks)
        return self.banks[i], ('dbank', id(self.banks[i]))

    def new_state(self):
        kb = self.kb
        H = kb.sb([self.dk, self.dv], F32)
        Hb = kb.sb([self.dk, self.dv], BF16)
        self.uid += 1
        key = ('H', id(self), self.uid)
        kb.memset(H[:], 0.0, [key])
        kb.memset(Hb[:], 0.0, [(key, 'b')])
        return (H, Hb, key)

    def group(self, st, PTs, KTs, QR, RbT, Qb, Ph, Kh, V, FA, FR, Gam, yT_out, rkeys, ykey):
        kb, dk, dv, mg = self.kb, self.dk, self.dv, self.merged
        H, Hb, hkey = st
        me = id(self)
        K_ = lambda n: (n, me)
        rk = list(rkeys)
        v3 = lambda bank, rows, w: bank[0:rows, 0:NG * w].rearrange("p (g w) -> p g w", w=w)
        b0, k0 = self.nb()
        for c in range(NG):
            kb.mm(b0[0:CH, c * 128:(c + 1) * 128], PTs[:, c * CH:(c + 1) * CH], QR[:, c, :], True, True, rk, [k0])
        s0 = v3(b0, CH, 128)
        kb.tt(self.ApT[:], s0[:, :, 0:CH], FA, ALU.mult, [k0] + rk, [K_('ApT')])
        kb.tt(self.RpT_b[:], s0[:, :, CH:128], FR, ALU.mult, [k0] + rk, [K_('RpT')])
        if not mg:
            b1, k1 = self.nb()
            for c in range(NG):
                kb.mm(b1[0:CH, c * 128:(c + 1) * 128], KTs[:, c * CH:(c + 1) * CH], QR[:, c, :], True, True, rk, [k1])
            s1 = v3(b1, CH, 128)
            kb.tt(self.AkT_b[:], s1[:, :, 0:CH], FA, ALU.mult, [k1] + rk, [K_('AkT')])
            kb.tt(self.RkT_b[:], s1[:, :, CH:128], FR, ALU.mult, [k1] + rk, [K_('RkT')])
        b2, k2 = self.nb()
        for c in range(NG):
            kb.tr(b2[0:CH, c * CH:(c + 1) * CH], self.ApT[:, c, :], self.ident[0:CH, 0:CH],
                  [K_('ApT'), self.ident_key], [k2])
        kb.cp(self.P[0][:], v3(b2, CH, CH), [k2], [K_('P0')], eng='act')
        kb.tt(self.Y[:], self.identrep, self.ApT[:], ALU.subtract, [K_('ApT'), self.ident_key], [K_('Y')])
        Pc, Pk = self.P[0], K_('P0')
        Qc, Qk = self.ApT, K_('ApT')
        for lev in range(1, 6):
            Pn, Pnk = self.P[lev % 2], K_('P%d' % (lev % 2))
            Qn, Qnk = self.Q[lev % 2], K_('Q%d' % (lev % 2))
            bp, kp = self.nb()
            for c in range(NG):
                kb.mm(bp[0:CH, c * CH:(c + 1) * CH], Qc[:, c, :], Pc[:, c, :], True, True, [Qk, Pk], [kp])
            if lev < 5:
                bq, kq = self.nb()
                for c in range(NG):
                    kb.mm(bq[0:CH, c * CH:(c + 1) * CH], Pc[:, c, :], Qc[:, c, :], True, True, [Qk, Pk], [kq])
            kb.cp(Pn[:], v3(bp, CH, CH), [kp], [Pnk], eng='act')
            if lev < 5:
                kb.cp(Qn[:], v3(bq, CH, CH), [kq], [Qnk], eng='dve')
            by, ky = self.nb()
            for c in range(NG):
                kb.mm(by[0:CH, c * CH:(c + 1) * CH], Pn[:, c, :], self.Y[:, c, :], True, True, [Pnk, K_('Y')], [ky])
            kb.tt(self.Y[:], self.Y[:], v3(by, CH, CH), ALU.add, [ky, K_('Y')], [K_('Y')])
            Pc, Pk, Qc, Qk = Pn, Pnk, Qn, Qnk
        kb.cp(self.Yb[:], self.Y[:], [K_('Y')], [K_('Yb')], eng='act')
        bw, kw = self.nb()
        for c in range(NG):
            kb.mm(bw[0:CH, c * dk:(c + 1) * dk], self.Yb[:, c, :], Qb[:, c, :], True, True, [K_('Yb')] + rk, [kw])
        kb.cp(self.Wq_b[:], v3(bw, CH, dk), [kw], [K_('Wq')], eng='act')
        if mg:
            bu, ku = self.nb()
            for c in range(NG):
                kb.mm(bu[0:CH, c * dv:(c + 1) * dv], self.Yb[:, c, :], V[:, c, :], True, True, [K_('Yb')] + rk, [ku])
            kb.cp(self.Ul_b[:], v3(bu, CH, dv), [ku], [K_('Ul')], eng='dve')
        else:
            ba, ka = self.nb()
            for c in range(NG):
                kb.mm(ba[0:CH, c * dv:(c + 1) * dv], self.AkT_b[:, c, :], V[:, c, :], True, True, [K_('AkT')] + rk, [ka])
            kb.cp(self.AV_b[:], v3(ba, CH, dv), [ka], [K_('AV')], eng='dve')
            bu, ku = self.nb()
            for c in range(NG):
                kb.mm(bu[0:CH, c * dv:(c + 1) * dv], self.Yb[:, c, :], self.AV_b[:, c, :], True, True,
                      [K_('Yb'), K_('AV')], [ku])
            kb.ts(self.Ul_b[:], v3(bu, CH, dv), -1.0, None, ALU.mult, reads=[ku], writes=[K_('Ul')])
        br, kr_ = self.nb()
        for c in range(NG):
            kb.mm(br[0:dk, c * CH:(c + 1) * CH], self.Wq_b[:, c, :], self.RpT_b[:, c, :], True, True,
                  [K_('Wq'), K_('RpT')], [kr_])
        kb.tt(self.RtT_b[:], RbT, v3(br, dk, CH), ALU.subtract, [kr_] + rk, [K_('RtT')])
        bn, kn = self.nb()
        for c in range(NG):
            kb.mm(bn[0:dk, c * dv:(c + 1) * dv], Ph[:, c, :], self.Ul_b[:, c, :], True, mg, [K_('Ul')] + rk, [kn])
            if not mg:
                kb.mm(bn[0:dk, c * dv:(c + 1) * dv], Kh[:, c, :], V[:, c, :], False, True, rk, [kn])
        kb.cp(self.N[:], v3(bn, dk, dv), [kn], [K_('N')], eng='act')
        bm, km = self.nb()
        for c in range(NG):
            kb.mm(bm[0:dk, c * dk:(c + 1) * dk], self.Wq_b[:, c, :], Ph[:, c, :], True, True, [K_('Wq')] + rk, [km])
        for c in range(NG):
            kb.stt(self.MT[:, c, :], self.ident[0:dk, 0:dk], Gam[:, c:c + 1], bm[0:dk, c * dk:(c + 1) * dk],
                   ALU.mult, ALU.subtract, [km, self.ident_key] + rk, [K_('MT')])
        byy, kyy = self.ybank, ('dbank', id(self.ybank))
        for c in range(NG):
            o_ = byy[0:dv, c * CH:(c + 1) * CH]
            kb.mm(o_, self.Ul_b[:, c, :], self.RpT_b[:, c, :], True, False, [K_('Ul'), K_('RpT')], [kyy])
            if not mg:
                kb.mm(o_, V[:, c, :], self.RkT_b[:, c, :], False, False, [K_('RkT')] + rk, [kyy])
            kb.mm(o_, Hb[:], self.RtT_b[:, c, :], False, True, [(hkey, 'b'), K_('RtT')], [kyy])
            bh, kh = self.nb()
            kb.mm(bh[0:dk, 0:dv], self.MT[:, c, :], H[:], True, True, [K_('MT'), hkey], [kh])
            kb.tt(H[:], bh[0:dk, 0:dv], self.N[:, c, :], ALU.add, [kh, K_('N')], [hkey])
            kb.cp(Hb[:], H[:], [hkey], [(hkey, 'b')], eng='act')
        kb.cp(yT_out, byy[0:dv, 0:NG * CH], [kyy], [ykey], eng='dve')


def delta_consts():
    s = np.arange(64)[:, None]
    t = np.arange(64)[None, :]
    c = {}
    c['ident'] = np.eye(128, dtype=np.float32)
    c['identrep'] = np.ascontiguousarray(np.tile(np.eye(64, dtype=np.float32)[:, None, :], (1, NG, 1)).reshape(64, NG * 64))
    c['maskA'] = np.ascontiguousarray(np.tile((s < t).astype(np.float32)[:, None, :], (1, NG, 1)).reshape(64, NG * 64))
    c['maskR'] = np.ascontiguousarray(np.tile((s <= t).astype(np.float32)[:, None, :], (1, NG, 1)).reshape(64, NG * 64))
    rm = np.ones((128, NG * 64), np.float32)
    rm[:, ::64] = 0.0
    c['rmask'] = rm
    sel = np.zeros((64, 64), np.float32)
    sel[63, :] = 1.0
    c['sel63'] = sel
    return c


DCONST_SHAPES = (('ident', [128, 128]), ('identrep', [64, NG * CH]), ('maskA', [64, NG * CH]),
                 ('maskR', [64, NG * CH]), ('rmask', [128, NG * CH]), ('sel63', [64, 64]))


def delta_const_inputs(kb):
    return {nm: kb.dram(nm, shp, F32, "ExternalInput") for nm, shp in DCONST_SHAPES}


class DeltaConsts:
    def __init__(self, kb, drams):
        d = {}
        for nm, shp in DCONST_SHAPES:
            t = kb.sb(shp, F32)
            kb.dma(t[:], drams[nm], (), ['dconst'])
            d[nm] = t
        self.ident = d['ident']
        self.identrep3 = d['identrep'][:].rearrange("p (g w) -> p g w", w=CH)
        self.maskA3 = d['maskA'][:].rearrange("p (g w) -> p g w", w=CH)
        self.maskR3 = d['maskR'][:].rearrange("p (g w) -> p g w", w=CH)
        self.rmask = d['rmask']
        self.sel63 = d['sel63']
        self.maskR = d['maskR']
        self.ident_b = kb.sb([128, 128], BF16)
        kb.cp(self.ident_b[:], self.ident[:], ['dconst'], ['dconst_b'])
        self.ones = kb.sb([128, 128], F32)
        kb.memset(self.ones[:], 1.0, ['dones'])


def transpose_group(kb, dl, dst3, src, rows, reads, wkey, scale_cols=None):
    bt = dl.bankT
    for c in range(NG):
        kb.tr(bt[0:CH, c * rows:(c + 1) * rows], src[:, c * CH:(c + 1) * CH], dl.cst.ident_b[0:rows, 0:rows],
              list(reads) + ['dconst_b'], ['bankT'])
    if scale_cols is None:
        kb.cp(dst3, bt[0:CH, 0:NG * rows].rearrange("p (g w) -> p g w", w=rows), ['bankT'], [wkey], eng='act')
    else:
        for c in range(NG):
            kb.ts(dst3[:, c, :], bt[0:CH, c * rows:(c + 1) * rows], scale_cols[:, c:c + 1], None, ALU.mult,
                  reads=['bankT'] + list(reads), writes=[wkey])


RW_DEC = float(np.exp(-0.5))


def rwkv_inputs(kb, sfx=''):
    d = {}
    d['cols'] = kb.dram("rw_cols" + sfx, [64, 40], F32, "ExternalInput")
    d['mulow'] = kb.dram("rw_mulow" + sfx, [128, 3], F32, "ExternalInput")
    d['w_up'] = kb.dram("rw_w_up" + sfx, [64, 256], F32, "ExternalInput")
    d['a_up'] = kb.dram("rw_a_up" + sfx, [64, 256], F32, "ExternalInput")
    d['g_up'] = kb.dram("rw_g_up" + sfx, [128, 256], F32, "ExternalInput")
    return d


def emit_rwkv(kb, rkvT, lowT, d, ocT, cst, heads=(0, 1, 2, 3), psum=None):
    GW = NG * CH
    NGRP = S // GW
    colsd, mulow, wupd, aupd, gupd = d['cols'], d['mulow'], d['w_up'], d['a_up'], d['g_up']
    dl = Delta(kb, 64, 64, False, cst.ident, 'dconst', cst.identrep3, psum)
    dl.cst = cst
    cols = kb.sb([64, 40], F32)
    mul = kb.sb([128, 3], F32)
    kb.dma(cols[:], colsd, (), ['cols'])
    kb.dma(mul[:], mulow, (), ['cols'])
    wst = kb.sb([128, 256], F32)
    wup_b = kb.sb([64, 256], BF16)
    aup_b = kb.sb([64, 256], BF16)
    gup_b = kb.sb([128, 256], BF16)
    for (dr, dst, rows) in ((wupd, wup_b, 64), (aupd, aup_b, 64), (gupd, gup_b, 128)):
        kb.dma(wst[0:rows, :], dr, (), ['wst'])
        kb.cp(dst[:], wst[0:rows, :], ['wst'], ['wlow'])
    col = lambda j, h: cols[:, 12 + j * 4 + h: 12 + j * 4 + h + 1]

    f = lambda r, w=GW, dt=F32: kb.sb([r, w], dt)
    lw_x = f(64, GW + 1); la_x = f(64, GW + 1); lg_x = f(128, GW + 1)
    tanh_b = f(64, GW, BF16); xa_b = f(64, GW, BF16); sg_b = f(128, GW, BF16)
    tmp = f(128); tmp2 = f(64)
    X3 = [kb.sb([64, GW + 1], F32) for _ in range(3)]
    r_s = f(64); k_s = f(64); v_s = f(64)
    lgw = f(64); a_s = f(64); g_s = f(64); kk = f(64); kp = f(64); bb = f(64)
    cum = f(64); ex = f(64); cumC = kb.sb([64, NG], F32); gam = kb.sb([64, NG], F32)
    PTs = f(64, GW, BF16); KTs = f(64, GW, BF16); QR = kb.sb([64, NG, 128], BF16)
    PhT = f(64, GW, BF16); KhT = f(64, GW, BF16); qbT = f(64, GW, BF16); vT_b = f(64, GW, BF16)
    Qb = kb.sb([64, NG, 64], BF16); Ph = kb.sb([64, NG, 64], BF16); Kh = kb.sb([64, NG, 64], BF16)
    Vt = kb.sb([64, NG, 64], BF16)
    yT = f(64); yc = f(64); sq = f(64); rs = f(64); outb = [f(64) for _ in range(2)]
    states = {h: dl.new_state() for h in heads}
    v3 = lambda t: t[:].rearrange("p (g w) -> p g w", w=CH)

    def shift(dst, X, mucol, rows, rkey, wkey):
        kb.tt(tmp[0:rows, :], X[0:rows, 0:GW], X[0:rows, 1:GW + 1], ALU.subtract, [rkey], ['tmp'])
        kb.stt(dst, tmp[0:rows, :], mucol, X[0:rows, 1:GW + 1], ALU.mult, ALU.add, ['tmp', rkey, 'cols'], [wkey])

    def rsq(out, ps_ap, eps, reads, wkey, scale=1.0):
        kb.ts(tmp2[:], ps_ap, scale, eps, ALU.mult, ALU.add, reads=reads, writes=['tmp2'])
        kb.act(tmp2[:], tmp2[:], AF.Ln, ['tmp2'], ['tmp2'])
        kb.act(out, tmp2[:], AF.Exp, ['tmp2'], [wkey], scale=-0.5)

    for gi in range(NGRP):
        c0 = gi * GW
        kb.dma(lw_x[:], lowT[0:64, c0:c0 + GW + 1], (), ['lw_x'])
        kb.dma(la_x[:], lowT[64:128, c0:c0 + GW + 1], (), ['la_x'])
        kb.dma(lg_x[:], lowT[128:256, c0:c0 + GW + 1], (), ['lg_x'])
        shift(tmp2[:], lw_x, mul[0:64, 0:1], 64, 'lw_x', 'tmp2')
        kb.act(tanh_b[:], tmp2[:], AF.Tanh, ['tmp2'], ['tanh_b'])
        shift(xa_b[:], la_x, mul[0:64, 1:2], 64, 'la_x', 'xa_b')
        shift(tmp[:], lg_x, mul[:, 2:3], 128, 'lg_x', 'tmp')
        kb.act(sg_b[:], tmp[:], AF.Sigmoid, ['tmp'], ['sg_b'])
        for h in heads:
            hs = slice(h * 64, (h + 1) * 64)
            for part in range(3):
                r0 = (part * 4 + h) * 64
                kb.dma(X3[part][:], rkvT[r0:r0 + 64, c0:c0 + GW + 1], (), [('X3', part)])
            for part, dst, nm in ((0, r_s, 'r_s'), (1, k_s, 'k_s'), (2, v_s, 'v_s')):
                shift(dst[:], X3[part], cols[:, part * 4 + h: part * 4 + h + 1], 64, ('X3', part), nm)
            psA, kA = dl.nb()
            kb.mm(psA[0:64, 0:GW], wup_b[:, hs], tanh_b[:], True, True, ['wlow', 'tanh_b'], [kA])
            kb.act(lgw[:], psA[0:64, 0:GW], AF.Sigmoid, [kA, 'cols'], ['lgw'], bias=col(0, h))
            kb.ts(lgw[:], lgw[:], -RW_DEC, None, ALU.mult, reads=['lgw'], writes=['lgw'])
            psA, kA = dl.nb()
            kb.mm(psA[0:64, 0:GW], aup_b[:, hs], xa_b[:], True, True, ['wlow', 'xa_b'], [kA])
            kb.act(a_s[:], psA[0:64, 0:GW], AF.Sigmoid, [kA, 'cols'], ['a_s'], bias=col(1, h))
            psA, kA = dl.nb()
            kb.mm(psA[0:64, 0:GW], gup_b[:, hs], sg_b[:], True, True, ['wlow', 'sg_b'], [kA])
            kb.cp(g_s[:], psA[0:64, 0:GW], [kA], ['g_s'], eng='act')
            kb.ts(kk[:], k_s[:], col(2, h), None, ALU.mult, reads=['k_s', 'cols'], writes=['kk'])
            kb.tt(sq[:], kk[:], kk[:], ALU.mult, ['kk'], ['sq'])
            psA, kA = dl.nb()
            kb.mm(psA[0:64, 0:GW], cst.ones[0:64, 0:64], sq[:], True, True, ['dones', 'sq'], [kA])
            rsq(rs[:], psA[0:64, 0:GW], 1e-6, [kA], 'rs')
            kb.tt(kk[:], kk[:], rs[:], ALU.mult, ['kk', 'rs'], ['kk'])
            kb.ts(kp[:], a_s[:], -1.0, col(3, h), ALU.add, ALU.mult, reads=['a_s', 'cols'], writes=['kp'])
            kb.stt(kp[:], kp[:], 1.0, k_s[:], ALU.add, ALU.mult, ['kp', 'k_s'], ['kp'])
            kb.tt(bb[:], kk[:], a_s[:], ALU.mult, ['kk', 'a_s'], ['bb'])
            kb.op('dve', lambda e: e.tensor_tensor_scan(cum[:], cst.rmask[0:64, :], lgw[:], 0.0, ALU.mult, ALU.add),
                  ['lgw', 'dconst'], ['cum'])
            kb.cp(cumC[:], v3(cum)[:, :, CH - 1], ['cum'], ['cumC'])
            kb.act(gam[:], cumC[:], AF.Exp, ['cumC'], ['gam'])
            kb.act(ex[:], cum[:], AF.Exp, ['cum'], ['ex'])
            kb.tt(QR[:, :, CH:128], v3(r_s), v3(ex), ALU.mult, ['r_s', 'ex'], ['QR'])
            kb.tt(tmp2[:], cum[:], lgw[:], ALU.subtract, ['cum', 'lgw'], ['tmp2'])
            kb.act(ex[:], tmp2[:], AF.Exp, ['tmp2'], ['ex'])
            kb.tt(qbT[:], kk[:], ex[:], ALU.mult, ['kk', 'ex'], ['qbT'])
            kb.cp(QR[:, :, 0:CH], v3(qbT), ['qbT'], ['QR'])
            kb.act(ex[:], cum[:], AF.Exp, ['cum'], ['ex'], scale=-1.0)
            kb.tt(PTs[:], bb[:], ex[:], ALU.mult, ['bb', 'ex'], ['PTs'])
            kb.tt(KTs[:], kp[:], ex[:], ALU.mult, ['kp', 'ex'], ['KTs'])
            for c in range(NG):
                kb.ts(tmp2[:, c * CH:(c + 1) * CH], cum[:, c * CH:(c + 1) * CH], cumC[:, c:c + 1], None, ALU.subtract,
                      reads=['cum', 'cumC'], writes=['tmp2'])
            kb.act(ex[:], tmp2[:], AF.Exp, ['tmp2'], ['ex'], scale=-1.0)
            kb.tt(PhT[:], bb[:], ex[:], ALU.mult, ['bb', 'ex'], ['PhT'])
            kb.tt(KhT[:], kp[:], ex[:], ALU.mult, ['kp', 'ex'], ['KhT'])
            kb.cp(vT_b[:], v_s[:], ['v_s'], ['vT_b'])
            transpose_group(kb, dl, Qb[:], qbT, 64, ['qbT'], 'Qb')
            transpose_group(kb, dl, Ph[:], PhT, 64, ['PhT'], 'Ph')
            transpose_group(kb, dl, Kh[:], KhT, 64, ['KhT'], 'Kh')
            transpose_group(kb, dl, Vt[:], vT_b, 64, ['vT_b'], 'Vt')
            dl.group(states[h], PTs[:], KTs[:], QR, QR[:, :, CH:128], Qb, Ph, Kh, Vt, cst.maskA3, cst.maskR3, gam,
                     yT[:], ['PTs', 'KTs', 'QR', 'Qb', 'Ph', 'Kh', 'Vt', 'gam', 'dconst'], 'yT')
            psA, kA = dl.nb()
            kb.mm(psA[0:64, 0:GW], cst.ones[0:64, 0:64], yT[:], True, True, ['dones', 'yT'], [kA])
            kb.stt(yc[:], psA[0:64, 0:GW], -1.0 / 64, yT[:], ALU.mult, ALU.add, [kA, 'yT'], ['yc'])
            kb.tt(sq[:], yc[:], yc[:], ALU.mult, ['yc'], ['sq'])
            psA, kA = dl.nb()
            kb.mm(psA[0:64, 0:GW], cst.ones[0:64, 0:64], sq[:], True, True, ['dones', 'sq'], [kA])
            rsq(rs[:], psA[0:64, 0:GW], 64e-5, [kA], 'rs', scale=1.0 / 64)
            kb.tt(yc[:], yc[:], rs[:], ALU.mult, ['yc', 'rs'], ['yc'])
            kb.ts(yc[:], yc[:], col(5, h), col(6, h), ALU.mult, ALU.add, reads=['yc', 'cols'], writes=['yc'])
            kb.stt(sq[:], r_s[:], col(4, h), kp[:], ALU.mult, ALU.mult, ['r_s', 'kp', 'cols'], ['sq'])
            psA, kA = dl.nb()
            kb.mm(psA[0:64, 0:GW], cst.ones[0:64, 0:64], sq[:], True, True, ['dones', 'sq'], [kA])
            kb.tt(sq[:], psA[0:64, 0:GW], v_s[:], ALU.mult, [kA, 'v_s'], ['sq'])
            kb.tt(yc[:], yc[:], sq[:], ALU.add, ['yc', 'sq'], ['yc'])
            ob = outb[(gi * 4 + h) % 2]
            okey = ('outb', (gi * 4 + h) % 2)
            kb.tt(ob[:], yc[:], g_s[:], ALU.mult, ['yc', 'g_s'], [okey])
            kb.dma(ocT[h * 64:(h + 1) * 64, c0:c0 + GW], ob[:], [okey], ())


def gdn_inputs(kb, sfx=''):
    d = {}
    d['convw'] = kb.dram("gd_convw" + sfx, [128, 24], F32, "ExternalInput")
    d['hcols'] = kb.dram("gd_hcols" + sfx, [128, 5], F32, "ExternalInput")
    return d


def emit_gdn(kb, qkvT, zT, baT, d, obT, cst, psum=None):
    GW = NG * CH
    NGRP = S // GW
    NCHK = S // CH
    convw, hcolsd = d['convw'], d['hcols']
    dl = Delta(kb, 128, 128, True, cst.ident, 'dconst', cst.identrep3, psum)
    dl.cst = cst
    cw = kb.sb([128, 24], F32)
    hc = kb.sb([128, 5], F32)
    ab = kb.sb([64, 4 * NCHK], F32)
    kb.dma(cw[:], convw, (), ['cols'])
    kb.dma(hc[:], hcolsd, (), ['cols'])
    for hl in range(2):
        for which, row in ((0, 2 + hl), (1, hl)):
            j = 2 * hl + which
            kb.dma(ab[:, j * NCHK:(j + 1) * NCHK], baT[row, :].rearrange("(c i) -> i c", i=CH), (), ['ab'], slow=True)
    negA = kb.sb([128, 2], F32)
    for hl in range(2):
        kb.act(negA[:, hl:hl + 1], hc[:, 2 * hl:2 * hl + 1], AF.Exp, ['cols'], ['negA'])
    kb.ts(negA[:], negA[:], -1.0, None, ALU.mult, reads=['negA'], writes=['negA'])

    f = lambda r, w=GW, dt=F32: kb.sb([r, w], dt)
    Gtm = [f(64, NCHK) for _ in range(2)]
    eGtm = [f(64, NCHK) for _ in range(2)]
    coefP = [f(64, NCHK) for _ in range(2)]
    beta = [f(64, NCHK) for _ in range(2)]
    ttm = f(64, NCHK)
    for hl in range(2):
        a_tm = ab[:, (2 * hl) * NCHK:(2 * hl + 1) * NCHK]
        b_tm = ab[:, (2 * hl + 1) * NCHK:(2 * hl + 2) * NCHK]
        kb.act(ttm[:], a_tm, AF.Exp, ['ab', 'cols'], ['ttm'], bias=hc[0:64, 2 * hl + 1:2 * hl + 2])
        kb.act(ttm[:], ttm[:], AF.Ln, ['ttm'], ['ttm'], bias=1.0)
        kb.ts(ttm[:], ttm[:], negA[0:64, hl:hl + 1], None, ALU.mult, reads=['ttm', 'negA'], writes=['ttm'])
        psA, kA = dl.nb()
        kb.mm(psA[0:64, 0:NCHK], cst.maskR[:, 0:64], ttm[:], True, True, ['dconst', 'ttm'], [kA])
        kb.cp(Gtm[hl][:], psA[0:64, 0:NCHK], [kA], [('Gtm', hl)])
        kb.act(eGtm[hl][:], Gtm[hl][:], AF.Exp, [('Gtm', hl)], [('eGtm', hl)])
        psA, kA = dl.nb()
        kb.mm(psA[0:64, 0:NCHK], cst.sel63[:], Gtm[hl][:], True, True, ['dconst', ('Gtm', hl)], [kA])
        kb.tt(ttm[:], psA[0:64, 0:NCHK], Gtm[hl][:], ALU.subtract, [kA, ('Gtm', hl)], ['ttm'])
        kb.act(coefP[hl][:], ttm[:], AF.Exp, ['ttm'], [('coefP', hl)])
        kb.act(beta[hl][:], b_tm, AF.Sigmoid, ['ab'], [('beta', hl)])
        kb.tt(coefP[hl][:], coefP[hl][:], beta[hl][:], ALU.mult, [('coefP', hl), ('beta', hl)], [('coefP', hl)])

    X3 = [kb.sb([128, GW + 3], F32) for _ in range(3)]
    cv = [f(128) for _ in range(3)]
    sq = f(128); rs = f(128); tmp = f(128)
    abt = f(128); Gbc = f(128); eG = f(128)
    kn_b = f(128, GW, BF16); vc_b = f(128, GW, BF16)
    QR = kb.sb([128, NG, 128], BF16)
    RbT = kb.sb([128, NG, CH], BF16)
    gam = kb.sb([128, NG], F32)
    E = kb.sb([64, NG, CH], F32); FA = kb.sb([64, NG, CH], F32); FR = kb.sb([64, NG, CH], F32)
    Qb = kb.sb([64, NG, 128], BF16); Ph = kb.sb([64, NG, 128], BF16); Vt = kb.sb([64, NG, 128], BF16)
    yT = [f(128) for _ in range(2)]
    states = [dl.new_state() for _ in range(2)]
    v3 = lambda t: t[:].rearrange("p (g w) -> p g w", w=CH)

    def rsq(out, ps_ap, eps, reads, wkey):
        kb.ts(tmp[:], ps_ap, 1.0, eps, ALU.mult, ALU.add, reads=reads, writes=['tmp'])
        kb.act(tmp[:], tmp[:], AF.Ln, ['tmp'], ['tmp'])
        kb.act(out, tmp[:], AF.Exp, ['tmp'], [wkey], scale=-0.5)

    for gi in range(NGRP):
        c0 = gi * GW
        for hl in range(2):
            for part in range(3):
                r0 = (part * 2 + hl) * 128
                kb.dma(X3[part][:], qkvT[r0:r0 + 128, c0:c0 + GW + 3], (), [('X3', part)])
                wc = lambda j: cw[:, (part * 2 + hl) * 4 + j:(part * 2 + hl) * 4 + j + 1]
                kb.ts(cv[part][:], X3[part][:, 3:GW + 3], wc(3), None, ALU.mult, reads=[('X3', part), 'cols'],
                      writes=[('cv', part)])
                for j in (2, 1, 0):
                    kb.stt(cv[part][:], X3[part][:, j:GW + j], wc(j), cv[part][:], ALU.mult, ALU.add,
                           [('X3', part), 'cols', ('cv', part)], [('cv', part)])
                kb.act(cv[part][:], cv[part][:], AF.Silu, [('cv', part)], [('cv', part)])
            for part, dst3, scl in ((1, QR[:, :, 0:CH], 1.0), (0, QR[:, :, CH:128], float(128 ** -0.5))):
                kb.tt(sq[:], cv[part][:], cv[part][:], ALU.mult, [('cv', part)], ['sq'])
                psA, kA = dl.nb()
                kb.mm(psA[:, 0:GW], cst.ones[:], sq[:], True, True, ['dones', 'sq'], [kA])
                rsq(rs[:], psA[:, 0:GW], 1e-6, [kA], 'rs')
                kb.stt(cv[part][:], cv[part][:], scl, rs[:], ALU.mult, ALU.mult, [('cv', part), 'rs'], [('cv', part)])
                kb.cp(dst3, v3(cv[part]), [('cv', part)], ['QR'])
            kb.cp(kn_b[:], cv[1][:], [('cv', 1)], ['kn_b'])
            kb.cp(vc_b[:], cv[2][:], [('cv', 2)], ['vc_b'], eng='act')
            kb.dma(abt[:], baT[2 + hl, c0:c0 + GW].partition_broadcast(128), (), ['abt'])
            kb.act(abt[:], abt[:], AF.Exp, ['abt', 'cols'], ['abt'], bias=hc[:, 2 * hl + 1:2 * hl + 2])
            kb.act(abt[:], abt[:], AF.Ln, ['abt'], ['abt'], bias=1.0)
            kb.ts(abt[:], abt[:], negA[:, hl:hl + 1], None, ALU.mult, reads=['abt', 'negA'], writes=['abt'])
            kb.op('dve', lambda e: e.tensor_tensor_scan(Gbc[:], cst.rmask[:], abt[:], 0.0, ALU.mult, ALU.add),
                  ['abt', 'dconst'], ['Gbc'])
            kb.act(eG[:], Gbc[:], AF.Exp, ['Gbc'], ['eG'])
            kb.cp(gam[:], v3(eG)[:, :, CH - 1], ['eG'], ['gam'])
            kb.tt(RbT[:], v3(cv[0]), v3(eG), ALU.mult, [('cv', 0), 'eG'], ['RbT'])
            for c in range(NG):
                ci = gi * NG + c
                kb.ts(E[:, c, :], Gbc[0:64, c * CH:(c + 1) * CH], Gtm[hl][:, ci:ci + 1], 0.0, ALU.subtract, ALU.min,
                      reads=['Gbc', ('Gtm', hl)], writes=['E'])
            kb.act(E[:], E[:], AF.Exp, ['E'], ['E'])
            for c in range(NG):
                ci = gi * NG + c
                kb.stt(FA[:, c, :], E[:, c, :], beta[hl][:, ci:ci + 1], cst.maskA3[:, c, :], ALU.mult, ALU.mult,
                       ['E', ('beta', hl), 'dconst'], ['FA'])
                kb.stt(FR[:, c, :], E[:, c, :], beta[hl][:, ci:ci + 1], cst.maskR3[:, c, :], ALU.mult, ALU.mult,
                       ['E', ('beta', hl), 'dconst'], ['FR'])
            transpose_group(kb, dl, Qb[:], kn_b, 128, ['kn_b', ('eGtm', hl)], 'Qb',
                            scale_cols=eGtm[hl][:, gi * NG:(gi + 1) * NG])
            transpose_group(kb, dl, Ph[:], kn_b, 128, ['kn_b', ('coefP', hl)], 'Ph',
                            scale_cols=coefP[hl][:, gi * NG:(gi + 1) * NG])
            transpose_group(kb, dl, Vt[:], vc_b, 128, ['vc_b'], 'Vt')
            yt = yT[(gi * 2 + hl) % 2]
            ykey = ('yT', (gi * 2 + hl) % 2)
            dl.group(states[hl], kn_b[:], None, QR, RbT[:], Qb, Ph, None, Vt, FA[:], FR[:], gam,
                     yt[:], ['kn_b', 'QR', 'RbT', 'Qb', 'Ph', 'Vt', 'gam', 'FA', 'FR', 'dconst'], ykey)
            kb.dma(abt[:], zT[hl * 128:(hl + 1) * 128, c0:c0 + GW], (), ['abt'])
            kb.act(abt[:], abt[:], AF.Silu, ['abt'], ['abt'])
            kb.tt(sq[:], yt[:], yt[:], ALU.mult, [ykey], ['sq'])
            psA, kA = dl.nb()
            kb.mm(psA[:, 0:GW], cst.ones[:], sq[:], True, True, ['dones', 'sq'], [kA])
            kb.ts(tmp[:], psA[:, 0:GW], 1.0 / 128, EPS, ALU.mult, ALU.add, reads=[kA], writes=['tmp'])
            kb.act(tmp[:], tmp[:], AF.Ln, ['tmp'], ['tmp'])
            kb.act(rs[:], tmp[:], AF.Exp, ['tmp'], ['rs'], scale=-0.5)
            kb.stt(yt[:], yt[:], hc[:, 4:5], rs[:], ALU.mult, ALU.mult, [ykey, 'rs', 'cols'], [ykey])
            kb.tt(yt[:], yt[:], abt[:], ALU.mult, [ykey, 'abt'], [ykey])
            kb.dma(obT[hl * 128:(hl + 1) * 128, c0:c0 + GW], yt[:], [ykey], ())


def build_stageC(final):
    kb = KB()
    xT = kb.dram("xT", [D, TOK], F32, "ExternalInput")
    oT = [kb.dram(n, [1024, TOK], F32, "ExternalInput") for n in ("oaT", "obT", "ocT")]
    colsd = kb.dram("cols", [128, 9 * KC], F32, "ExternalInput")
    w_g = kb.dram("w_g", [D, 3 * D], F32, "ExternalInput")
    w_br = kb.dram("w_branch", [3072, D], F32, "ExternalInput")
    w_out = kb.dram("w_out", [D, D], F32, "ExternalInput")
    w_gu = kb.dram("w_gate_up", [D, 2 * DFF], F32, "ExternalInput")
    w_dn = kb.dram("w_down", [DFF, D], F32, "ExternalInput")
    xo = kb.dram("xo", [D, TOK], F32, "ExternalOutput")
    cols = kb.sb([128, 9 * KC], F32)
    kb.dma(cols[:], colsd, (), ['cols'])
    c_ = lambda j: cols[:, j * KC:(j + 1) * KC]
    ca = dict(gt1=c_(0), nw2=c_(1), sc2=c_(2), sh2=c_(3), gt2=c_(4), fnw=c_(5), nw1=c_(6), sc1=c_(7), sh1=c_(8))
    emit_stageC(kb, TOK, xT, oT, ca, ['cols'], w_g, w_br, w_out, w_gu, w_dn, xo, final)
    return kb.finish()


def emit_stageC(kb, ntok, xT, oT, ca, cakeys, w_g, w_br, w_out, w_gu, w_dn, xo, final):
    T = TT
    NT = ntok // T
    cakeys = list(cakeys)
    ones = kb.sb([128, 128], F32)
    kb.memset(ones[:], 1.0, ['ones'])
    acol1 = kb.sb([128, KC], F32)
    kb.ts(acol1[:], ca['sc1'], 1.0, None, ALU.add, reads=cakeys, writes=['acol'])
    kb.tt(acol1[:], acol1[:], ca['nw1'], ALU.mult, ['acol'] + cakeys, ['acol'])
    bcol1 = ca['sh1']
    gt1 = ca['gt1']
    acol = kb.sb([128, KC], F32)
    kb.ts(acol[:], ca['sc2'], 1.0, None, ALU.add, reads=cakeys, writes=['acol'])
    kb.tt(acol[:], acol[:], ca['nw2'], ALU.mult, ['acol'] + cakeys, ['acol'])
    bcol = ca['sh2']
    gt2 = ca['gt2']
    fnw = ca['fnw']

    x = kb.sb([128, KC, T], F32)
    merged = kb.sb([128, KC, T], F32)
    mb = kb.sb([128, KC, T], BF16)
    stg = kb.sb([128, 8, T], F32)
    oi_b = kb.sb([128, 8, T], BF16)
    HK = DFF // 128 // 2
    act_b = kb.sb([128, HK, T], BF16)
    sg = [kb.sb([128, T], F32) for _ in range(2)]
    rstd = kb.sb([128, T], F32)
    ps_stat = kb.ps([128, T], F32)
    dn = Dense(kb, T)
    xv = xT.rearrange("(kc p) t -> p kc t", p=128)
    xov = xo.rearrange("(kc p) t -> p kc t", p=128)

    for t in range(NT):
        tsl = slice(t * T, (t + 1) * T)
        kb.dma(x[:], xv[:, :, tsl], (), ['x'])
        modulate_tile(kb, x, 'x', mb, 'mb', acol1, bcol1, ones, merged, 'merged', ps_stat, 'pstat', rstd, 'rstd', cakeys)

        def act_h(kc, tt_):
            return mb[:, kc, :], ['mb']

        for i in range(3):
            ov = oT[i].rearrange("(kc p) t -> p kc t", p=128)
            kb.dma(stg[:], ov[:, :, tsl], (), ['stg'])
            kb.cp(oi_b[:], stg[:], ['stg'], ['oi_b'])

            def act_fn(kc, tt_):
                return oi_b[:, kc, :], ['oi_b']

            for cb in range(0, D, 256):
                def evac_g(c0, m, ps_ap, pk, tt_):
                    j = c0 // 128
                    kb.act(sg[j][:], ps_ap, AF.Sigmoid, [pk], [('sg', j)])

                def evac(c0, m, ps_ap, pk, tt_, i=i, cb=cb):
                    j = c0 // 128
                    nc_ = (cb + c0) // 128
                    if i == 0:
                        kb.tt(merged[:, nc_, :], ps_ap, sg[j][:], ALU.mult, [pk, ('sg', j)], ['merged'])
                    else:
                        kb.tt(sg[j][:], ps_ap, sg[j][:], ALU.mult, [pk, ('sg', j)], [('sg', j)])
                        kb.tt(merged[:, nc_, :], merged[:, nc_, :], sg[j][:], ALU.add, ['merged', ('sg', j)],
                              ['merged'], eng='pool')
                dn.run(w_g, 0, KC, i * D + cb, 256, act_h, evac_g)
                dn.run(w_br, i * 1024, 8, cb, 256, act_fn, evac)
        for kc in range(KC):
            kb.cp(mb[:, kc, :], merged[:, kc, :], ['merged'], ['mb'], eng='act')
        def act_fn2(kc, tt_):
            return mb[:, kc, :], ['mb']

        def evac2(c0, m, ps_ap, pk, tt_):
            nc_ = c0 // 128
            kb.stt(x[:, nc_, :], ps_ap, gt1[:, nc_:nc_ + 1], x[:, nc_, :], ALU.mult, ALU.add, [pk, 'x'] + cakeys, ['x'])
        dn.run(w_out, 0, KC, 0, D, act_fn2, evac2)
        modulate_tile(kb, x, 'x', mb, 'mb', acol, bcol, ones, merged, 'merged', ps_stat, 'pstat', rstd, 'rstd', cakeys)
        for half in range(2):
            def evac_g(c0, m, ps_ap, pk, tt_):
                kb.act(act_b[:, c0 // 128, :], ps_ap, AF.Silu, [pk], [('actb', c0 // 128)])

            def evac_u(c0, m, ps_ap, pk, tt_):
                j = c0 // 128
                kb.tt(act_b[:, j, :], ps_ap, act_b[:, j, :], ALU.mult, [pk, ('actb', j)], [('actb', j)])
            dn.run(w_gu, 0, KC, half * HK * 128, HK * 128, act_fn2, evac_g)
            dn.run(w_gu, 0, KC, DFF + half * HK * 128, HK * 128, act_fn2, evac_u)

            def act_fn3(kc, tt_):
                return act_b[:, kc, :], [('actb', kc)]

            def evac3(c0, m, ps_ap, pk, tt_):
                nc_ = c0 // 128
                kb.stt(x[:, nc_, :], ps_ap, gt2[:, nc_:nc_ + 1], x[:, nc_, :], ALU.mult, ALU.add, [pk, 'x'] + cakeys, ['x'])
            dn.run(w_dn, half * HK * 128, HK, 0, D, act_fn3, evac3)
        if final:
            for k in range(KC):
                kb.act(merged[:, k, :], x[:, k, :], AF.Square, ['x'], ['merged'])
            for k in range(KC):
                kb.mm(ps_stat[:], ones[:], merged[:, k, :], k == 0, k == KC - 1, ['merged', 'ones'], ['pstat'])
            rstd_from_sumsq(kb, rstd[:], ps_stat[:], float(D), EPS, ['pstat'], ['rstd'], merged[:, 0, :], 'merged')
            for k in range(KC):
                kb.stt(x[:, k, :], x[:, k, :], fnw[:, k:k + 1], rstd[:], ALU.mult, ALU.mult, ['x', 'rstd'] + cakeys, ['x'])
        kb.dma(xov[:, :, tsl], x[:], ['x'], ())


NSEL = 3396


def build_AB():
    kb = KB()
    xT = kb.dram("xT", [D, S], F32, "ExternalInput")
    ncols = kb.dram("ncols", [128, 3 * KC], F32, "ExternalInput")
    wsel = kb.dram("wsel", [D, NSEL], F32, "ExternalInput")
    md = mla_inputs(kb)
    gd = gdn_inputs(kb)
    rd = rwkv_inputs(kb)
    cd = delta_const_inputs(kb)
    oaT = kb.dram("oaT", [256, S], F32, "ExternalOutput")
    obT = kb.dram("obT", [256, S], F32, "ExternalOutput")
    ocT = kb.dram("ocT", [256, S], F32, "ExternalOutput")
    pint = kb.dram("pint", [NSEL, S + 3], F32, "Internal")

    with kb.phase():
        TM = 256
        ones = kb.sb([128, 128], F32)
        kb.memset(ones[:], 1.0, ['ones'])
        zt = kb.sb([128, 3], F32)
        kb.memset(zt[:], 0.0, ['zt'])
        for r0 in range(0, NSEL, 128):
            m = min(128, NSEL - r0)
            kb.dma(pint[r0:r0 + m, 0:3], zt[0:m, :], ['zt'], ())
        cols = kb.sb([128, 3 * KC], F32)
        kb.dma(cols[:], ncols, (), ['cols'])
        acol = kb.sb([128, KC], F32)
        kb.ts(acol[:], cols[:, KC:2 * KC], 1.0, None, ALU.add, reads=['cols'], writes=['acol'])
        kb.tt(acol[:], acol[:], cols[:, 0:KC], ALU.mult, ['acol', 'cols'], ['acol'])
        bcol = cols[:, 2 * KC:3 * KC]
        wres = kb.sb([128, KC, NSEL], BF16)
        wstg = [kb.sb([128, KC, 128], F32) for _ in range(2)]
        for ci, c0 in enumerate(range(0, NSEL, 128)):
            cw = min(128, NSEL - c0)
            b_ = ci % 2
            kb.dma(wstg[b_][:, :, 0:cw], wsel[:, c0:c0 + cw].rearrange("(kc p) n -> p kc n", p=128), (), [('wstg', b_)])
            kb.cp(wres[:, :, c0:c0 + cw], wstg[b_][:, :, 0:cw], [('wstg', b_)], ['wres'], eng=('pool' if b_ else 'act'))
        xt = [kb.sb([128, KC, TM], F32) for _ in range(2)]
        scr = kb.sb([128, KC, TM], F32)
        hT = [kb.sb([128, KC, TM], BF16) for _ in range(2)]
        rstd = kb.sb([128, TM], F32)
        ps_stat = kb.ps([128, TM], F32)
        pp = [kb.ps([128, 512], F32) for _ in range(4)]
        ob = [kb.sb([128, TM], F32) for _ in range(4)]
        xv = xT.rearrange("(kc p) t -> p kc t", p=128)
        it = 0
        for t in range(S // TM):
            bi = t % 2
            kb.dma(xt[bi][:], xv[:, :, t * TM:(t + 1) * TM], (), [('x', bi)])
            modulate_tile(kb, xt[bi], ('x', bi), hT[bi], ('h', bi), acol, bcol, ones, scr, 'scr', ps_stat, 'pstat',
                          rstd, 'rstd')
            for c0 in range(0, NSEL, 128):
                m = min(128, NSEL - c0)
                j = it % 4
                it += 1
                for k in range(KC):
                    kb.mm(pp[j][0:m, 0:TM], wres[:, k, c0:c0 + m], hT[bi][:, k, :], k == 0, k == KC - 1,
                          ['wres', ('h', bi)], [('pp', j)])
                kb.cp(ob[j][0:m, :], pp[j][0:m, 0:TM], [('pp', j)], [('ob', j)], eng=('act' if j % 2 else 'dve'))
                kb.dma(pint[c0:c0 + m, 3 + t * TM:3 + (t + 1) * TM], ob[j][0:m, :], [('ob', j)], ())
    with kb.phase():
        emit_mla(kb, pint, md, oaT)
    with kb.phase():
        kb.set_stream(0)
        cst0 = DeltaConsts(kb, cd)
        ps0 = delta_psum(kb)
        emit_gdn(kb, pint[1344:2112, :], pint[2112:2368, 3:3 + S], pint[2368:2372, 3:3 + S], gd, obT, cst0, psum=ps0)
        emit_rwkv(kb, pint[2372:3140, 2:3 + S], pint[3140:3396, 2:3 + S], rd, ocT, cst0, heads=(0,), psum=ps0)
        kb.set_stream(1)
        cst1 = DeltaConsts(kb, cd)
        ps1 = delta_psum(kb)
        emit_rwkv(kb, pint[2372:3140, 2:3 + S], pint[3140:3396, 2:3 + S], rd, ocT, cst1, heads=(1, 2, 3), psum=ps1)
        kb.set_stream(0)
    return kb.finish()


_PROGS = {}
N_LAUNCH = [0]
FUSED = False


def prog(name, builder):
    if name not in _PROGS:
        _PROGS[name] = builder()
    return _PROGS[name]


def launch(name, builder, ims):
    N_LAUNCH[0] += 1
    return run(prog(name, builder), ims)


OFF_G = 1344
OFF_R = 1344 + 4112
OFF_PG = IN_W - 3 * D


def sel_columns(g):
    idx = list(range(0, 1344))
    hs = [2 * g, 2 * g + 1]
    for part in range(3):
        for h in hs:
            idx += list(range(OFF_G + part * 1024 + h * 128, OFF_G + part * 1024 + (h + 1) * 128))
    for h in hs:
        idx += list(range(OFF_G + 3072 + h * 128, OFF_G + 3072 + (h + 1) * 128))
    idx += [OFF_G + 4096 + h for h in hs] + [OFF_G + 4104 + h for h in hs]
    rh = [4 * g + i for i in range(4)]
    for part in range(3):
        for h in rh:
            idx += list(range(OFF_R + part * 1024 + h * 64, OFF_R + part * 1024 + (h + 1) * 64))
    idx += list(range(OFF_R + 3072, OFF_R + 3328))
    assert len(idx) == NSEL
    return np.array(idx)


def ab_inputs(inp, l, b, g, xTb, mod, cc, ss, consts):
    f32 = np.float32
    d = {}
    d['xT'] = xTb
    d['ncols'] = np.ascontiguousarray(np.concatenate(
        [fm_vec(inp['norm1_w'][l]), fm_vec(mod[b, l, 1]), fm_vec(mod[b, l, 0])], axis=1), dtype=f32)
    d['wsel'] = np.ascontiguousarray(inp['w_in'][l][:, sel_columns(g)])
    hs = [2 * g, 2 * g + 1]
    wuq = inp['mla_w_uq'][l].reshape(768, 8, 192)
    wukv = inp['mla_w_ukv'][l].reshape(512, 8, 256)
    wr = wuq[:, hs, 128:]
    wrs = np.concatenate([wr[:, :, 32:], wr[:, :, :32]], axis=2)
    d.update(cc=cc[b], ss=ss[b], qnw=fm_vec(inp['mla_q_norm_w'][l]), kvnw=fm_vec(inp['mla_kv_norm_w'][l]),
             wq_n=np.ascontiguousarray(wuq[:, hs, :128].reshape(768, 256)),
             wq_r=np.ascontiguousarray(wr.reshape(768, 128)), wq_rs=np.ascontiguousarray(wrs.reshape(768, 128)),
             wk=np.ascontiguousarray(wukv[:, hs, :128].reshape(512, 256)),
             wv=np.ascontiguousarray(wukv[:, hs, 128:].reshape(512, 256)), mask=consts['mask'])
    convw = np.zeros((128, 24), f32)
    cwl = inp['gdn_conv_w'][l]
    for part in range(3):
        for i, h in enumerate(hs):
            for j in range(4):
                convw[:, (part * 2 + i) * 4 + j] = cwl[j, part * 1024 + h * 128: part * 1024 + (h + 1) * 128]
    hcols = np.zeros((128, 5), f32)
    for i, h in enumerate(hs):
        hcols[:, 2 * i] = inp['gdn_a_log'][l][h]
        hcols[:, 2 * i + 1] = inp['gdn_dt_bias'][l][h]
    hcols[:, 4] = inp['gdn_norm_w'][l]
    d.update(gd_convw=convw, gd_hcols=hcols)
    rh = [4 * g + i for i in range(4)]
    mu = inp['rwkv_mu'][l]
    W = 1024
    cols = np.zeros((64, 40), f32)
    for part in range(3):
        for i, h in enumerate(rh):
            cols[:, part * 4 + i] = mu[part * W + h * 64: part * W + (h + 1) * 64]
    for j, nm in enumerate(['rwkv_w0', 'rwkv_a0', 'rwkv_k_k', 'rwkv_k_a', 'rwkv_r_k', 'rwkv_lnx_w', 'rwkv_lnx_b']):
        v = inp[nm][l].reshape(-1)
        for i, h in enumerate(rh):
            cols[:, 12 + j * 4 + i] = v[h * 64:(h + 1) * 64]
    mulow = np.zeros((128, 3), f32)
    mulow[:64, 0] = mu[3 * W:3 * W + 64]
    mulow[:64, 1] = mu[3 * W + 64:3 * W + 128]
    mulow[:, 2] = mu[3 * W + 128:]
    hc = np.concatenate([np.arange(h * 64, (h + 1) * 64) for h in rh])
    d.update(rw_cols=cols, rw_mulow=mulow, rw_w_up=np.ascontiguousarray(inp['rwkv_w_up'][l][:, hc]),
             rw_a_up=np.ascontiguousarray(inp['rwkv_a_up'][l][:, hc]),
             rw_g_up=np.ascontiguousarray(inp['rwkv_g_up'][l][:, hc]))
    for nm, _ in DCONST_SHAPES:
        d[nm] = consts[nm]
    return d


def run_prep(inp):
    c = inp['c']
    pos = inp['positions']
    cT = np.ascontiguousarray(c.T.reshape(KC, 128, B).transpose(1, 0, 2).reshape(128, KC * B))
    wall = np.concatenate([inp['w_ada'][l] for l in range(DEPTH)], axis=1)
    ball = np.concatenate([inp['b_ada'][l] for l in range(DEPTH)], axis=0)
    inv = (1.0 / (np.float32(10000.0) ** (np.arange(0, 64, 2, dtype=np.float32) / np.float32(64)))).astype(np.float32)
    cst = np.zeros((64, 2), np.float32)
    cst[:, 0] = np.tile(inv, 2)
    cst[:32, 1] = -1
    cst[32:, 1] = 1
    ims = []
    for i in range(NCORES):
        cs = slice(i * PREP_COLS, (i + 1) * PREP_COLS)
        ims.append(dict(cT=cT, wada=np.ascontiguousarray(wall[:, cs]),
                        bada=np.ascontiguousarray(np.broadcast_to(ball[cs], (B, PREP_COLS))),
                        pos=np.ascontiguousarray(np.broadcast_to(pos[i % B], (64, S))), cst=cst))
    res = launch('prep', build_prep, ims)
    mod = np.concatenate([r['mod'] for r in res], axis=1).reshape(B, DEPTH, 6, D)
    cc = [res[b]['cc'] for b in range(B)]
    ss = [res[b]['ss'] for b in range(B)]
    return mod, cc, ss


def run_layer(inp, l, xsh, mod, cc, ss, consts, final):
    xTb = [np.ascontiguousarray(np.concatenate(xsh[b * 4:(b + 1) * 4], axis=1)) for b in range(B)]
    ims = [ab_inputs(inp, l, i // 4, i % 4, xTb[i // 4], mod, cc, ss, consts) for i in range(NCORES)]
    res = launch('AB', build_AB, ims)
    wg = np.ascontiguousarray(inp['w_in'][l][:, OFF_PG:])
    ims = []
    for i in range(NCORES):
        b, q = i // 4, i % 4
        tsl = slice(q * TOK, (q + 1) * TOK)
        o = {nm: np.ascontiguousarray(np.concatenate([res[b * 4 + g][nm] for g in range(4)], axis=0)[:, tsl])
             for nm in ('oaT', 'obT', 'ocT')}
        cols = np.concatenate([fm_vec(mod[b, l, 2]), fm_vec(inp['norm2_w'][l]), fm_vec(mod[b, l, 4]),
                               fm_vec(mod[b, l, 3]), fm_vec(mod[b, l, 5]), fm_vec(inp['final_norm_w']),
                               fm_vec(inp['norm1_w'][l]), fm_vec(mod[b, l, 1]), fm_vec(mod[b, l, 0])], axis=1)
        ims.append(dict(xT=xsh[i], cols=np.ascontiguousarray(cols, dtype=np.float32), w_g=wg,
                        w_branch=inp['w_branch'][l], w_out=inp['w_out'][l], w_gate_up=inp['w_gate_up'][l],
                        w_down=inp['w_down'][l], **o))
    name = 'Cf' if final else 'C'
    res = launch(name, (lambda: build_stageC(True)) if final else (lambda: build_stageC(False)), ims)
    return [r['xo'] for r in res]


def make_consts():
    c = delta_consts()
    c['mask'] = causal_masks()
    return c


def kernel(**inp):
    inp = {k: np.asarray(v) for k, v in inp.items()}
    N_LAUNCH[0] = 0
    if FUSED:
        return kernel_fused(inp, DEPTH)
    return kernel_unfused(**inp)


def kernel_unfused(**inp):
    inp = {k: np.asarray(v) for k, v in inp.items()}
    N_LAUNCH[0] = 0
    mod, cc, ss = run_prep(inp)
    consts = make_consts()
    xf = inp['x'].reshape(B * S, D)
    xsh = [np.ascontiguousarray(xf[i * TOK:(i + 1) * TOK].T) for i in range(NCORES)]
    for l in range(DEPTH):
        xsh = run_layer(inp, l, xsh, mod, cc, ss, consts, final=(l == DEPTH - 1))
    out = np.concatenate([s_.T for s_ in xsh], axis=0).reshape(B, S, D)
    return np.ascontiguousarray(out, dtype=np.float32)


GBLK = NSEL - 1344
NPROJ = 1344 + 4 * GBLK


def emit_projA(kb, x_src, nw1, sc1, sh1, ckeys, wproj, pints):
    TM = 256
    ones = kb.sb([128, 128], F32)
    kb.memset(ones[:], 1.0, ['ones'])
    zt = kb.sb([128, 3], F32)
    kb.memset(zt[:], 0.0, ['zt'])
    for pt in pints:
        nr = pt.shape[0]
        for r0 in range(0, nr, 128):
            m = min(128, nr - r0)
            kb.dma(pt[r0:r0 + m, 0:3], zt[0:m, :], ['zt'], ())
    acol = kb.sb([128, KC], F32)
    kb.ts(acol[:], sc1, 1.0, None, ALU.add, reads=ckeys, writes=['acol'])
    kb.tt(acol[:], acol[:], nw1, ALU.mult, ['acol'] + ckeys, ['acol'])
    wres = kb.sb([128, KC, NSEL], BF16)
    wstg = [kb.sb([128, KC, 128], F32) for _ in range(2)]
    xt = [kb.sb([128, KC, TM], F32) for _ in range(2)]
    scr = kb.sb([128, KC, TM], F32)
    hT = [kb.sb([128, KC, TM], BF16) for _ in range(2)]
    rstd = kb.sb([128, TM], F32)
    ps_stat = kb.ps([128, TM], F32)
    pp = [kb.ps([128, 512], F32) for _ in range(4)]
    ob = [kb.sb([128, TM], F32) for _ in range(4)]
    xv = x_src.rearrange("(kc p) t -> p kc t", p=128)
    it = 0
    ci = 0
    passes = [(0, [(pints[0], 1344), (pints[1], GBLK)])] + \
             [(1344 + g * GBLK, [(pints[1 + g], GBLK)]) for g in range(1, 4)]
    for w0, segs in passes:
        pw = sum(n for _, n in segs)
        for c0 in range(0, pw, 128):
            cw = min(128, pw - c0)
            b_ = ci % 2
            ci += 1
            kb.dma(wstg[b_][:, :, 0:cw], wproj[:, w0 + c0:w0 + c0 + cw].rearrange("(kc p) n -> p kc n", p=128),
                   (), [('wstg', b_)])
            kb.cp(wres[:, :, c0:c0 + cw], wstg[b_][:, :, 0:cw], [('wstg', b_)], ['wres'],
                  eng=('pool' if b_ else 'act'))
        for t in range(S // TM):
            bi = t % 2
            kb.dma(xt[bi][:], xv[:, :, t * TM:(t + 1) * TM], (), [('x', bi)])
            modulate_tile(kb, xt[bi], ('x', bi), hT[bi], ('h', bi), acol, sh1, ones, scr, 'scr', ps_stat, 'pstat',
                          rstd, 'rstd', ckeys)
            soff = 0
            for pt, n in segs:
                for c0 in range(0, n, 128):
                    m = min(128, n - c0)
                    j = it % 4
                    it += 1
                    for k in range(KC):
                        kb.mm(pp[j][0:m, 0:TM], wres[:, k, soff + c0:soff + c0 + m], hT[bi][:, k, :], k == 0,
                              k == KC - 1, ['wres', ('h', bi)], [('pp', j)])
                    kb.cp(ob[j][0:m, :], pp[j][0:m, 0:TM], [('pp', j)], [('ob', j)], eng=('act' if j % 2 else 'dve'))
                    kb.dma(pt[c0:c0 + m, 3 + t * TM:3 + (t + 1) * TM], ob[j][0:m, :], [('ob', j)], ())
                soff += n


def emit_rope(kb, pos, cst, cc, ss):
    CHW = 2048
    cs = kb.sb([64, 2], F32)
    kb.dma(cs[:], cst, (), ['cs'])
    pi = kb.sb([64, CHW], I32)
    ang = kb.sb([64, CHW], F32)
    r = kb.sb([64, CHW], F32)
    ki = kb.sb([64, CHW], I32)
    kf = kb.sb([64, CHW], F32)
    m = kb.sb([64, CHW], F32)
    osb = {'s': kb.sb([64, CHW], F32), 'c': kb.sb([64, CHW], F32)}
    k_ = 'rope'
    for ci in range(S // CHW):
        kb.dma(pi[:], pos[:, ci * CHW:(ci + 1) * CHW], (), [k_])
        kb.cp(ang[:], pi[:], [k_], [k_])
        kb.ts(ang[:], ang[:], cs[:, 0:1], None, ALU.mult, reads=[k_, 'cs'], writes=[k_])
        kb.ts(kf[:], ang[:], float(1.0 / (2 * np.pi)), None, ALU.mult, reads=[k_], writes=[k_])
        kb.cp(ki[:], kf[:], [k_], [k_])
        kb.cp(kf[:], ki[:], [k_], [k_])
        kb.stt(ang[:], kf[:], -6.28125, ang[:], ALU.mult, ALU.add, [k_], [k_])
        kb.stt(ang[:], kf[:], -0.0019353071795864769, ang[:], ALU.mult, ALU.add, [k_], [k_])
        for which, shift, dst in (('s', 0.0, ss), ('c', float(np.pi / 2), cc)):
            kb.ts(r[:], ang[:], shift, None, ALU.add, reads=[k_], writes=[k_])
            kb.ts(m[:], r[:], float(np.pi), None, ALU.is_gt, reads=[k_], writes=[k_])
            kb.stt(r[:], m[:], float(-2 * np.pi), r[:], ALU.mult, ALU.add, [k_], [k_])
            kb.ts(m[:], r[:], float(-np.pi), None, ALU.is_lt, reads=[k_], writes=[k_])
            kb.stt(r[:], m[:], float(2 * np.pi), r[:], ALU.mult, ALU.add, [k_], [k_])
            kb.ts(r[:], r[:], 3.1415925, -3.1415925, ALU.min, ALU.max, reads=[k_], writes=[k_])
            o = osb[which]
            kb.act(o[:], r[:], AF.Sin, [k_], [(k_, which)])
            if which == 's':
                kb.ts(o[:], o[:], cs[:, 1:2], None, ALU.mult, reads=[(k_, which), 'cs'], writes=[(k_, which)])
            kb.dma(dst[:, ci * CHW:(ci + 1) * CHW], o[:], [(k_, which)], ())


def build_fused(nlayers=DEPTH):
    kb = KB()
    NM = DEPTH * 96
    xT = kb.dram("xT", [D, S], F32, "ExternalInput")
    cTd = kb.dram("cT", [128, KC], F32, "ExternalInput")
    wada = kb.dram("wada", [DEPTH * D, 6 * D], F32, "ExternalInput")
    badad = kb.dram("bada", [128, NM], F32, "ExternalInput")
    pos = kb.dram("pos", [64, S], I32, "ExternalInput")
    cstd = kb.dram("cst", [64, 2], F32, "ExternalInput")
    fnwd = kb.dram("fnw", [128, KC], F32, "ExternalInput")
    maskd = kb.dram("mask", [128, 2048], F32, "ExternalInput")
    cd = delta_const_inputs(kb)
    L = []
    for l in range(nlayers):
        e = {}
        e['lcols'] = kb.dram("lcols_%d" % l, [128, 2 * KC], F32, "ExternalInput")
        e['wproj'] = kb.dram("wproj_%d" % l, [D, NPROJ], F32, "ExternalInput")
        e['w_g'] = kb.dram("w_g_%d" % l, [D, 3 * D], F32, "ExternalInput")
        e['w_br'] = kb.dram("w_branch_%d" % l, [3072, D], F32, "ExternalInput")
        e['w_out'] = kb.dram("w_out_%d" % l, [D, D], F32, "ExternalInput")
        e['w_gu'] = kb.dram("w_gate_up_%d" % l, [D, 2 * DFF], F32, "ExternalInput")
        e['w_dn'] = kb.dram("w_down_%d" % l, [DFF, D], F32, "ExternalInput")
        e['qnw'] = kb.dram("qnw_%d" % l, [128, 6], F32, "ExternalInput")
        e['kvnw'] = kb.dram("kvnw_%d" % l, [128, 4], F32, "ExternalInput")
        e['g'] = []
        for g in range(4):
            sfx = "_%d_%d" % (l, g)
            m = {}
            for nm, shp in (('wq_n', [768, 256]), ('wq_r', [768, 128]), ('wq_rs', [768, 128]), ('wk', [512, 256]),
                            ('wv', [512, 256])):
                m[nm] = kb.dram(nm + sfx, shp, F32, "ExternalInput")
            e['g'].append(dict(mla=m, gdn=gdn_inputs(kb, sfx), rwkv=rwkv_inputs(kb, sfx)))
        L.append(e)
    xo = kb.dram("xo", [D, S], F32, "ExternalOutput")
    pints = [kb.dram("pint_m", [1344, S + 3], F32, "Internal")] + \
            [kb.dram("pint_g%d" % g, [GBLK, S + 3], F32, "Internal") for g in range(4)]
    cc_d = kb.dram("cc_d", [64, S], F32, "Internal")
    ss_d = kb.dram("ss_d", [64, S], F32, "Internal")
    xbuf = [kb.dram("xbuf%d" % i, [D, S], F32, "Internal") for i in range(2)]
    o_d = [kb.dram("o_d%d" % i, [1024, S], F32, "Internal") for i in range(3)]

    modcol = kb.sb([128, NM], F32)
    fnw_s = kb.sb([128, KC], F32)
    lc_s = kb.sb([128, nlayers * 2 * KC], F32)
    with kb.phase():
        kb.dma(fnw_s[:], fnwd, (), ['pc'])
        for l in range(nlayers):
            kb.dma(lc_s[:, l * 2 * KC:(l + 1) * 2 * KC], L[l]['lcols'], (), ['pc'])
        c_sb = kb.sb([128, KC], F32)
        b_sb = kb.sb([128, NM], F32)
        kb.dma(c_sb[:], cTd, (), ['c'])
        kb.dma(b_sb[:], badad, (), ['b'])
        kb.act(c_sb[:], c_sb[:], AF.Silu, ['c'], ['c'])
        wt = [kb.sb([128, KC, 512], F32) for _ in range(2)]
        pp = [kb.ps([128, 512], F32) for _ in range(2)]
        si = 0
        for l in range(DEPTH):
            for s0 in range(0, 6 * D, 512):
                bi = si % 2
                si += 1
                kb.dma(wt[bi][:], wada[l * D:(l + 1) * D, s0:s0 + 512].rearrange("(kc p) n -> p kc n", p=128),
                       (), [('w', bi)])
                for fj in range(4):
                    col = s0 // 128 + fj
                    for k in range(KC):
                        kb.mm(pp[l % 2][:, col:col + 1], wt[bi][:, k, fj * 128:(fj + 1) * 128], c_sb[:, k:k + 1],
                              k == 0, k == KC - 1, ['c', ('w', bi)], [('pp', l % 2)])
            kb.tt(modcol[:, l * 96:(l + 1) * 96], pp[l % 2][:, 0:96], b_sb[:, l * 96:(l + 1) * 96], ALU.add,
                  [('pp', l % 2), 'b'], ['modcol'])
        emit_rope(kb, pos, cstd, cc_d, ss_d)
    mc = lambda l, j: modcol[:, (l * 6 + j) * KC:(l * 6 + j + 1) * KC]
    for l in range(nlayers):
        e = L[l]
        final = (l == nlayers - 1)
        x_src = xT if l == 0 else xbuf[(l - 1) % 2]
        x_dst = xo if final else xbuf[l % 2]
        nw1 = lc_s[:, l * 2 * KC:l * 2 * KC + KC]
        nw2 = lc_s[:, l * 2 * KC + KC:(l + 1) * 2 * KC]
        with kb.phase():
            emit_projA(kb, x_src, nw1, mc(l, 1), mc(l, 0), [], e['wproj'], pints)
        for g in range(4):
            with kb.phase():
                md = dict(e['g'][g]['mla'])
                md.update(cc=cc_d, ss=ss_d, qnw=e['qnw'], kvnw=e['kvnw'], mask=maskd)
                emit_mla(kb, pints[0], md, o_d[0][g * 256:(g + 1) * 256, :])
        for g in range(4):
            pint = pints[1 + g]
            with kb.phase():
                kb.set_stream(0)
                cst0 = DeltaConsts(kb, cd)
                ps0 = delta_psum(kb)
                emit_gdn(kb, pint[0:768, :], pint[768:1024, 3:3 + S], pint[1024:1028, 3:3 + S],
                         e['g'][g]['gdn'], o_d[1][g * 256:(g + 1) * 256, :], cst0, psum=ps0)
                emit_rwkv(kb, pint[1028:1796, 2:3 + S], pint[1796:2052, 2:3 + S],
                          e['g'][g]['rwkv'], o_d[2][g * 256:(g + 1) * 256, :], cst0, heads=(0,), psum=ps0)
                kb.set_stream(1)
                cst1 = DeltaConsts(kb, cd)
                ps1 = delta_psum(kb)
                emit_rwkv(kb, pint[1028:1796, 2:3 + S], pint[1796:2052, 2:3 + S],
                          e['g'][g]['rwkv'], o_d[2][g * 256:(g + 1) * 256, :], cst1, heads=(1, 2, 3), psum=ps1)
                kb.set_stream(0)
        with kb.phase():
            ca = dict(gt1=mc(l, 2), nw2=nw2, sc2=mc(l, 4), sh2=mc(l, 3), gt2=mc(l, 5), fnw=fnw_s[:], nw1=nw1,
                      sc1=mc(l, 1), sh1=mc(l, 0))
            emit_stageC(kb, S, x_src, o_d, ca, [], e['w_g'], e['w_br'], e['w_out'], e['w_gu'], e['w_dn'], x_dst, final)
    return kb.finish()


def fused_inputs(inp, b, consts, nlayers=DEPTH):
    f32 = np.float32
    d = {}
    d['xT'] = np.ascontiguousarray(inp['x'][b].T)
    d['cT'] = fm_vec(inp['c'][b])
    d['wada'] = np.ascontiguousarray(inp['w_ada'].reshape(DEPTH * D, 6 * D))
    d['bada'] = np.ascontiguousarray(np.concatenate(
        [fm_vec(inp['b_ada'][l, j * D:(j + 1) * D]) for l in range(DEPTH) for j in range(6)], axis=1), dtype=f32)
    d['pos'] = np.ascontiguousarray(np.broadcast_to(inp['positions'][b], (64, S)))
    inv = (1.0 / (np.float32(10000.0) ** (np.arange(0, 64, 2, dtype=np.float32) / np.float32(64)))).astype(np.float32)
    cst = np.zeros((64, 2), f32)
    cst[:, 0] = np.tile(inv, 2)
    cst[:32, 1] = -1
    cst[32:, 1] = 1
    d['cst'] = cst
    d['fnw'] = fm_vec(inp['final_norm_w'])
    d['mask'] = consts['mask']
    for nm, _ in DCONST_SHAPES:
        d[nm] = consts[nm]
    dummy_mod = np.zeros((B, DEPTH, 6, D), f32)
    for l in range(nlayers):
        d['lcols_%d' % l] = np.ascontiguousarray(
            np.concatenate([fm_vec(inp['norm1_w'][l]), fm_vec(inp['norm2_w'][l])], axis=1), dtype=f32)
        cols = np.concatenate([np.arange(1344)] + [sel_columns(g)[1344:] for g in range(4)])
        d['wproj_%d' % l] = np.ascontiguousarray(inp['w_in'][l][:, cols])
        d['w_g_%d' % l] = np.ascontiguousarray(inp['w_in'][l][:, OFF_PG:])
        d['w_branch_%d' % l] = inp['w_branch'][l]
        d['w_out_%d' % l] = inp['w_out'][l]
        d['w_gate_up_%d' % l] = inp['w_gate_up'][l]
        d['w_down_%d' % l] = inp['w_down'][l]
        d['qnw_%d' % l] = fm_vec(inp['mla_q_norm_w'][l])
        d['kvnw_%d' % l] = fm_vec(inp['mla_kv_norm_w'][l])
        for g in range(4):
            sfx = "_%d_%d" % (l, g)
            a = ab_inputs(inp, l, b, g, None, dummy_mod, [None] * B, [None] * B, consts)
            for nm in ('wq_n', 'wq_r', 'wq_rs', 'wk', 'wv', 'gd_convw', 'gd_hcols', 'rw_cols', 'rw_mulow',
                       'rw_w_up', 'rw_a_up', 'rw_g_up'):
                d[nm + sfx] = a[nm]
    return d


def kernel_fused(inp, nlayers=DEPTH):
    consts = make_consts()
    ims = [fused_inputs(inp, b, consts, nlayers) for b in range(B)]
    res = launch('fused%d' % nlayers, lambda: build_fused(nlayers), ims)
    out = np.stack([r['xo'].T for r in res], axis=0)
    return np.ascontiguousarray(out, dtype=np.float32)
```

```python
import contextlib
import numpy as np
import concourse.bass as bass
import concourse.mybir as mybir
from concourse.bass_utils import run_bass_kernel_spmd

F32 = mybir.dt.float32
BF16 = mybir.dt.bfloat16
I32 = mybir.dt.int32
AF = mybir.ActivationFunctionType
ALU = mybir.AluOpType

NCORES = 8
D = 2048
B = 2
S = 8192
DEPTH = 4
KC = D // 128
IN_W = 14928
DFF = 5632
EPS = 1e-6

ENGS = ['pe', 'act', 'dve', 'pool', 'sp']
EPOCH = 60000
NDMA = 12
DMA_EPOCH = 3900


class KB:
    def __init__(self):
        self.nc = bass.Bass("TRN2", target_bir_lowering=False)
        self.stack = contextlib.ExitStack()
        self.tstack = self.stack
        self.nsem = 0
        self.stream = 0
        self.sops = {}
        self.cnt = {}
        self.sem = {}
        self.dsem = {}
        self.dcnt = {}
        self.drr = {}
        self.waited = {}
        self.final_ops = {e: [] for e in ENGS}
        self.last_w = {}
        self.readers = {}
        self.dma_toks = []
        self.ntile = 0
        self.set_stream(0)

    def _newsem(self):
        self.nsem += 1
        return self.stack.enter_context(self.nc.semaphore("s%d" % self.nsem))

    def set_stream(self, i):
        self.stream = i
        if i not in self.sops:
            self.sops[i] = []
            self.dsem[i] = [self._newsem() for _ in range(NDMA)]
            self.dcnt[i] = [0] * NDMA
            self.drr[i] = 0
            for e in ENGS:
                self.cnt[(e, i)] = 0
                self.sem[(e, i)] = self._newsem()
                self.waited[(e, i)] = {}

    def dram(self, name, shape, dt, kind):
        return self.nc.dram_tensor(name, list(shape), dt, kind=kind).ap()

    def sb(self, shape, dt, name=None):
        self.ntile += 1
        return self.tstack.enter_context(
            self.nc.sbuf_tensor(name or ("t%d" % self.ntile), list(shape), dt))

    def ps(self, shape, dt=F32, name=None):
        self.ntile += 1
        return self.tstack.enter_context(
            self.nc.psum_tensor(name or ("p%d" % self.ntile), list(shape), dt))

    @contextlib.contextmanager
    def phase(self):
        old = self.tstack
        self.tstack = contextlib.ExitStack()
        try:
            yield
        finally:
            self.barrier()
            self.tstack.close()
            self.tstack = old

    def flush(self):
        lists = [(k, v) for k, v in sorted(self.sops.items()) if v]
        if len(lists) == 1:
            for (e, deps, fn, sem, inc) in lists[0][1]:
                self.final_ops[e].append((deps, fn, sem, inc))
        elif lists:
            pos = [0] * len(lists)
            tot = [len(v) for _, v in lists]
            n = sum(tot)
            for _ in range(n):
                best, bf = None, None
                for i in range(len(lists)):
                    if pos[i] < tot[i]:
                        fr = pos[i] / tot[i]
                        if bf is None or fr < bf:
                            best, bf = i, fr
                (e, deps, fn, sem, inc) = lists[best][1][pos[best]]
                pos[best] += 1
                self.final_ops[e].append((deps, fn, sem, inc))
        for k in self.sops:
            self.sops[k] = []

    def barrier(self):
        self.flush()
        toks = {}
        for (e, st), c in self.cnt.items():
            if c > 0 and e != 'sp':
                toks[id(self.sem[(e, st)])] = (self.sem[(e, st)], c)
        for (_, sem, val) in self.dma_toks:
            k = id(sem)
            if k not in toks or toks[k][1] < val:
                toks[k] = (sem, val)
        self.dma_toks = [('dma', s_, v_) for (s_, v_) in toks.values()]
        for e in ENGS:
            deps = list(toks.values())
            self.final_ops[e].append((deps, None, None, 0))
            for st in self.sops:
                for k, (sem, val) in toks.items():
                    if self.waited[(e, st)].get(k, 0) < val:
                        self.waited[(e, st)][k] = val
        self.last_w = {}
        self.readers = {}

    def _deps(self, eng, reads, writes):
        deps = {}

        def add(tok):
            te, sem, val = tok
            if te == 'pe' and eng == 'pe':
                return
            k = id(sem)
            if k not in deps or deps[k][1] < val:
                deps[k] = (sem, val)

        for k in reads:
            w = self.last_w.get(k)
            if w is not None:
                add(w)
        for k in writes:
            w = self.last_w.get(k)
            if w is not None:
                add(w)
            for r in self.readers.get(k, {}).values():
                if isinstance(r, list):
                    for t in r:
                        add(t)
                else:
                    add(r)
        out = []
        wd = self.waited[(eng, self.stream)]
        for k, (sem, val) in deps.items():
            if wd.get(k, 0) >= val:
                continue
            wd[k] = val
            out.append((sem, val))
        return out

    def _update(self, tok, reads, writes, is_dma):
        for k in writes:
            self.last_w[k] = tok
            self.readers[k] = {}
        for k in reads:
            rd = self.readers.setdefault(k, {})
            if is_dma:
                rd.setdefault('dma', []).append(tok)
            else:
                rd[tok[0]] = tok

    def op(self, eng, fn, reads=(), writes=()):
        st = self.stream
        reads = [(st, k) for k in reads]
        writes = [(st, k) for k in writes]
        deps = self._deps(eng, reads, writes)
        es = (eng, st)
        if self.cnt[es] >= EPOCH:
            self.sem[es] = self._newsem()
            self.cnt[es] = 0
        self.cnt[es] += 1
        tok = (eng, self.sem[es], self.cnt[es])
        self.sops[st].append((eng, deps, fn, self.sem[es], 1))
        self._update(tok, reads, writes, False)

    def dma(self, out, in_, reads=(), writes=(), q='sp', slow=False):
        st = self.stream
        reads = [(st, k) for k in reads]
        writes = [(st, k) for k in writes]
        i = self.drr[st]
        self.drr[st] = (i + 1) % NDMA
        dsem, dcnt = self.dsem[st], self.dcnt[st]
        if dcnt[i] >= DMA_EPOCH:
            dsem[i] = self._newsem()
            dcnt[i] = 0
        deps = self._deps(q, reads, writes)
        wd = self.waited[(q, st)]
        if dcnt[i] > 0:
            k = id(dsem[i])
            v = 16 * dcnt[i]
            if wd.get(k, 0) < v:
                wd[k] = v
                deps.append((dsem[i], v))
        dcnt[i] += 1
        tok = ('dma', dsem[i], 16 * dcnt[i])
        if slow:
            fn = lambda e: e.dma_start(out=out, in_=in_, allow_slow_non_contiguous=True)
        else:
            fn = lambda e: e.dma_start(out=out, in_=in_)
        self.sops[st].append((q, deps, fn, dsem[i], 16))
        self._update(tok, reads, writes, True)
        self.dma_toks.append(tok)

    def finish(self):
        self.flush()
        final = {}
        for (_, sem, val) in self.dma_toks:
            k = id(sem)
            if k not in final or final[k][1] < val:
                final[k] = (sem, val)
        for (e, st), c in self.cnt.items():
            if c > 0 and e != 'sp':
                final[id(self.sem[(e, st)])] = (self.sem[(e, st)], c)
        self.final_ops['sp'].append((list(final.values()), None, None, 0))
        nc = self.nc
        ops = self.final_ops

        def replay(name, e):
            for deps, fn, sem, inc in ops[name]:
                for (s, v) in deps:
                    e.wait_ge(s, v)
                if fn is not None:
                    fn(e).then_inc(sem, inc)

        with nc.Block() as block:
            @block.tensor
            def _(e):
                replay('pe', e)

            @block.scalar
            def _(e):
                replay('act', e)

            @block.vector
            def _(e):
                replay('dve', e)

            @block.gpsimd
            def _(e):
                replay('pool', e)

            @block.sync
            def _(e):
                replay('sp', e)
        self.stack.close()
        return nc

    def mm(self, out, lhsT, rhs, start, stop, reads, writes):
        self.op('pe', lambda e: e.matmul(out, lhsT, rhs, start=start, stop=stop), reads, writes)

    def tr(self, out, in_, ident, reads, writes):
        self.op('pe', lambda e: e.transpose(out, in_, ident), reads, writes)

    def act(self, out, in_, func, reads, writes, bias=None, scale=None, accum_out=None):
        kw = {}
        if bias is not None:
            kw['bias'] = bias
        if scale is not None:
            kw['scale'] = scale
        if accum_out is not None:
            kw['accum_out'] = accum_out
        self.op('act', lambda e: e.activation(out, in_, func, **kw), reads, writes)

    def ts(self, out, in0, s1, s2, op0, op1=None, reads=(), writes=(), eng='dve'):
        if op1 is None:
            self.op(eng, lambda e: e.tensor_scalar(out, in0, s1, None, op0), reads, writes)
        else:
            self.op(eng, lambda e: e.tensor_scalar(out, in0, s1, s2, op0, op1), reads, writes)

    def tt(self, out, in0, in1, op, reads, writes, eng='dve'):
        self.op(eng, lambda e: e.tensor_tensor(out, in0, in1, op), reads, writes)

    def stt(self, out, in0, scalar, in1, op0, op1, reads, writes):
        self.op('dve', lambda e: e.scalar_tensor_tensor(out, in0, scalar, in1, op0, op1), reads, writes)

    def cp(self, out, in_, reads, writes, eng='dve'):
        if eng == 'act':
            self.op('act', lambda e: e.activation(out, in_, AF.Copy), reads, writes)
        else:
            self.op(eng, lambda e: e.tensor_copy(out, in_), reads, writes)

    def memset(self, ap, val, writes, eng='dve'):
        self.op(eng, lambda e: e.memset(ap, val), (), writes)


def run(nc, in_maps):
    res = run_bass_kernel_spmd(nc, in_maps, core_ids=list(range(len(in_maps))))
    return res.results


def fm_vec(v):
    v = np.asarray(v)
    return np.ascontiguousarray(v.reshape(-1, 128).T)


class Dense:
    def __init__(self, kb, T, slab_w=256, slab_kc=16, nps=2):
        self.kb = kb
        self.T = T
        self.W = slab_w
        self.SK = slab_kc
        self.stage = [kb.sb([128, slab_kc, slab_w], F32) for _ in range(2)]
        self.wbf = [kb.sb([128, slab_kc, slab_w], BF16) for _ in range(2)]
        self.nsub = slab_w // 128
        self.psum = [[kb.ps([128, 512], F32) for _ in range(self.nsub)] for _ in range(nps)]
        self.nps = nps
        self.it = 0
        self.git = 0

    def run(self, w_ap, k0_rows, kc_n, n0, n_cols, act_fn, evac_fn, ntok=1):
        kb = self.kb
        T = self.T
        assert ntok == 1 or kc_n <= self.SK
        for c0 in range(0, n_cols, self.W):
            cw = min(self.W, n_cols - c0)
            nsub = (cw + 127) // 128
            gs = []
            for t in range(ntok):
                gs.append(self.git % self.nps)
                self.git += 1
            for s0 in range(0, kc_n, self.SK):
                sk = min(self.SK, kc_n - s0)
                b = self.it % 2
                self.it += 1
                src = w_ap[k0_rows + s0 * 128: k0_rows + (s0 + sk) * 128, n0 + c0: n0 + c0 + cw]
                src = src.rearrange("(kc p) n -> p kc n", p=128)
                kb.dma(self.stage[b][:, 0:sk, 0:cw], src, (), [('wst', id(self), b)])
                kb.cp(self.wbf[b][:, 0:sk, 0:cw], self.stage[b][:, 0:sk, 0:cw],
                      [('wst', id(self), b)], [('wbf', id(self), b)], eng='pool')
                for t in range(ntok):
                    g = gs[t]
                    for j in range(nsub):
                        m = min(128, cw - j * 128)
                        pk = ('dps', id(self), g, j)
                        for kc in range(sk):
                            a_ap, a_keys = act_fn(s0 + kc, t)
                            kb.mm(self.psum[g][j][0:m, 0:T], self.wbf[b][:, kc, j * 128: j * 128 + m], a_ap,
                                  start=(s0 + kc == 0), stop=(s0 + kc == kc_n - 1),
                                  reads=[('wbf', id(self), b)] + list(a_keys), writes=[pk])
                    if s0 + sk == kc_n:
                        for j in range(nsub):
                            m = min(128, cw - j * 128)
                            evac_fn(c0 + j * 128, m, self.psum[g][j][0:m, 0:T], ('dps', id(self), g, j), t)


def rstd_from_sumsq(kb, out, ps_ap, n, eps, reads, writes, tmp, tmpkey):
    kb.ts(tmp, ps_ap, 1.0 / n, eps, ALU.mult, ALU.add, reads=reads, writes=[tmpkey])
    kb.act(tmp, tmp, AF.Ln, [tmpkey], [tmpkey])
    kb.act(out, tmp, AF.Exp, [tmpkey], writes, scale=-0.5)


PREP_COLS = DEPTH * 6 * D // NCORES


def build_prep():
    kb = KB()
    cT = kb.dram("cT", [128, KC * B], F32, "ExternalInput")
    wada = kb.dram("wada", [D, PREP_COLS], F32, "ExternalInput")
    bada = kb.dram("bada", [B, PREP_COLS], F32, "ExternalInput")
    pos = kb.dram("pos", [64, S], I32, "ExternalInput")
    cst = kb.dram("cst", [64, 2], F32, "ExternalInput")
    mod = kb.dram("mod", [B, PREP_COLS], F32, "ExternalOutput")
    cc = kb.dram("cc", [64, S], F32, "ExternalOutput")
    ss = kb.dram("ss", [64, S], F32, "ExternalOutput")

    c_sb = kb.sb([128, KC * B], F32)
    b_sb = kb.sb([B, PREP_COLS], F32)
    o_sb = kb.sb([B, PREP_COLS], F32)
    kb.dma(c_sb[:], cT, (), ['c'])
    kb.dma(b_sb[:], bada, (), ['b'])
    kb.act(c_sb[:], c_sb[:], AF.Silu, ['c'], ['c'])
    wt = [kb.sb([128, KC, 512], F32) for _ in range(2)]
    pp = [kb.ps([B, 512], F32) for _ in range(2)]
    c3 = c_sb[:].rearrange("p (k b) -> p k b", b=B)
    for ci in range(PREP_COLS // 512):
        bi = ci % 2
        kb.dma(wt[bi][:], wada[:, ci * 512:(ci + 1) * 512].rearrange("(kc p) n -> p kc n", p=128),
               (), [('w', bi)])
        for k in range(KC):
            kb.mm(pp[bi][:], c3[:, k, :], wt[bi][:, k, :], start=(k == 0), stop=(k == KC - 1),
                  reads=['c', ('w', bi)], writes=[('pp', bi)])
        kb.tt(o_sb[:, ci * 512:(ci + 1) * 512], pp[bi][:], b_sb[:, ci * 512:(ci + 1) * 512], ALU.add,
              [('pp', bi), 'b'], ['o'])
    kb.dma(mod, o_sb[:], ['o'], ())

    CH = 2048
    cs = kb.sb([64, 2], F32)
    kb.dma(cs[:], cst, (), ['cs'])
    pi = kb.sb([64, CH], I32)
    ang = kb.sb([64, CH], F32)
    r = kb.sb([64, CH], F32)
    ki = kb.sb([64, CH], I32)
    kf = kb.sb([64, CH], F32)
    m = kb.sb([64, CH], F32)
    osb = {'s': kb.sb([64, CH], F32), 'c': kb.sb([64, CH], F32)}
    k_ = 'rope'
    for ci in range(S // CH):
        kb.dma(pi[:], pos[:, ci * CH:(ci + 1) * CH], (), [k_])
        kb.cp(ang[:], pi[:], [k_], [k_])
        kb.ts(ang[:], ang[:], cs[:, 0:1], None, ALU.mult, reads=[k_, 'cs'], writes=[k_])
        kb.ts(kf[:], ang[:], float(1.0 / (2 * np.pi)), None, ALU.mult, reads=[k_], writes=[k_])
        kb.cp(ki[:], kf[:], [k_], [k_])
        kb.cp(kf[:], ki[:], [k_], [k_])
        kb.stt(ang[:], kf[:], -6.28125, ang[:], ALU.mult, ALU.add, [k_], [k_])
        kb.stt(ang[:], kf[:], -0.0019353071795864769, ang[:], ALU.mult, ALU.add, [k_], [k_])
        for which, shift, dst in (('s', 0.0, ss), ('c', float(np.pi / 2), cc)):
            kb.ts(r[:], ang[:], shift, None, ALU.add, reads=[k_], writes=[k_])
            kb.ts(m[:], r[:], float(np.pi), None, ALU.is_gt, reads=[k_], writes=[k_])
            kb.stt(r[:], m[:], float(-2 * np.pi), r[:], ALU.mult, ALU.add, [k_], [k_])
            kb.ts(m[:], r[:], float(-np.pi), None, ALU.is_lt, reads=[k_], writes=[k_])
            kb.stt(r[:], m[:], float(2 * np.pi), r[:], ALU.mult, ALU.add, [k_], [k_])
            kb.ts(r[:], r[:], 3.1415925, -3.1415925, ALU.min, ALU.max, reads=[k_], writes=[k_])
            o = osb[which]
            kb.act(o[:], r[:], AF.Sin, [k_], [(k_, which)])
            if which == 's':
                kb.ts(o[:], o[:], cs[:, 1:2], None, ALU.mult, reads=[(k_, which), 'cs'], writes=[(k_, which)])
            kb.dma(dst[:, ci * CH:(ci + 1) * CH], o[:], [(k_, which)], ())
    return kb.finish()


TOK = B * S // NCORES
TT = 512


def modulate_tile(kb, xT, xkey, hT, hkey, acol, bcol, ones, scr, scrkey, ps_stat, pskey, rstd, rkey, xk=()):
    for k in range(KC):
        kb.act(scr[:, k, :], xT[:, k, :], AF.Square, [xkey], [scrkey])
    for k in range(KC):
        kb.mm(ps_stat[:], ones[:], scr[:, k, :], start=(k == 0), stop=(k == KC - 1),
              reads=[scrkey, 'ones'], writes=[pskey])
    rstd_from_sumsq(kb, rstd[:], ps_stat[:], float(D), EPS, [pskey], [rkey], scr[:, 0, :], scrkey)
    for k in range(KC):
        kb.stt(scr[:, k, :], xT[:, k, :], acol[:, k:k + 1], rstd[:], ALU.mult, ALU.mult,
               [xkey, rkey, 'acol'], [scrkey])
        kb.act(hT[:, k, :], scr[:, k, :], AF.Identity, [scrkey, 'acol'] + list(xk), [hkey], bias=bcol[:, k:k + 1])


def build_stageA():
    kb = KB()
    xT = kb.dram("xT", [D, TOK], F32, "ExternalInput")
    nw = kb.dram("nw", [128, KC], F32, "ExternalInput")
    sc = kb.dram("sc", [128, KC], F32, "ExternalInput")
    sh = kb.dram("sh", [128, KC], F32, "ExternalInput")
    w_in = kb.dram("w_in", [D, IN_W], F32, "ExternalInput")
    pT = kb.dram("pT", [IN_W, TOK], F32, "ExternalOutput")

    ones = kb.sb([128, 128], F32)
    kb.memset(ones[:], 1.0, ['ones'])
    nw_s = kb.sb([128, KC], F32)
    sc_s = kb.sb([128, KC], F32)
    sh_s = kb.sb([128, KC], F32)
    kb.dma(nw_s[:], nw, (), ['nw'])
    kb.dma(sc_s[:], sc, (), ['sc'])
    kb.dma(sh_s[:], sh, (), ['acol'])
    kb.ts(sc_s[:], sc_s[:], 1.0, None, ALU.add, reads=['sc'], writes=['sc'])
    kb.tt(sc_s[:], sc_s[:], nw_s[:], ALU.mult, ['sc', 'nw', 'acol'], ['acol'])

    TM = 256
    hT = kb.sb([128, KC, TOK], BF16)
    xt = [kb.sb([128, KC, TM], F32) for _ in range(2)]
    scr = kb.sb([128, KC, TM], F32)
    rstd = kb.sb([128, TM], F32)
    ps_stat = kb.ps([128, TM], F32)
    xv = xT.rearrange("(kc p) t -> p kc t", p=128)
    for t in range(TOK // TM):
        bi = t % 2
        kb.dma(xt[bi][:], xv[:, :, t * TM:(t + 1) * TM], (), [('x', bi)])
        modulate_tile(kb, xt[bi], ('x', bi), hT[:, :, t * TM:(t + 1) * TM], ('h', t // 2), sc_s, sh_s, ones,
                      scr, 'scr', ps_stat, 'pstat', rstd, 'rstd')

    NT = TOK // TT
    dn = Dense(kb, TT)
    ob = [kb.sb([128, TT], F32) for _ in range(4)]
    cnt = [0]

    def act_fn(kc, t):
        return hT[:, kc, t * TT:(t + 1) * TT], [('h', t)]

    def evac(c0, m, ps_ap, pk, t):
        i = cnt[0] % 4
        cnt[0] += 1
        kb.cp(ob[i][0:m, :], ps_ap, [pk], [('ob', i)], eng=('act' if i % 2 else 'dve'))
        kb.dma(pT[c0:c0 + m, t * TT:(t + 1) * TT], ob[i][0:m, :], [('ob', i)], ())
    dn.run(w_in, 0, KC, 0, IN_W, act_fn, evac, ntok=NT)
    return kb.finish()


MLA_SCALE = float(192 ** -0.5)


def norm_tile(kb, src, nkc, T, wcol, dst, ones, scr, ps_stat, rstd, key_in, key_out, n):
    for k in range(nkc):
        kb.act(scr[:, k, 0:T], src[:, k, 0:T], AF.Square, [key_in], ['nscr'])
    for k in range(nkc):
        kb.mm(ps_stat[:, 0:T], ones[:], scr[:, k, 0:T], start=(k == 0), stop=(k == nkc - 1),
              reads=['nscr', 'ones'], writes=['nps'])
    rstd_from_sumsq(kb, rstd[:, 0:T], ps_stat[:, 0:T], float(n), EPS, ['nps'], ['nrstd'], scr[:, 0, 0:T], 'nscr')
    for k in range(nkc):
        kb.stt(dst[:, k, 0:T], src[:, k, 0:T], wcol[:, k:k + 1], rstd[:, 0:T], ALU.mult, ALU.mult,
               [key_in, 'nrstd', 'nw'], [key_out])


def load_cast(kb, dst_bf, src_ap, stage, nkc, ncol, key):
    kb.dma(stage[:, 0:nkc, 0:ncol], src_ap.rearrange("(kc p) n -> p kc n", p=128), (), ['wstage'])
    kb.cp(dst_bf[:, 0:nkc, 0:ncol], stage[:, 0:nkc, 0:ncol], ['wstage'], [key])


def mla_inputs(kb, sfx='', shared=None):
    d = {}
    d['cc'] = kb.dram("cc" + sfx, [64, S], F32, "ExternalInput")
    d['ss'] = kb.dram("ss" + sfx, [64, S], F32, "ExternalInput")
    d['qnw'] = kb.dram("qnw" + sfx, [128, 6], F32, "ExternalInput")
    d['kvnw'] = kb.dram("kvnw" + sfx, [128, 4], F32, "ExternalInput")
    d['wq_n'] = kb.dram("wq_n" + sfx, [768, 256], F32, "ExternalInput")
    d['wq_r'] = kb.dram("wq_r" + sfx, [768, 128], F32, "ExternalInput")
    d['wq_rs'] = kb.dram("wq_rs" + sfx, [768, 128], F32, "ExternalInput")
    d['wk'] = kb.dram("wk" + sfx, [512, 256], F32, "ExternalInput")
    d['wv'] = kb.dram("wv" + sfx, [512, 256], F32, "ExternalInput")
    d['mask'] = kb.dram("mask" + sfx, [128, 2048], F32, "ExternalInput")
    if shared:
        d.update(shared)
    return d


def emit_mla(kb, pint, d, oT):
    TP = 512
    NTILE = S // TP
    cqT = pint[0:768, 3:3 + S]
    ckvT = pint[768:1280, 3:3 + S]
    krT = pint[1280:1344, 3:3 + S]
    CCd, SSd, qnw, kvnw = d['cc'], d['ss'], d['qnw'], d['kvnw']
    wq_n, wq_r, wq_rs, wk, wv, maskd = d['wq_n'], d['wq_r'], d['wq_rs'], d['wk'], d['wv'], d['mask']

    ones = kb.sb([128, 128], F32)
    kb.memset(ones[:], 1.0, ['ones'])
    ones_b = kb.sb([128, 128], BF16)
    kb.memset(ones_b[:], 1.0, ['ones_b'])
    qnw_s = kb.sb([128, 6], F32)
    kvnw_s = kb.sb([128, 4], F32)
    kb.dma(qnw_s[:], qnw, (), ['nw'])
    kb.dma(kvnw_s[:], kvnw, (), ['nw'])
    mstage = kb.sb([128, 2048], F32)
    mask_b = kb.sb([128, 2048], BF16)
    kb.dma(mstage[:], maskd, (), ['mstage'])
    kb.cp(mask_b[:], mstage[:], ['mstage'], ['mask'])

    QTn = kb.sb([128, S], BF16)
    QTr = kb.sb([64, S], BF16)
    KTn = kb.sb([128, S], BF16)
    KTr = kb.sb([64, S], BF16)
    V = kb.sb([128, S // 128, 128], BF16)
    src = [kb.sb([128, 6, TP], F32) for _ in range(2)]
    scr = kb.sb([128, 6, TP], F32)
    cn = kb.sb([128, 6, TP], BF16)
    rstd = kb.sb([128, TP], F32)
    wstage = kb.sb([128, 6, 128], F32)
    wqn_b = kb.sb([128, 6, 128], BF16)
    wqr_b = kb.sb([128, 6, 64], BF16)
    wqs_b = kb.sb([128, 6, 64], BF16)
    wk_b = kb.sb([128, 4, 128], BF16)
    wv_b = kb.sb([128, 4, 128], BF16)
    cc_s = [kb.sb([64, TP], F32) for _ in range(2)]
    ss_s = [kb.sb([64, TP], F32) for _ in range(2)]
    kr_s = [kb.sb([64, TP], F32) for _ in range(2)]
    krs_s = [kb.sb([64, TP], F32) for _ in range(2)]
    t1 = kb.sb([64, TP], F32)
    t2 = kb.sb([64, TP], F32)
    pT = [kb.sb([128, 512], BF16) for _ in range(3)]
    rec = kb.sb([128, 512], F32)
    osb = [kb.sb([128, 512], F32) for _ in range(2)]
    ps = [kb.ps([128, 512], F32) for _ in range(8)]

    def rope_combine(dst, a_ap, b_ap, bi, reads, wkey):
        kb.tt(t1[:], a_ap, cc_s[bi][:], ALU.mult, reads + [('cs', bi)], ['t1'])
        kb.tt(t2[:], b_ap, ss_s[bi][:], ALU.mult, reads + [('cs', bi)], ['t2'])
        kb.tt(dst, t1[:], t2[:], ALU.add, ['t1', 't2'], [wkey])

    for t in range(NTILE):
        bi = t % 2
        sl = slice(t * TP, (t + 1) * TP)
        kb.dma(cc_s[bi][:], CCd[:, sl], (), [('cs', bi)])
        kb.dma(ss_s[bi][:], SSd[:, sl], (), [('cs', bi)])
        kb.dma(kr_s[bi][:], krT[:, sl], (), [('kr', bi)])
        kb.dma(krs_s[bi][0:32, :], krT[32:64, sl], (), [('kr', bi)])
        kb.dma(krs_s[bi][32:64, :], krT[0:32, sl], (), [('kr', bi)])
        rope_combine(KTr[:, sl], kr_s[bi][:], krs_s[bi][:], bi, [('kr', bi)], ('KTr', t))

    cqv = cqT.rearrange("(kc p) t -> p kc t", p=128)
    ckvv = ckvT.rearrange("(kc p) t -> p kc t", p=128)
    for h in range(2):
        load_cast(kb, wqn_b, wq_n[:, h * 128:(h + 1) * 128], wstage, 6, 128, 'wqn')
        load_cast(kb, wqr_b, wq_r[:, h * 64:(h + 1) * 64], wstage, 6, 64, 'wqr')
        load_cast(kb, wqs_b, wq_rs[:, h * 64:(h + 1) * 64], wstage, 6, 64, 'wqs')
        load_cast(kb, wk_b, wk[:, h * 128:(h + 1) * 128], wstage, 4, 128, 'wk')
        load_cast(kb, wv_b, wv[:, h * 128:(h + 1) * 128], wstage, 4, 128, 'wv')
        for t in range(NTILE):
            bi = t % 2
            sl = slice(t * TP, (t + 1) * TP)
            kb.dma(src[bi][:], cqv[:, :, sl], (), [('src', bi)])
            kb.dma(cc_s[bi][:], CCd[:, sl], (), [('cs', bi)])
            kb.dma(ss_s[bi][:], SSd[:, sl], (), [('cs', bi)])
            norm_tile(kb, src[bi], 6, TP, qnw_s, cn, ones, scr, ps[0], rstd, ('src', bi), 'cn', 768)
            for k in range(6):
                kb.mm(ps[1][:], wqn_b[:, k, :], cn[:, k, :], start=(k == 0), stop=(k == 5),
                      reads=['wqn', 'cn'], writes=['ps1'])
            kb.cp(QTn[:, sl], ps[1][:], ['ps1'], [('QTn', t)], eng='act')
            for k in range(6):
                kb.mm(ps[2][0:64, :], wqr_b[:, k, :], cn[:, k, :], start=(k == 0), stop=(k == 5),
                      reads=['wqr', 'cn'], writes=['ps2'])
            for k in range(6):
                kb.mm(ps[3][0:64, :], wqs_b[:, k, :], cn[:, k, :], start=(k == 0), stop=(k == 5),
                      reads=['wqs', 'cn'], writes=['ps3'])
            rope_combine(QTr[:, sl], ps[2][0:64, :], ps[3][0:64, :], bi, ['ps2', 'ps3'], ('QTr', t))
        for t in range(NTILE):
            bi = t % 2
            sl = slice(t * TP, (t + 1) * TP)
            kb.dma(src[bi][:, 0:4, :], ckvv[:, :, sl], (), [('src', bi)])
            norm_tile(kb, src[bi], 4, TP, kvnw_s, cn, ones, scr, ps[0], rstd, ('src', bi), 'cn', 512)
            for k in range(4):
                kb.mm(ps[1][:], wk_b[:, k, :], cn[:, k, :], start=(k == 0), stop=(k == 3),
                      reads=['wk', 'cn'], writes=['ps1'])
            kb.cp(KTn[:, sl], ps[1][:], ['ps1'], [('KTn', t)], eng='act')
            for blk in range(TP // 128):
                for k in range(4):
                    kb.mm(ps[2][:, blk * 128:(blk + 1) * 128], cn[:, k, blk * 128:(blk + 1) * 128], wv_b[:, k, :],
                          start=(k == 0), stop=(k == 3), reads=['wv', 'cn'], writes=['ps2'])
            kb.cp(V[:, t * 4:(t + 1) * 4, :], ps[2][:].rearrange("p (a b) -> p a b", b=128),
                  ['ps2'], [('V', t)])
        it = 0
        for I in range(S // 512):
            qsl = slice(I * 512, (I + 1) * 512)
            nj = 4 * I + 4
            for j in range(nj):
                ksl = slice(j * 128, (j + 1) * 128)
                sp_ = ps[4 + (it % 2)]
                spk = ('sps', it % 2)
                pb = it % 3
                it += 1
                kb.mm(sp_[:], KTn[:, ksl], QTn[:, qsl], start=True, stop=False,
                      reads=[('KTn', j // 4), ('QTn', I)], writes=[spk])
                kb.mm(sp_[:], KTr[:, ksl], QTr[:, qsl], start=False, stop=True,
                      reads=[('KTr', j // 4), ('QTr', I)], writes=[spk])
                kb.act(pT[pb][:], sp_[:], AF.Exp, [spk], [('pT', pb)], scale=MLA_SCALE)
                jj = j - 4 * I
                if jj >= 0:
                    kb.tt(pT[pb][:], pT[pb][:], mask_b[:, jj * 512:(jj + 1) * 512], ALU.mult,
                          [('pT', pb), 'mask'], [('pT', pb)])
                kb.mm(ps[6][:], V[:, j, :], pT[pb][:], start=(j == 0), stop=(j == nj - 1),
                      reads=[('V', j // 4), ('pT', pb)], writes=['ops'])
                kb.mm(ps[7][:], ones_b[:], pT[pb][:], start=(j == 0), stop=(j == nj - 1),
                      reads=['ones_b', ('pT', pb)], writes=['sums'])
            kb.op('dve', lambda e: e.reciprocal(rec[:], ps[7][:]), ['sums'], ['rec'])
            ob = osb[I % 2]
            kb.tt(ob[:], ps[6][:], rec[:], ALU.mult, ['ops', 'rec'], [('ob', I % 2)])
            kb.dma(oT[h * 128:(h + 1) * 128, qsl], ob[:], [('ob', I % 2)], ())


def causal_masks():
    k = np.arange(128)[:, None]
    q = np.arange(512)[None, :]
    return np.concatenate([(q >= k + 128 * jj).astype(np.float32) for jj in range(4)], axis=1)


CH = 64
NG = 4


def delta_psum(kb, nbanks=2):
    return [kb.ps([128, 512], F32) for _ in range(nbanks)]


class Delta:
    def __init__(self, kb, dk, dv, merged, ident, ident_key, identrep, psum=None):
        self.kb, self.dk, self.dv, self.merged = kb, dk, dv, merged
        self.ident, self.ident_key, self.identrep = ident, ident_key, identrep
        if psum is None:
            psum = delta_psum(kb)
        self.banks = psum
        self.bi = 0
        f3 = lambda a, b_, dt: kb.sb([a, NG, b_], dt)
        self.ApT = f3(CH, CH, F32)
        self.RpT_b = f3(CH, CH, BF16)
        self.AkT_b = f3(CH, CH, BF16)
        self.RkT_b = f3(CH, CH, BF16)
        self.P = [f3(CH, CH, F32) for _ in range(2)]
        self.Q = [f3(CH, CH, F32) for _ in range(2)]
        self.Y = f3(CH, CH, F32)
        self.Yb = f3(CH, CH, BF16)
        self.Wq_b = f3(CH, dk, BF16)
        self.AV_b = f3(CH, dv, BF16)
        self.Ul_b = f3(CH, dv, BF16)
        self.RtT_b = f3(dk, CH, BF16)
        self.N = f3(dk, dv, F32)
        self.MT = f3(dk, dk, F32)
        self.uid = 0

    def nb(self):
        i = self.bi
        self.bi = (i + 1) % len(self.banks)
        return self.banks[i], ('dbank', id(self.banks[i]))

    def new_state(self):
        kb = self.kb
        H = kb.sb([self.dk, self.dv], F32)
        Hb = kb.sb([self.dk, self.dv], BF16)
        self.uid += 1
        key = ('H', id(self), self.uid)
        kb.memset(H[:], 0.0, [key])
        kb.memset(Hb[:], 0.0, [(key, 'b')])
        return (H, Hb, key)

    def group(self, st, PTs, KTs, QR, RbT, Qb, Ph, Kh, V, FA, FR, Gam, yT_out, rkeys, ykey):
        kb, dk, dv, mg = self.kb, self.dk, self.dv, self.merged
        H, Hb, hkey = st
        me = id(self)
        K_ = lambda n: (n, me)
        rk = list(rkeys)
        v3 = lambda bank, rows, w: bank[0:rows, 0:NG * w].rearrange("p (g w) -> p g w", w=w)
        b0, k0 = self.nb()
        for c in range(NG):
            kb.mm(b0[0:CH, c * 128:(c + 1) * 128], PTs[:, c * CH:(c + 1) * CH], QR[:, c, :], True, True, rk, [k0])
        s0 = v3(b0, CH, 128)
        kb.tt(self.ApT[:], s0[:, :, 0:CH], FA, ALU.mult, [k0] + rk, [K_('ApT')])
        kb.tt(self.RpT_b[:], s0[:, :, CH:128], FR, ALU.mult, [k0] + rk, [K_('RpT')])
        if not mg:
            b1, k1 = self.nb()
            for c in range(NG):
                kb.mm(b1[0:CH, c * 128:(c + 1) * 128], KTs[:, c * CH:(c + 1) * CH], QR[:, c, :], True, True, rk, [k1])
            s1 = v3(b1, CH, 128)
            kb.tt(self.AkT_b[:], s1[:, :, 0:CH], FA, ALU.mult, [k1] + rk, [K_('AkT')])
            kb.tt(self.RkT_b[:], s1[:, :, CH:128], FR, ALU.mult, [k1] + rk, [K_('RkT')])
        b2, k2 = self.nb()
        for c in range(NG):
            kb.tr(b2[0:CH, c * CH:(c + 1) * CH], self.ApT[:, c, :], self.ident[0:CH, 0:CH],
                  [K_('ApT'), self.ident_key], [k2])
        kb.cp(self.P[0][:], v3(b2, CH, CH), [k2], [K_('P0')], eng='act')
        kb.tt(self.Y[:], self.identrep, self.ApT[:], ALU.subtract, [K_('ApT'), self.ident_key], [K_('Y')])
        Pc, Pk = self.P[0], K_('P0')
        Qc, Qk = self.ApT, K_('ApT')
        for lev in range(1, 6):
            Pn, Pnk = self.P[lev % 2], K_('P%d' % (lev % 2))
            Qn, Qnk = self.Q[lev % 2], K_('Q%d' % (lev % 2))
            bp, kp = self.nb()
            for c in range(NG):
                kb.mm(bp[0:CH, c * CH:(c + 1) * CH], Qc[:, c, :], Pc[:, c, :], True, True, [Qk, Pk], [kp])
            if lev < 5:
                bq, kq = self.nb()
                for c in range(NG):
                    kb.mm(bq[0:CH, c * CH:(c + 1) * CH], Pc[:, c, :], Qc[:, c, :], True, True, [Qk, Pk], [kq])
            kb.cp(Pn[:], v3(bp, CH, CH), [kp], [Pnk], eng='act')
            if lev < 5:
                kb.cp(Qn[:], v3(bq, CH, CH), [kq], [Qnk], eng='dve')
            by, ky = self.nb()
            for c in range(NG):
                kb.mm(by[0:CH, c * CH:(c + 1) * CH], Pn[:, c, :], self.Y[:, c, :], True, True, [Pnk, K_('Y')], [ky])
            kb.tt(self.Y[:], self.Y[:], v3(by, CH, CH), ALU.add, [ky, K_('Y')], [K_('Y')])
            Pc, Pk, Qc, Qk = Pn, Pnk, Qn, Qnk
        kb.cp(self.Yb[:], self.Y[:], [K_('Y')], [K_('Yb')], eng='act')
        bw, kw = self.nb()
        for c in range(NG):
            kb.mm(bw[0:CH, c * dk:(c + 1) * dk], self.Yb[:, c, :], Qb[:, c, :], True, True, [K_('Yb')] + rk, [kw])
        kb.cp(self.Wq_b[:], v3(bw, CH, dk), [kw], [K_('Wq')], eng='act')
        if mg:
            bu, ku = self.nb()
            for c in range(NG):
                kb.mm(bu[0:CH, c * dv:(c + 1) * dv], self.Yb[:, c, :], V[:, c, :], True, True, [K_('Yb')] + rk, [ku])
            kb.cp(self.Ul_b[:], v3(bu, CH, dv), [ku], [K_('Ul')], eng='dve')
        else:
            ba, ka = self.nb()
            for c in range(NG):
                kb.mm(ba[0:CH, c * dv:(c + 1) * dv], self.AkT_b[:, c, :], V[:, c, :], True, True, [K_('AkT')] + rk, [ka])
            kb.cp(self.AV_b[:], v3(ba, CH, dv), [ka], [K_('AV')], eng='dve')
            bu, ku = self.nb()
            for c in range(NG):
                kb.mm(bu[0:CH, c * dv:(c + 1) * dv], self.Yb[:, c, :], self.AV_b[:, c, :], True, True,
                      [K_('Yb'), K_('AV')], [ku])
            kb.ts(self.Ul_b[:], v3(bu, CH, dv), -1.0, None, ALU.mult, reads=[ku], writes=[K_('Ul')])
        br, kr_ = self.nb()
        for c in range(NG):
            kb.mm(br[0:dk, c * CH:(c + 1) * CH], self.Wq_b[:, c, :], self.RpT_b[:, c, :], True, True,
                  [K_('Wq'), K_('RpT')], [kr_])
        kb.tt(self.RtT_b[:], RbT, v3(br, dk, CH), ALU.subtract, [kr_] + rk, [K_('RtT')])
        bn, kn = self.nb()
        for c in range(NG):
            kb.mm(bn[0:dk, c * dv:(c + 1) * dv], Ph[:, c, :], self.Ul_b[:, c, :], True, mg, [K_('Ul')] + rk, [kn])
            if not mg:
                kb.mm(bn[0:dk, c * dv:(c + 1) * dv], Kh[:, c, :], V[:, c, :], False, True, rk, [kn])
        kb.cp(self.N[:], v3(bn, dk, dv), [kn], [K_('N')], eng='act')
        bm, km = self.nb()
        for c in range(NG):
            kb.mm(bm[0:dk, c * dk:(c + 1) * dk], self.Wq_b[:, c, :], Ph[:, c, :], True, True, [K_('Wq')] + rk, [km])
        for c in range(NG):
            kb.stt(self.MT[:, c, :], self.ident[0:dk, 0:dk], Gam[:, c:c + 1], bm[0:dk, c * dk:(c + 1) * dk],
                   ALU.mult, ALU.subtract, [km, self.ident_key] + rk, [K_('MT')])
        for c in range(NG):
            byy, kyy = self.nb()
            o_ = byy[0:dv, 0:CH]
            kb.mm(o_, self.Ul_b[:, c, :], self.RpT_b[:, c, :], True, False, [K_('Ul'), K_('RpT')], [kyy])
            if not mg:
                kb.mm(o_, V[:, c, :], self.RkT_b[:, c, :], False, False, [K_('RkT')] + rk, [kyy])
            kb.mm(o_, Hb[:], self.RtT_b[:, c, :], False, True, [(hkey, 'b'), K_('RtT')], [kyy])
            bh, kh = self.nb()
            kb.mm(bh[0:dk, 0:dv], self.MT[:, c, :], H[:], True, True, [K_('MT'), hkey], [kh])
            kb.tt(H[:], bh[0:dk, 0:dv], self.N[:, c, :], ALU.add, [kh, K_('N')], [hkey])
            kb.cp(Hb[:], H[:], [hkey], [(hkey, 'b')], eng='act')
            kb.cp(yT_out[:, c * CH:(c + 1) * CH], o_, [kyy], [ykey], eng='dve')


def delta_consts():
    s = np.arange(64)[:, None]
    t = np.arange(64)[None, :]
    c = {}
    c['ident'] = np.eye(128, dtype=np.float32)
    c['identrep'] = np.ascontiguousarray(np.tile(np.eye(64, dtype=np.float32)[:, None, :], (1, NG, 1)).reshape(64, NG * 64))
    c['maskA'] = np.ascontiguousarray(np.tile((s < t).astype(np.float32)[:, None, :], (1, NG, 1)).reshape(64, NG * 64))
    c['maskR'] = np.ascontiguousarray(np.tile((s <= t).astype(np.float32)[:, None, :], (1, NG, 1)).reshape(64, NG * 64))
    rm = np.ones((128, NG * 64), np.float32)
    rm[:, ::64] = 0.0
    c['rmask'] = rm
    sel = np.zeros((64, 64), np.float32)
    sel[63, :] = 1.0
    c['sel63'] = sel
    return c


DCONST_SHAPES = (('ident', [128, 128]), ('identrep', [64, NG * CH]), ('maskA', [64, NG * CH]),
                 ('maskR', [64, NG * CH]), ('rmask', [128, NG * CH]), ('sel63', [64, 64]))


def delta_const_inputs(kb):
    return {nm: kb.dram(nm, shp, F32, "ExternalInput") for nm, shp in DCONST_SHAPES}


class DeltaConsts:
    def __init__(self, kb, drams):
        d = {}
        for nm, shp in DCONST_SHAPES:
            t = kb.sb(shp, F32)
            kb.dma(t[:], drams[nm], (), ['dconst'])
            d[nm] = t
        self.ident = d['ident']
        self.identrep3 = d['identrep'][:].rearrange("p (g w) -> p g w", w=CH)
        self.maskA3 = d['maskA'][:].rearrange("p (g w) -> p g w", w=CH)
        self.maskR3 = d['maskR'][:].rearrange("p (g w) -> p g w", w=CH)
        self.rmask = d['rmask']
        self.sel63 = d['sel63']
        self.maskR = d['maskR']
        self.ident_b = kb.sb([128, 128], BF16)
        kb.cp(self.ident_b[:], self.ident[:], ['dconst'], ['dconst_b'])
        self.ones = kb.sb([128, 128], F32)
        kb.memset(self.ones[:], 1.0, ['dones'])


def transpose_group(kb, dl, dst3, src, rows, reads, wkey, scale_cols=None):
    bt, kT = dl.nb()
    for c in range(NG):
        kb.tr(bt[0:CH, c * rows:(c + 1) * rows], src[:, c * CH:(c + 1) * CH], dl.cst.ident[0:rows, 0:rows],
              list(reads) + ['dconst'], [kT])
    if scale_cols is None:
        kb.cp(dst3, bt[0:CH, 0:NG * rows].rearrange("p (g w) -> p g w", w=rows), [kT], [wkey], eng='act')
    else:
        for c in range(NG):
            kb.ts(dst3[:, c, :], bt[0:CH, c * rows:(c + 1) * rows], scale_cols[:, c:c + 1], None, ALU.mult,
                  reads=[kT] + list(reads), writes=[wkey])


RW_DEC = float(np.exp(-0.5))


def rwkv_inputs(kb, sfx=''):
    d = {}
    d['cols'] = kb.dram("rw_cols" + sfx, [64, 40], F32, "ExternalInput")
    d['mulow'] = kb.dram("rw_mulow" + sfx, [128, 3], F32, "ExternalInput")
    d['w_up'] = kb.dram("rw_w_up" + sfx, [64, 256], F32, "ExternalInput")
    d['a_up'] = kb.dram("rw_a_up" + sfx, [64, 256], F32, "ExternalInput")
    d['g_up'] = kb.dram("rw_g_up" + sfx, [128, 256], F32, "ExternalInput")
    return d


def emit_rwkv(kb, rkvT, lowT, d, ocT, cst, heads=(0, 1, 2, 3), psum=None):
    GW = NG * CH
    NGRP = S // GW
    colsd, mulow, wupd, aupd, gupd = d['cols'], d['mulow'], d['w_up'], d['a_up'], d['g_up']
    dl = Delta(kb, 64, 64, False, cst.ident, 'dconst', cst.identrep3, psum)
    dl.cst = cst
    cols = kb.sb([64, 40], F32)
    mul = kb.sb([128, 3], F32)
    kb.dma(cols[:], colsd, (), ['cols'])
    kb.dma(mul[:], mulow, (), ['cols'])
    wst = kb.sb([128, 256], F32)
    wup_b = kb.sb([64, 256], BF16)
    aup_b = kb.sb([64, 256], BF16)
    gup_b = kb.sb([128, 256], BF16)
    for (dr, dst, rows) in ((wupd, wup_b, 64), (aupd, aup_b, 64), (gupd, gup_b, 128)):
        kb.dma(wst[0:rows, :], dr, (), ['wst'])
        kb.cp(dst[:], wst[0:rows, :], ['wst'], ['wlow'])
    col = lambda j, h: cols[:, 12 + j * 4 + h: 12 + j * 4 + h + 1]

    f = lambda r, w=GW, dt=F32: kb.sb([r, w], dt)
    lw_x = f(64, GW + 1); la_x = f(64, GW + 1); lg_x = f(128, GW + 1)
    tanh_b = f(64, GW, BF16); xa_b = f(64, GW, BF16); sg_b = f(128, GW, BF16)
    tmp = f(128); tmp2 = f(64)
    X3 = [kb.sb([64, GW + 1], F32) for _ in range(3)]
    r_s = f(64); k_s = f(64); v_s = f(64)
    lgw = f(64); a_s = f(64); g_s = f(64); kk = f(64); kp = f(64); bb = f(64)
    cum = f(64); ex = f(64); cumC = kb.sb([64, NG], F32); gam = kb.sb([64, NG], F32)
    PTs = f(64, GW, BF16); KTs = f(64, GW, BF16); QR = kb.sb([64, NG, 128], BF16)
    PhT = f(64); KhT = f(64); qbT = f(64)
    Qb = kb.sb([64, NG, 64], BF16); Ph = kb.sb([64, NG, 64], BF16); Kh = kb.sb([64, NG, 64], BF16)
    Vt = kb.sb([64, NG, 64], BF16)
    yT = f(64); yc = f(64); sq = f(64); rs = f(64); outb = [f(64) for _ in range(2)]
    states = {h: dl.new_state() for h in heads}
    v3 = lambda t: t[:].rearrange("p (g w) -> p g w", w=CH)

    def shift(dst, X, mucol, rows, rkey, wkey):
        kb.tt(tmp[0:rows, :], X[0:rows, 0:GW], X[0:rows, 1:GW + 1], ALU.subtract, [rkey], ['tmp'])
        kb.stt(dst, tmp[0:rows, :], mucol, X[0:rows, 1:GW + 1], ALU.mult, ALU.add, ['tmp', rkey, 'cols'], [wkey])

    def rsq(out, ps_ap, eps, reads, wkey, scale=1.0):
        kb.ts(tmp2[:], ps_ap, scale, eps, ALU.mult, ALU.add, reads=reads, writes=['tmp2'])
        kb.act(tmp2[:], tmp2[:], AF.Ln, ['tmp2'], ['tmp2'])
        kb.act(out, tmp2[:], AF.Exp, ['tmp2'], [wkey], scale=-0.5)

    for gi in range(NGRP):
        c0 = gi * GW
        kb.dma(lw_x[:], lowT[0:64, c0:c0 + GW + 1], (), ['lw_x'])
        kb.dma(la_x[:], lowT[64:128, c0:c0 + GW + 1], (), ['la_x'])
        kb.dma(lg_x[:], lowT[128:256, c0:c0 + GW + 1], (), ['lg_x'])
        shift(tmp2[:], lw_x, mul[0:64, 0:1], 64, 'lw_x', 'tmp2')
        kb.act(tanh_b[:], tmp2[:], AF.Tanh, ['tmp2'], ['tanh_b'])
        shift(xa_b[:], la_x, mul[0:64, 1:2], 64, 'la_x', 'xa_b')
        shift(tmp[:], lg_x, mul[:, 2:3], 128, 'lg_x', 'tmp')
        kb.act(sg_b[:], tmp[:], AF.Sigmoid, ['tmp'], ['sg_b'])
        for h in heads:
            hs = slice(h * 64, (h + 1) * 64)
            for part in range(3):
                r0 = (part * 4 + h) * 64
                kb.dma(X3[part][:], rkvT[r0:r0 + 64, c0:c0 + GW + 1], (), [('X3', part)])
            for part, dst, nm in ((0, r_s, 'r_s'), (1, k_s, 'k_s'), (2, v_s, 'v_s')):
                shift(dst[:], X3[part], cols[:, part * 4 + h: part * 4 + h + 1], 64, ('X3', part), nm)
            psA, kA = dl.nb()
            kb.mm(psA[0:64, 0:GW], wup_b[:, hs], tanh_b[:], True, True, ['wlow', 'tanh_b'], [kA])
            kb.act(lgw[:], psA[0:64, 0:GW], AF.Sigmoid, [kA, 'cols'], ['lgw'], bias=col(0, h))
            kb.ts(lgw[:], lgw[:], -RW_DEC, None, ALU.mult, reads=['lgw'], writes=['lgw'])
            psA, kA = dl.nb()
            kb.mm(psA[0:64, 0:GW], aup_b[:, hs], xa_b[:], True, True, ['wlow', 'xa_b'], [kA])
            kb.act(a_s[:], psA[0:64, 0:GW], AF.Sigmoid, [kA, 'cols'], ['a_s'], bias=col(1, h))
            psA, kA = dl.nb()
            kb.mm(psA[0:64, 0:GW], gup_b[:, hs], sg_b[:], True, True, ['wlow', 'sg_b'], [kA])
            kb.cp(g_s[:], psA[0:64, 0:GW], [kA], ['g_s'], eng='act')
            kb.ts(kk[:], k_s[:], col(2, h), None, ALU.mult, reads=['k_s', 'cols'], writes=['kk'])
            kb.tt(sq[:], kk[:], kk[:], ALU.mult, ['kk'], ['sq'])
            psA, kA = dl.nb()
            kb.mm(psA[0:64, 0:GW], cst.ones[0:64, 0:64], sq[:], True, True, ['dones', 'sq'], [kA])
            rsq(rs[:], psA[0:64, 0:GW], 1e-6, [kA], 'rs')
            kb.tt(kk[:], kk[:], rs[:], ALU.mult, ['kk', 'rs'], ['kk'])
            kb.ts(kp[:], a_s[:], -1.0, col(3, h), ALU.add, ALU.mult, reads=['a_s', 'cols'], writes=['kp'])
            kb.stt(kp[:], kp[:], 1.0, k_s[:], ALU.add, ALU.mult, ['kp', 'k_s'], ['kp'])
            kb.tt(bb[:], kk[:], a_s[:], ALU.mult, ['kk', 'a_s'], ['bb'])
            kb.op('dve', lambda e: e.tensor_tensor_scan(cum[:], cst.rmask[0:64, :], lgw[:], 0.0, ALU.mult, ALU.add),
                  ['lgw', 'dconst'], ['cum'])
            kb.cp(cumC[:], v3(cum)[:, :, CH - 1], ['cum'], ['cumC'])
            kb.act(gam[:], cumC[:], AF.Exp, ['cumC'], ['gam'])
            kb.act(ex[:], cum[:], AF.Exp, ['cum'], ['ex'])
            kb.tt(QR[:, :, CH:128], v3(r_s), v3(ex), ALU.mult, ['r_s', 'ex'], ['QR'])
            kb.tt(tmp2[:], cum[:], lgw[:], ALU.subtract, ['cum', 'lgw'], ['tmp2'])
            kb.act(ex[:], tmp2[:], AF.Exp, ['tmp2'], ['ex'])
            kb.tt(qbT[:], kk[:], ex[:], ALU.mult, ['kk', 'ex'], ['qbT'])
            kb.cp(QR[:, :, 0:CH], v3(qbT), ['qbT'], ['QR'])
            kb.act(ex[:], cum[:], AF.Exp, ['cum'], ['ex'], scale=-1.0)
            kb.tt(PTs[:], bb[:], ex[:], ALU.mult, ['bb', 'ex'], ['PTs'])
            kb.tt(KTs[:], kp[:], ex[:], ALU.mult, ['kp', 'ex'], ['KTs'])
            for c in range(NG):
                kb.ts(tmp2[:, c * CH:(c + 1) * CH], cum[:, c * CH:(c + 1) * CH], cumC[:, c:c + 1], None, ALU.subtract,
                      reads=['cum', 'cumC'], writes=['tmp2'])
            kb.act(ex[:], tmp2[:], AF.Exp, ['tmp2'], ['ex'], scale=-1.0)
            kb.tt(PhT[:], bb[:], ex[:], ALU.mult, ['bb', 'ex'], ['PhT'])
            kb.tt(KhT[:], kp[:], ex[:], ALU.mult, ['kp', 'ex'], ['KhT'])
            transpose_group(kb, dl, Qb[:], qbT, 64, ['qbT'], 'Qb')
            transpose_group(kb, dl, Ph[:], PhT, 64, ['PhT'], 'Ph')
            transpose_group(kb, dl, Kh[:], KhT, 64, ['KhT'], 'Kh')
            transpose_group(kb, dl, Vt[:], v_s, 64, ['v_s'], 'Vt')
            dl.group(states[h], PTs[:], KTs[:], QR, QR[:, :, CH:128], Qb, Ph, Kh, Vt, cst.maskA3, cst.maskR3, gam,
                     yT[:], ['PTs', 'KTs', 'QR', 'Qb', 'Ph', 'Kh', 'Vt', 'gam', 'dconst'], 'yT')
            psA, kA = dl.nb()
            kb.mm(psA[0:64, 0:GW], cst.ones[0:64, 0:64], yT[:], True, True, ['dones', 'yT'], [kA])
            kb.stt(yc[:], psA[0:64, 0:GW], -1.0 / 64, yT[:], ALU.mult, ALU.add, [kA, 'yT'], ['yc'])
            kb.tt(sq[:], yc[:], yc[:], ALU.mult, ['yc'], ['sq'])
            psA, kA = dl.nb()
            kb.mm(psA[0:64, 0:GW], cst.ones[0:64, 0:64], sq[:], True, True, ['dones', 'sq'], [kA])
            rsq(rs[:], psA[0:64, 0:GW], 64e-5, [kA], 'rs', scale=1.0 / 64)
            kb.tt(yc[:], yc[:], rs[:], ALU.mult, ['yc', 'rs'], ['yc'])
            kb.ts(yc[:], yc[:], col(5, h), col(6, h), ALU.mult, ALU.add, reads=['yc', 'cols'], writes=['yc'])
            kb.stt(sq[:], r_s[:], col(4, h), kp[:], ALU.mult, ALU.mult, ['r_s', 'kp', 'cols'], ['sq'])
            psA, kA = dl.nb()
            kb.mm(psA[0:64, 0:GW], cst.ones[0:64, 0:64], sq[:], True, True, ['dones', 'sq'], [kA])
            kb.tt(sq[:], psA[0:64, 0:GW], v_s[:], ALU.mult, [kA, 'v_s'], ['sq'])
            kb.tt(yc[:], yc[:], sq[:], ALU.add, ['yc', 'sq'], ['yc'])
            ob = outb[(gi * 4 + h) % 2]
            okey = ('outb', (gi * 4 + h) % 2)
            kb.tt(ob[:], yc[:], g_s[:], ALU.mult, ['yc', 'g_s'], [okey])
            kb.dma(ocT[h * 64:(h + 1) * 64, c0:c0 + GW], ob[:], [okey], ())


def gdn_inputs(kb, sfx=''):
    d = {}
    d['convw'] = kb.dram("gd_convw" + sfx, [128, 24], F32, "ExternalInput")
    d['hcols'] = kb.dram("gd_hcols" + sfx, [128, 5], F32, "ExternalInput")
    return d


def emit_gdn(kb, qkvT, zT, baT, d, obT, cst, psum=None):
    GW = NG * CH
    NGRP = S // GW
    NCHK = S // CH
    convw, hcolsd = d['convw'], d['hcols']
    dl = Delta(kb, 128, 128, True, cst.ident, 'dconst', cst.identrep3, psum)
    dl.cst = cst
    cw = kb.sb([128, 24], F32)
    hc = kb.sb([128, 5], F32)
    ab = kb.sb([64, 4 * NCHK], F32)
    kb.dma(cw[:], convw, (), ['cols'])
    kb.dma(hc[:], hcolsd, (), ['cols'])
    for hl in range(2):
        for which, row in ((0, 2 + hl), (1, hl)):
            j = 2 * hl + which
            kb.dma(ab[:, j * NCHK:(j + 1) * NCHK], baT[row, :].rearrange("(c i) -> i c", i=CH), (), ['ab'], slow=True)
    negA = kb.sb([128, 2], F32)
    for hl in range(2):
        kb.act(negA[:, hl:hl + 1], hc[:, 2 * hl:2 * hl + 1], AF.Exp, ['cols'], ['negA'])
    kb.ts(negA[:], negA[:], -1.0, None, ALU.mult, reads=['negA'], writes=['negA'])

    f = lambda r, w=GW, dt=F32: kb.sb([r, w], dt)
    Gtm = [f(64, NCHK) for _ in range(2)]
    eGtm = [f(64, NCHK) for _ in range(2)]
    coefP = [f(64, NCHK) for _ in range(2)]
    beta = [f(64, NCHK) for _ in range(2)]
    ttm = f(64, NCHK)
    for hl in range(2):
        a_tm = ab[:, (2 * hl) * NCHK:(2 * hl + 1) * NCHK]
        b_tm = ab[:, (2 * hl + 1) * NCHK:(2 * hl + 2) * NCHK]
        kb.act(ttm[:], a_tm, AF.Exp, ['ab', 'cols'], ['ttm'], bias=hc[0:64, 2 * hl + 1:2 * hl + 2])
        kb.act(ttm[:], ttm[:], AF.Ln, ['ttm'], ['ttm'], bias=1.0)
        kb.ts(ttm[:], ttm[:], negA[0:64, hl:hl + 1], None, ALU.mult, reads=['ttm', 'negA'], writes=['ttm'])
        psA, kA = dl.nb()
        kb.mm(psA[0:64, 0:NCHK], cst.maskR[:, 0:64], ttm[:], True, True, ['dconst', 'ttm'], [kA])
        kb.cp(Gtm[hl][:], psA[0:64, 0:NCHK], [kA], [('Gtm', hl)])
        kb.act(eGtm[hl][:], Gtm[hl][:], AF.Exp, [('Gtm', hl)], [('eGtm', hl)])
        psA, kA = dl.nb()
        kb.mm(psA[0:64, 0:NCHK], cst.sel63[:], Gtm[hl][:], True, True, ['dconst', ('Gtm', hl)], [kA])
        kb.tt(ttm[:], psA[0:64, 0:NCHK], Gtm[hl][:], ALU.subtract, [kA, ('Gtm', hl)], ['ttm'])
        kb.act(coefP[hl][:], ttm[:], AF.Exp, ['ttm'], [('coefP', hl)])
        kb.act(beta[hl][:], b_tm, AF.Sigmoid, ['ab'], [('beta', hl)])
        kb.tt(coefP[hl][:], coefP[hl][:], beta[hl][:], ALU.mult, [('coefP', hl), ('beta', hl)], [('coefP', hl)])

    X3 = [kb.sb([128, GW + 3], F32) for _ in range(3)]
    cv = [f(128) for _ in range(3)]
    sq = f(128); rs = f(128); tmp = f(128)
    abt = f(128); Gbc = f(128); eG = f(128)
    kn_b = f(128, GW, BF16); vc_b = f(128, GW, BF16)
    QR = kb.sb([128, NG, 128], BF16)
    RbT = kb.sb([128, NG, CH], BF16)
    gam = kb.sb([128, NG], F32)
    E = kb.sb([64, NG, CH], F32); FA = kb.sb([64, NG, CH], F32); FR = kb.sb([64, NG, CH], F32)
    Qb = kb.sb([64, NG, 128], BF16); Ph = kb.sb([64, NG, 128], BF16); Vt = kb.sb([64, NG, 128], BF16)
    yT = [f(128) for _ in range(2)]
    states = [dl.new_state() for _ in range(2)]
    v3 = lambda t: t[:].rearrange("p (g w) -> p g w", w=CH)

    def rsq(out, ps_ap, eps, reads, wkey):
        kb.ts(tmp[:], ps_ap, 1.0, eps, ALU.mult, ALU.add, reads=reads, writes=['tmp'])
        kb.act(tmp[:], tmp[:], AF.Ln, ['tmp'], ['tmp'])
        kb.act(out, tmp[:], AF.Exp, ['tmp'], [wkey], scale=-0.5)

    for gi in range(NGRP):
        c0 = gi * GW
        for hl in range(2):
            for part in range(3):
                r0 = (part * 2 + hl) * 128
                kb.dma(X3[part][:], qkvT[r0:r0 + 128, c0:c0 + GW + 3], (), [('X3', part)])
                wc = lambda j: cw[:, (part * 2 + hl) * 4 + j:(part * 2 + hl) * 4 + j + 1]
                kb.ts(cv[part][:], X3[part][:, 3:GW + 3], wc(3), None, ALU.mult, reads=[('X3', part), 'cols'],
                      writes=[('cv', part)])
                for j in (2, 1, 0):
                    kb.stt(cv[part][:], X3[part][:, j:GW + j], wc(j), cv[part][:], ALU.mult, ALU.add,
                           [('X3', part), 'cols', ('cv', part)], [('cv', part)])
                kb.act(cv[part][:], cv[part][:], AF.Silu, [('cv', part)], [('cv', part)])
            for part, dst3, scl in ((1, QR[:, :, 0:CH], 1.0), (0, QR[:, :, CH:128], float(128 ** -0.5))):
                kb.tt(sq[:], cv[part][:], cv[part][:], ALU.mult, [('cv', part)], ['sq'])
                psA, kA = dl.nb()
                kb.mm(psA[:, 0:GW], cst.ones[:], sq[:], True, True, ['dones', 'sq'], [kA])
                rsq(rs[:], psA[:, 0:GW], 1e-6, [kA], 'rs')
                kb.stt(cv[part][:], cv[part][:], scl, rs[:], ALU.mult, ALU.mult, [('cv', part), 'rs'], [('cv', part)])
                kb.cp(dst3, v3(cv[part]), [('cv', part)], ['QR'])
            kb.cp(kn_b[:], cv[1][:], [('cv', 1)], ['kn_b'])
            kb.dma(abt[:], baT[2 + hl, c0:c0 + GW].partition_broadcast(128), (), ['abt'])
            kb.act(abt[:], abt[:], AF.Exp, ['abt', 'cols'], ['abt'], bias=hc[:, 2 * hl + 1:2 * hl + 2])
            kb.act(abt[:], abt[:], AF.Ln, ['abt'], ['abt'], bias=1.0)
            kb.ts(abt[:], abt[:], negA[:, hl:hl + 1], None, ALU.mult, reads=['abt', 'negA'], writes=['abt'])
            kb.op('dve', lambda e: e.tensor_tensor_scan(Gbc[:], cst.rmask[:], abt[:], 0.0, ALU.mult, ALU.add),
                  ['abt', 'dconst'], ['Gbc'])
            kb.act(eG[:], Gbc[:], AF.Exp, ['Gbc'], ['eG'])
            kb.cp(gam[:], v3(eG)[:, :, CH - 1], ['eG'], ['gam'])
            kb.tt(RbT[:], v3(cv[0]), v3(eG), ALU.mult, [('cv', 0), 'eG'], ['RbT'])
            for c in range(NG):
                ci = gi * NG + c
                kb.ts(E[:, c, :], Gbc[0:64, c * CH:(c + 1) * CH], Gtm[hl][:, ci:ci + 1], 0.0, ALU.subtract, ALU.min,
                      reads=['Gbc', ('Gtm', hl)], writes=['E'])
            kb.act(E[:], E[:], AF.Exp, ['E'], ['E'])
            for c in range(NG):
                ci = gi * NG + c
                kb.stt(FA[:, c, :], E[:, c, :], beta[hl][:, ci:ci + 1], cst.maskA3[:, c, :], ALU.mult, ALU.mult,
                       ['E', ('beta', hl), 'dconst'], ['FA'])
                kb.stt(FR[:, c, :], E[:, c, :], beta[hl][:, ci:ci + 1], cst.maskR3[:, c, :], ALU.mult, ALU.mult,
                       ['E', ('beta', hl), 'dconst'], ['FR'])
            transpose_group(kb, dl, Qb[:], cv[1], 128, [('cv', 1), ('eGtm', hl)], 'Qb',
                            scale_cols=eGtm[hl][:, gi * NG:(gi + 1) * NG])
            transpose_group(kb, dl, Ph[:], cv[1], 128, [('cv', 1), ('coefP', hl)], 'Ph',
                            scale_cols=coefP[hl][:, gi * NG:(gi + 1) * NG])
            transpose_group(kb, dl, Vt[:], cv[2], 128, [('cv', 2)], 'Vt')
            yt = yT[(gi * 2 + hl) % 2]
            ykey = ('yT', (gi * 2 + hl) % 2)
            dl.group(states[hl], kn_b[:], None, QR, RbT[:], Qb, Ph, None, Vt, FA[:], FR[:], gam,
                     yt[:], ['kn_b', 'QR', 'RbT', 'Qb', 'Ph', 'Vt', 'gam', 'FA', 'FR', 'dconst'], ykey)
            kb.dma(abt[:], zT[hl * 128:(hl + 1) * 128, c0:c0 + GW], (), ['abt'])
            kb.act(abt[:], abt[:], AF.Silu, ['abt'], ['abt'])
            kb.tt(sq[:], yt[:], yt[:], ALU.mult, [ykey], ['sq'])
            psA, kA = dl.nb()
            kb.mm(psA[:, 0:GW], cst.ones[:], sq[:], True, True, ['dones', 'sq'], [kA])
            kb.ts(tmp[:], psA[:, 0:GW], 1.0 / 128, EPS, ALU.mult, ALU.add, reads=[kA], writes=['tmp'])
            kb.act(tmp[:], tmp[:], AF.Ln, ['tmp'], ['tmp'])
            kb.act(rs[:], tmp[:], AF.Exp, ['tmp'], ['rs'], scale=-0.5)
            kb.stt(yt[:], yt[:], hc[:, 4:5], rs[:], ALU.mult, ALU.mult, [ykey, 'rs', 'cols'], [ykey])
            kb.tt(yt[:], yt[:], abt[:], ALU.mult, [ykey, 'abt'], [ykey])
            kb.dma(obT[hl * 128:(hl + 1) * 128, c0:c0 + GW], yt[:], [ykey], ())


def build_stageC(final):
    kb = KB()
    xT = kb.dram("xT", [D, TOK], F32, "ExternalInput")
    oT = [kb.dram(n, [1024, TOK], F32, "ExternalInput") for n in ("oaT", "obT", "ocT")]
    colsd = kb.dram("cols", [128, 9 * KC], F32, "ExternalInput")
    w_g = kb.dram("w_g", [D, 3 * D], F32, "ExternalInput")
    w_br = kb.dram("w_branch", [3072, D], F32, "ExternalInput")
    w_out = kb.dram("w_out", [D, D], F32, "ExternalInput")
    w_gu = kb.dram("w_gate_up", [D, 2 * DFF], F32, "ExternalInput")
    w_dn = kb.dram("w_down", [DFF, D], F32, "ExternalInput")
    xo = kb.dram("xo", [D, TOK], F32, "ExternalOutput")
    cols = kb.sb([128, 9 * KC], F32)
    kb.dma(cols[:], colsd, (), ['cols'])
    c_ = lambda j: cols[:, j * KC:(j + 1) * KC]
    ca = dict(gt1=c_(0), nw2=c_(1), sc2=c_(2), sh2=c_(3), gt2=c_(4), fnw=c_(5), nw1=c_(6), sc1=c_(7), sh1=c_(8))
    emit_stageC(kb, TOK, xT, oT, ca, ['cols'], w_g, w_br, w_out, w_gu, w_dn, xo, final)
    return kb.finish()


def emit_stageC(kb, ntok, xT, oT, ca, cakeys, w_g, w_br, w_out, w_gu, w_dn, xo, final):
    T = TT
    NT = ntok // T
    cakeys = list(cakeys)
    ones = kb.sb([128, 128], F32)
    kb.memset(ones[:], 1.0, ['ones'])
    acol1 = kb.sb([128, KC], F32)
    kb.ts(acol1[:], ca['sc1'], 1.0, None, ALU.add, reads=cakeys, writes=['acol'])
    kb.tt(acol1[:], acol1[:], ca['nw1'], ALU.mult, ['acol'] + cakeys, ['acol'])
    bcol1 = ca['sh1']
    gt1 = ca['gt1']
    acol = kb.sb([128, KC], F32)
    kb.ts(acol[:], ca['sc2'], 1.0, None, ALU.add, reads=cakeys, writes=['acol'])
    kb.tt(acol[:], acol[:], ca['nw2'], ALU.mult, ['acol'] + cakeys, ['acol'])
    bcol = ca['sh2']
    gt2 = ca['gt2']
    fnw = ca['fnw']

    x = kb.sb([128, KC, T], F32)
    merged = kb.sb([128, KC, T], F32)
    mb = kb.sb([128, KC, T], BF16)
    stg = kb.sb([128, 8, T], F32)
    oi_b = kb.sb([128, 8, T], BF16)
    HK = DFF // 128 // 2
    act_b = kb.sb([128, HK, T], BF16)
    sg = [kb.sb([128, T], F32) for _ in range(2)]
    rstd = kb.sb([128, T], F32)
    ps_stat = kb.ps([128, T], F32)
    dn = Dense(kb, T)
    xv = xT.rearrange("(kc p) t -> p kc t", p=128)
    xov = xo.rearrange("(kc p) t -> p kc t", p=128)

    for t in range(NT):
        tsl = slice(t * T, (t + 1) * T)
        kb.dma(x[:], xv[:, :, tsl], (), ['x'])
        modulate_tile(kb, x, 'x', mb, 'mb', acol1, bcol1, ones, merged, 'merged', ps_stat, 'pstat', rstd, 'rstd', cakeys)

        def act_h(kc, tt_):
            return mb[:, kc, :], ['mb']

        for i in range(3):
            ov = oT[i].rearrange("(kc p) t -> p kc t", p=128)
            kb.dma(stg[:], ov[:, :, tsl], (), ['stg'])
            kb.cp(oi_b[:], stg[:], ['stg'], ['oi_b'])

            def act_fn(kc, tt_):
                return oi_b[:, kc, :], ['oi_b']

            for cb in range(0, D, 256):
                def evac_g(c0, m, ps_ap, pk, tt_):
                    j = c0 // 128
                    kb.act(sg[j][:], ps_ap, AF.Sigmoid, [pk], [('sg', j)])

                def evac(c0, m, ps_ap, pk, tt_, i=i, cb=cb):
                    j = c0 // 128
                    nc_ = (cb + c0) // 128
                    if i == 0:
                        kb.tt(merged[:, nc_, :], ps_ap, sg[j][:], ALU.mult, [pk, ('sg', j)], ['merged'])
                    else:
                        kb.tt(sg[j][:], ps_ap, sg[j][:], ALU.mult, [pk, ('sg', j)], [('sg', j)])
                        kb.tt(merged[:, nc_, :], merged[:, nc_, :], sg[j][:], ALU.add, ['merged', ('sg', j)],
                              ['merged'], eng='pool')
                dn.run(w_g, 0, KC, i * D + cb, 256, act_h, evac_g)
                dn.run(w_br, i * 1024, 8, cb, 256, act_fn, evac)
        for kc in range(KC):
            kb.cp(mb[:, kc, :], merged[:, kc, :], ['merged'], ['mb'], eng='act')
        def act_fn2(kc, tt_):
            return mb[:, kc, :], ['mb']

        def evac2(c0, m, ps_ap, pk, tt_):
            nc_ = c0 // 128
            kb.stt(x[:, nc_, :], ps_ap, gt1[:, nc_:nc_ + 1], x[:, nc_, :], ALU.mult, ALU.add, [pk, 'x'] + cakeys, ['x'])
        dn.run(w_out, 0, KC, 0, D, act_fn2, evac2)
        modulate_tile(kb, x, 'x', mb, 'mb', acol, bcol, ones, merged, 'merged', ps_stat, 'pstat', rstd, 'rstd', cakeys)
        for half in range(2):
            def evac_g(c0, m, ps_ap, pk, tt_):
                kb.act(act_b[:, c0 // 128, :], ps_ap, AF.Silu, [pk], [('actb', c0 // 128)])

            def evac_u(c0, m, ps_ap, pk, tt_):
                j = c0 // 128
                kb.tt(act_b[:, j, :], ps_ap, act_b[:, j, :], ALU.mult, [pk, ('actb', j)], [('actb', j)])
            dn.run(w_gu, 0, KC, half * HK * 128, HK * 128, act_fn2, evac_g)
            dn.run(w_gu, 0, KC, DFF + half * HK * 128, HK * 128, act_fn2, evac_u)

            def act_fn3(kc, tt_):
                return act_b[:, kc, :], [('actb', kc)]

            def evac3(c0, m, ps_ap, pk, tt_):
                nc_ = c0 // 128
                kb.stt(x[:, nc_, :], ps_ap, gt2[:, nc_:nc_ + 1], x[:, nc_, :], ALU.mult, ALU.add, [pk, 'x'] + cakeys, ['x'])
            dn.run(w_dn, half * HK * 128, HK, 0, D, act_fn3, evac3)
        if final:
            for k in range(KC):
                kb.act(merged[:, k, :], x[:, k, :], AF.Square, ['x'], ['merged'])
            for k in range(KC):
                kb.mm(ps_stat[:], ones[:], merged[:, k, :], k == 0, k == KC - 1, ['merged', 'ones'], ['pstat'])
            rstd_from_sumsq(kb, rstd[:], ps_stat[:], float(D), EPS, ['pstat'], ['rstd'], merged[:, 0, :], 'merged')
            for k in range(KC):
                kb.stt(x[:, k, :], x[:, k, :], fnw[:, k:k + 1], rstd[:], ALU.mult, ALU.mult, ['x', 'rstd'] + cakeys, ['x'])
        kb.dma(xov[:, :, tsl], x[:], ['x'], ())


NSEL = 3396


def build_AB():
    kb = KB()
    xT = kb.dram("xT", [D, S], F32, "ExternalInput")
    ncols = kb.dram("ncols", [128, 3 * KC], F32, "ExternalInput")
    wsel = kb.dram("wsel", [D, NSEL], F32, "ExternalInput")
    md = mla_inputs(kb)
    gd = gdn_inputs(kb)
    rd = rwkv_inputs(kb)
    cd = delta_const_inputs(kb)
    oaT = kb.dram("oaT", [256, S], F32, "ExternalOutput")
    obT = kb.dram("obT", [256, S], F32, "ExternalOutput")
    ocT = kb.dram("ocT", [256, S], F32, "ExternalOutput")
    pint = kb.dram("pint", [NSEL, S + 3], F32, "Internal")

    with kb.phase():
        TM = 256
        ones = kb.sb([128, 128], F32)
        kb.memset(ones[:], 1.0, ['ones'])
        zt = kb.sb([128, 3], F32)
        kb.memset(zt[:], 0.0, ['zt'])
        for r0 in range(0, NSEL, 128):
            m = min(128, NSEL - r0)
            kb.dma(pint[r0:r0 + m, 0:3], zt[0:m, :], ['zt'], ())
        cols = kb.sb([128, 3 * KC], F32)
        kb.dma(cols[:], ncols, (), ['cols'])
        acol = kb.sb([128, KC], F32)
        kb.ts(acol[:], cols[:, KC:2 * KC], 1.0, None, ALU.add, reads=['cols'], writes=['acol'])
        kb.tt(acol[:], acol[:], cols[:, 0:KC], ALU.mult, ['acol', 'cols'], ['acol'])
        bcol = cols[:, 2 * KC:3 * KC]
        wres = kb.sb([128, KC, NSEL], BF16)
        wstg = [kb.sb([128, KC, 128], F32) for _ in range(2)]
        for ci, c0 in enumerate(range(0, NSEL, 128)):
            cw = min(128, NSEL - c0)
            b_ = ci % 2
            kb.dma(wstg[b_][:, :, 0:cw], wsel[:, c0:c0 + cw].rearrange("(kc p) n -> p kc n", p=128), (), [('wstg', b_)])
            kb.cp(wres[:, :, c0:c0 + cw], wstg[b_][:, :, 0:cw], [('wstg', b_)], ['wres'], eng=('pool' if b_ else 'act'))
        xt = [kb.sb([128, KC, TM], F32) for _ in range(2)]
        scr = kb.sb([128, KC, TM], F32)
        hT = [kb.sb([128, KC, TM], BF16) for _ in range(2)]
        rstd = kb.sb([128, TM], F32)
        ps_stat = kb.ps([128, TM], F32)
        pp = [kb.ps([128, 512], F32) for _ in range(4)]
        ob = [kb.sb([128, TM], F32) for _ in range(4)]
        xv = xT.rearrange("(kc p) t -> p kc t", p=128)
        it = 0
        for t in range(S // TM):
            bi = t % 2
            kb.dma(xt[bi][:], xv[:, :, t * TM:(t + 1) * TM], (), [('x', bi)])
            modulate_tile(kb, xt[bi], ('x', bi), hT[bi], ('h', bi), acol, bcol, ones, scr, 'scr', ps_stat, 'pstat',
                          rstd, 'rstd')
            for c0 in range(0, NSEL, 128):
                m = min(128, NSEL - c0)
                j = it % 4
                it += 1
                for k in range(KC):
                    kb.mm(pp[j][0:m, 0:TM], wres[:, k, c0:c0 + m], hT[bi][:, k, :], k == 0, k == KC - 1,
                          ['wres', ('h', bi)], [('pp', j)])
                kb.cp(ob[j][0:m, :], pp[j][0:m, 0:TM], [('pp', j)], [('ob', j)], eng=('act' if j % 2 else 'dve'))
                kb.dma(pint[c0:c0 + m, 3 + t * TM:3 + (t + 1) * TM], ob[j][0:m, :], [('ob', j)], ())
    with kb.phase():
        emit_mla(kb, pint, md, oaT)
    with kb.phase():
        kb.set_stream(0)
        cst0 = DeltaConsts(kb, cd)
        emit_gdn(kb, pint[1344:2112, :], pint[2112:2368, 3:3 + S], pint[2368:2372, 3:3 + S], gd, obT, cst0,
                 psum=delta_psum(kb))
        for si, hs_ in ((1, (0, 1)), (2, (2, 3))):
            kb.set_stream(si)
            cst1 = DeltaConsts(kb, cd)
            emit_rwkv(kb, pint[2372:3140, 2:3 + S], pint[3140:3396, 2:3 + S], rd, ocT, cst1, heads=hs_,
                      psum=delta_psum(kb))
        kb.set_stream(0)
    return kb.finish()


_PROGS = {}
N_LAUNCH = [0]
FUSED = False


def prog(name, builder):
    if name not in _PROGS:
        _PROGS[name] = builder()
    return _PROGS[name]


def launch(name, builder, ims):
    N_LAUNCH[0] += 1
    return run(prog(name, builder), ims)


OFF_G = 1344
OFF_R = 1344 + 4112
OFF_PG = IN_W - 3 * D


def sel_columns(g):
    idx = list(range(0, 1344))
    hs = [2 * g, 2 * g + 1]
    for part in range(3):
        for h in hs:
            idx += list(range(OFF_G + part * 1024 + h * 128, OFF_G + part * 1024 + (h + 1) * 128))
    for h in hs:
        idx += list(range(OFF_G + 3072 + h * 128, OFF_G + 3072 + (h + 1) * 128))
    idx += [OFF_G + 4096 + h for h in hs] + [OFF_G + 4104 + h for h in hs]
    rh = [4 * g + i for i in range(4)]
    for part in range(3):
        for h in rh:
            idx += list(range(OFF_R + part * 1024 + h * 64, OFF_R + part * 1024 + (h + 1) * 64))
    idx += list(range(OFF_R + 3072, OFF_R + 3328))
    assert len(idx) == NSEL
    return np.array(idx)


def ab_inputs(inp, l, b, g, xTb, mod, cc, ss, consts):
    f32 = np.float32
    d = {}
    d['xT'] = xTb
    d['ncols'] = np.ascontiguousarray(np.concatenate(
        [fm_vec(inp['norm1_w'][l]), fm_vec(mod[b, l, 1]), fm_vec(mod[b, l, 0])], axis=1), dtype=f32)
    d['wsel'] = np.ascontiguousarray(inp['w_in'][l][:, sel_columns(g)])
    hs = [2 * g, 2 * g + 1]
    wuq = inp['mla_w_uq'][l].reshape(768, 8, 192)
    wukv = inp['mla_w_ukv'][l].reshape(512, 8, 256)
    wr = wuq[:, hs, 128:]
    wrs = np.concatenate([wr[:, :, 32:], wr[:, :, :32]], axis=2)
    d.update(cc=cc[b], ss=ss[b], qnw=fm_vec(inp['mla_q_norm_w'][l]), kvnw=fm_vec(inp['mla_kv_norm_w'][l]),
             wq_n=np.ascontiguousarray(wuq[:, hs, :128].reshape(768, 256)),
             wq_r=np.ascontiguousarray(wr.reshape(768, 128)), wq_rs=np.ascontiguousarray(wrs.reshape(768, 128)),
             wk=np.ascontiguousarray(wukv[:, hs, :128].reshape(512, 256)),
             wv=np.ascontiguousarray(wukv[:, hs, 128:].reshape(512, 256)), mask=consts['mask'])
    convw = np.zeros((128, 24), f32)
    cwl = inp['gdn_conv_w'][l]
    for part in range(3):
        for i, h in enumerate(hs):
            for j in range(4):
                convw[:, (part * 2 + i) * 4 + j] = cwl[j, part * 1024 + h * 128: part * 1024 + (h + 1) * 128]
    hcols = np.zeros((128, 5), f32)
    for i, h in enumerate(hs):
        hcols[:, 2 * i] = inp['gdn_a_log'][l][h]
        hcols[:, 2 * i + 1] = inp['gdn_dt_bias'][l][h]
    hcols[:, 4] = inp['gdn_norm_w'][l]
    d.update(gd_convw=convw, gd_hcols=hcols)
    rh = [4 * g + i for i in range(4)]
    mu = inp['rwkv_mu'][l]
    W = 1024
    cols = np.zeros((64, 40), f32)
    for part in range(3):
        for i, h in enumerate(rh):
            cols[:, part * 4 + i] = mu[part * W + h * 64: part * W + (h + 1) * 64]
    for j, nm in enumerate(['rwkv_w0', 'rwkv_a0', 'rwkv_k_k', 'rwkv_k_a', 'rwkv_r_k', 'rwkv_lnx_w', 'rwkv_lnx_b']):
        v = inp[nm][l].reshape(-1)
        for i, h in enumerate(rh):
            cols[:, 12 + j * 4 + i] = v[h * 64:(h + 1) * 64]
    mulow = np.zeros((128, 3), f32)
    mulow[:64, 0] = mu[3 * W:3 * W + 64]
    mulow[:64, 1] = mu[3 * W + 64:3 * W + 128]
    mulow[:, 2] = mu[3 * W + 128:]
    hc = np.concatenate([np.arange(h * 64, (h + 1) * 64) for h in rh])
    d.update(rw_cols=cols, rw_mulow=mulow, rw_w_up=np.ascontiguousarray(inp['rwkv_w_up'][l][:, hc]),
             rw_a_up=np.ascontiguousarray(inp['rwkv_a_up'][l][:, hc]),
             rw_g_up=np.ascontiguousarray(inp['rwkv_g_up'][l][:, hc]))
    for nm, _ in DCONST_SHAPES:
        d[nm] = consts[nm]
    return d


def run_prep(inp):
    c = inp['c']
    pos = inp['positions']
    cT = np.ascontiguousarray(c.T.reshape(KC, 128, B).transpose(1, 0, 2).reshape(128, KC * B))
    wall = np.concatenate([inp['w_ada'][l] for l in range(DEPTH)], axis=1)
    ball = np.concatenate([inp['b_ada'][l] for l in range(DEPTH)], axis=0)
    inv = (1.0 / (np.float32(10000.0) ** (np.arange(0, 64, 2, dtype=np.float32) / np.float32(64)))).astype(np.float32)
    cst = np.zeros((64, 2), np.float32)
    cst[:, 0] = np.tile(inv, 2)
    cst[:32, 1] = -1
    cst[32:, 1] = 1
    ims = []
    for i in range(NCORES):
        cs = slice(i * PREP_COLS, (i + 1) * PREP_COLS)
        ims.append(dict(cT=cT, wada=np.ascontiguousarray(wall[:, cs]),
                        bada=np.ascontiguousarray(np.broadcast_to(ball[cs], (B, PREP_COLS))),
                        pos=np.ascontiguousarray(np.broadcast_to(pos[i % B], (64, S))), cst=cst))
    res = launch('prep', build_prep, ims)
    mod = np.concatenate([r['mod'] for r in res], axis=1).reshape(B, DEPTH, 6, D)
    cc = [res[b]['cc'] for b in range(B)]
    ss = [res[b]['ss'] for b in range(B)]
    return mod, cc, ss


def run_layer(inp, l, xsh, mod, cc, ss, consts, final):
    xTb = [np.ascontiguousarray(np.concatenate(xsh[b * 4:(b + 1) * 4], axis=1)) for b in range(B)]
    ims = [ab_inputs(inp, l, i // 4, i % 4, xTb[i // 4], mod, cc, ss, consts) for i in range(NCORES)]
    res = launch('AB', build_AB, ims)
    wg = np.ascontiguousarray(inp['w_in'][l][:, OFF_PG:])
    ims = []
    for i in range(NCORES):
        b, q = i // 4, i % 4
        tsl = slice(q * TOK, (q + 1) * TOK)
        o = {nm: np.ascontiguousarray(np.concatenate([res[b * 4 + g][nm] for g in range(4)], axis=0)[:, tsl])
             for nm in ('oaT', 'obT', 'ocT')}
        cols = np.concatenate([fm_vec(mod[b, l, 2]), fm_vec(inp['norm2_w'][l]), fm_vec(mod[b, l, 4]),
                               fm_vec(mod[b, l, 3]), fm_vec(mod[b, l, 5]), fm_vec(inp['final_norm_w']),
                               fm_vec(inp['norm1_w'][l]), fm_vec(mod[b, l, 1]), fm_vec(mod[b, l, 0])], axis=1)
        ims.append(dict(xT=xsh[i], cols=np.ascontiguousarray(cols, dtype=np.float32), w_g=wg,
                        w_branch=inp['w_branch'][l], w_out=inp['w_out'][l], w_gate_up=inp['w_gate_up'][l],
                        w_down=inp['w_down'][l], **o))
    name = 'Cf' if final else 'C'
    res = launch(name, (lambda: build_stageC(True)) if final else (lambda: build_stageC(False)), ims)
    return [r['xo'] for r in res]


def make_consts():
    c = delta_consts()
    c['mask'] = causal_masks()
    return c


def kernel(**inp):
    inp = {k: np.asarray(v) for k, v in inp.items()}
    N_LAUNCH[0] = 0
    if FUSED:
        return kernel_fused(inp, DEPTH)
    return kernel_unfused(**inp)


def kernel_unfused(**inp):
    inp = {k: np.asarray(v) for k, v in inp.items()}
    N_LAUNCH[0] = 0
    mod, cc, ss = run_prep(inp)
    consts = make_consts()
    xf = inp['x'].reshape(B * S, D)
    xsh = [np.ascontiguousarray(xf[i * TOK:(i + 1) * TOK].T) for i in range(NCORES)]
    for l in range(DEPTH):
        xsh = run_layer(inp, l, xsh, mod, cc, ss, consts, final=(l == DEPTH - 1))
    out = np.concatenate([s_.T for s_ in xsh], axis=0).reshape(B, S, D)
    return np.ascontiguousarray(out, dtype=np.float32)


GBLK = NSEL - 1344
NPROJ = 1344 + 4 * GBLK


def emit_projA(kb, x_src, nw1, sc1, sh1, ckeys, wproj, pints):
    TM = 256
    ones = kb.sb([128, 128], F32)
    kb.memset(ones[:], 1.0, ['ones'])
    zt = kb.sb([128, 3], F32)
    kb.memset(zt[:], 0.0, ['zt'])
    for pt in pints:
        nr = pt.shape[0]
        for r0 in range(0, nr, 128):
            m = min(128, nr - r0)
            kb.dma(pt[r0:r0 + m, 0:3], zt[0:m, :], ['zt'], ())
    acol = kb.sb([128, KC], F32)
    kb.ts(acol[:], sc1, 1.0, None, ALU.add, reads=ckeys, writes=['acol'])
    kb.tt(acol[:], acol[:], nw1, ALU.mult, ['acol'] + ckeys, ['acol'])
    wres = kb.sb([128, KC, NSEL], BF16)
    wstg = [kb.sb([128, KC, 128], F32) for _ in range(2)]
    xt = [kb.sb([128, KC, TM], F32) for _ in range(2)]
    scr = kb.sb([128, KC, TM], F32)
    hT = [kb.sb([128, KC, TM], BF16) for _ in range(2)]
    rstd = kb.sb([128, TM], F32)
    ps_stat = kb.ps([128, TM], F32)
    pp = [kb.ps([128, 512], F32) for _ in range(4)]
    ob = [kb.sb([128, TM], F32) for _ in range(4)]
    xv = x_src.rearrange("(kc p) t -> p kc t", p=128)
    it = 0
    ci = 0
    passes = [(0, [(pints[0], 1344), (pints[1], GBLK)])] + \
             [(1344 + g * GBLK, [(pints[1 + g], GBLK)]) for g in range(1, 4)]
    for w0, segs in passes:
        pw = sum(n for _, n in segs)
        for c0 in range(0, pw, 128):
            cw = min(128, pw - c0)
            b_ = ci % 2
            ci += 1
            kb.dma(wstg[b_][:, :, 0:cw], wproj[:, w0 + c0:w0 + c0 + cw].rearrange("(kc p) n -> p kc n", p=128),
                   (), [('wstg', b_)])
            kb.cp(wres[:, :, c0:c0 + cw], wstg[b_][:, :, 0:cw], [('wstg', b_)], ['wres'],
                  eng=('pool' if b_ else 'act'))
        for t in range(S // TM):
            bi = t % 2
            kb.dma(xt[bi][:], xv[:, :, t * TM:(t + 1) * TM], (), [('x', bi)])
            modulate_tile(kb, xt[bi], ('x', bi), hT[bi], ('h', bi), acol, sh1, ones, scr, 'scr', ps_stat, 'pstat',
                          rstd, 'rstd', ckeys)
            soff = 0
            for pt, n in segs:
                for c0 in range(0, n, 128):
                    m = min(128, n - c0)
                    j = it % 4
                    it += 1
                    for k in range(KC):
                        kb.mm(pp[j][0:m, 0:TM], wres[:, k, soff + c0:soff + c0 + m], hT[bi][:, k, :], k == 0,
                              k == KC - 1, ['wres', ('h', bi)], [('pp', j)])
                    kb.cp(ob[j][0:m, :], pp[j][0:m, 0:TM], [('pp', j)], [('ob', j)], eng=('act' if j % 2 else 'dve'))
                    kb.dma(pt[c0:c0 + m, 3 + t * TM:3 + (t + 1) * TM], ob[j][0:m, :], [('ob', j)], ())
                soff += n


def emit_rope(kb, pos, cst, cc, ss):
    CHW = 2048
    cs = kb.sb([64, 2], F32)
    kb.dma(cs[:], cst, (), ['cs'])
    pi = kb.sb([64, CHW], I32)
    ang = kb.sb([64, CHW], F32)
    r = kb.sb([64, CHW], F32)
    ki = kb.sb([64, CHW], I32)
    kf = kb.sb([64, CHW], F32)
    m = kb.sb([64, CHW], F32)
    osb = {'s': kb.sb([64, CHW], F32), 'c': kb.sb([64, CHW], F32)}
    k_ = 'rope'
    for ci in range(S // CHW):
        kb.dma(pi[:], pos[:, ci * CHW:(ci + 1) * CHW], (), [k_])
        kb.cp(ang[:], pi[:], [k_], [k_])
        kb.ts(ang[:], ang[:], cs[:, 0:1], None, ALU.mult, reads=[k_, 'cs'], writes=[k_])
        kb.ts(kf[:], ang[:], float(1.0 / (2 * np.pi)), None, ALU.mult, reads=[k_], writes=[k_])
        kb.cp(ki[:], kf[:], [k_], [k_])
        kb.cp(kf[:], ki[:], [k_], [k_])
        kb.stt(ang[:], kf[:], -6.28125, ang[:], ALU.mult, ALU.add, [k_], [k_])
        kb.stt(ang[:], kf[:], -0.0019353071795864769, ang[:], ALU.mult, ALU.add, [k_], [k_])
        for which, shift, dst in (('s', 0.0, ss), ('c', float(np.pi / 2), cc)):
            kb.ts(r[:], ang[:], shift, None, ALU.add, reads=[k_], writes=[k_])
            kb.ts(m[:], r[:], float(np.pi), None, ALU.is_gt, reads=[k_], writes=[k_])
            kb.stt(r[:], m[:], float(-2 * np.pi), r[:], ALU.mult, ALU.add, [k_], [k_])
            kb.ts(m[:], r[:], float(-np.pi), None, ALU.is_lt, reads=[k_], writes=[k_])
            kb.stt(r[:], m[:], float(2 * np.pi), r[:], ALU.mult, ALU.add, [k_], [k_])
            kb.ts(r[:], r[:], 3.1415925, -3.1415925, ALU.min, ALU.max, reads=[k_], writes=[k_])
            o = osb[which]
            kb.act(o[:], r[:], AF.Sin, [k_], [(k_, which)])
            if which == 's':
                kb.ts(o[:], o[:], cs[:, 1:2], None, ALU.mult, reads=[(k_, which), 'cs'], writes=[(k_, which)])
            kb.dma(dst[:, ci * CHW:(ci + 1) * CHW], o[:], [(k_, which)], ())


def build_fused(nlayers=DEPTH):
    kb = KB()
    NM = DEPTH * 96
    xT = kb.dram("xT", [D, S], F32, "ExternalInput")
    cTd = kb.dram("cT", [128, KC], F32, "ExternalInput")
    wada = kb.dram("wada", [DEPTH * D, 6 * D], F32, "ExternalInput")
    badad = kb.dram("bada", [128, NM], F32, "ExternalInput")
    pos = kb.dram("pos", [64, S], I32, "ExternalInput")
    cstd = kb.dram("cst", [64, 2], F32, "ExternalInput")
    fnwd = kb.dram("fnw", [128, KC], F32, "ExternalInput")
    maskd = kb.dram("mask", [128, 2048], F32, "ExternalInput")
    cd = delta_const_inputs(kb)
    L = []
    for l in range(nlayers):
        e = {}
        e['lcols'] = kb.dram("lcols_%d" % l, [128, 2 * KC], F32, "ExternalInput")
        e['wproj'] = kb.dram("wproj_%d" % l, [D, NPROJ], F32, "ExternalInput")
        e['w_g'] = kb.dram("w_g_%d" % l, [D, 3 * D], F32, "ExternalInput")
        e['w_br'] = kb.dram("w_branch_%d" % l, [3072, D], F32, "ExternalInput")
        e['w_out'] = kb.dram("w_out_%d" % l, [D, D], F32, "ExternalInput")
        e['w_gu'] = kb.dram("w_gate_up_%d" % l, [D, 2 * DFF], F32, "ExternalInput")
        e['w_dn'] = kb.dram("w_down_%d" % l, [DFF, D], F32, "ExternalInput")
        e['qnw'] = kb.dram("qnw_%d" % l, [128, 6], F32, "ExternalInput")
        e['kvnw'] = kb.dram("kvnw_%d" % l, [128, 4], F32, "ExternalInput")
        e['g'] = []
        for g in range(4):
            sfx = "_%d_%d" % (l, g)
            m = {}
            for nm, shp in (('wq_n', [768, 256]), ('wq_r', [768, 128]), ('wq_rs', [768, 128]), ('wk', [512, 256]),
                            ('wv', [512, 256])):
                m[nm] = kb.dram(nm + sfx, shp, F32, "ExternalInput")
            e['g'].append(dict(mla=m, gdn=gdn_inputs(kb, sfx), rwkv=rwkv_inputs(kb, sfx)))
        L.append(e)
    xo = kb.dram("xo", [D, S], F32, "ExternalOutput")
    pints = [kb.dram("pint_m", [1344, S + 3], F32, "Internal")] + \
            [kb.dram("pint_g%d" % g, [GBLK, S + 3], F32, "Internal") for g in range(4)]
    cc_d = kb.dram("cc_d", [64, S], F32, "Internal")
    ss_d = kb.dram("ss_d", [64, S], F32, "Internal")
    xbuf = [kb.dram("xbuf%d" % i, [D, S], F32, "Internal") for i in range(2)]
    o_d = [kb.dram("o_d%d" % i, [1024, S], F32, "Internal") for i in range(3)]

    modcol = kb.sb([128, NM], F32)
    fnw_s = kb.sb([128, KC], F32)
    lc_s = kb.sb([128, nlayers * 2 * KC], F32)
    with kb.phase():
        kb.dma(fnw_s[:], fnwd, (), ['pc'])
        for l in range(nlayers):
            kb.dma(lc_s[:, l * 2 * KC:(l + 1) * 2 * KC], L[l]['lcols'], (), ['pc'])
        c_sb = kb.sb([128, KC], F32)
        b_sb = kb.sb([128, NM], F32)
        kb.dma(c_sb[:], cTd, (), ['c'])
        kb.dma(b_sb[:], badad, (), ['b'])
        kb.act(c_sb[:], c_sb[:], AF.Silu, ['c'], ['c'])
        wt = [kb.sb([128, KC, 512], F32) for _ in range(2)]
        pp = [kb.ps([128, 512], F32) for _ in range(2)]
        si = 0
        for l in range(DEPTH):
            for s0 in range(0, 6 * D, 512):
                bi = si % 2
                si += 1
                kb.dma(wt[bi][:], wada[l * D:(l + 1) * D, s0:s0 + 512].rearrange("(kc p) n -> p kc n", p=128),
                       (), [('w', bi)])
                for fj in range(4):
                    col = s0 // 128 + fj
                    for k in range(KC):
                        kb.mm(pp[l % 2][:, col:col + 1], wt[bi][:, k, fj * 128:(fj + 1) * 128], c_sb[:, k:k + 1],
                              k == 0, k == KC - 1, ['c', ('w', bi)], [('pp', l % 2)])
            kb.tt(modcol[:, l * 96:(l + 1) * 96], pp[l % 2][:, 0:96], b_sb[:, l * 96:(l + 1) * 96], ALU.add,
                  [('pp', l % 2), 'b'], ['modcol'])
        emit_rope(kb, pos, cstd, cc_d, ss_d)
    mc = lambda l, j: modcol[:, (l * 6 + j) * KC:(l * 6 + j + 1) * KC]
    for l in range(nlayers):
        e = L[l]
        final = (l == nlayers - 1)
        x_src = xT if l == 0 else xbuf[(l - 1) % 2]
        x_dst = xo if final else xbuf[l % 2]
        nw1 = lc_s[:, l * 2 * KC:l * 2 * KC + KC]
        nw2 = lc_s[:, l * 2 * KC + KC:(l + 1) * 2 * KC]
        with kb.phase():
            emit_projA(kb, x_src, nw1, mc(l, 1), mc(l, 0), [], e['wproj'], pints)
        for g in range(4):
            with kb.phase():
                md = dict(e['g'][g]['mla'])
                md.update(cc=cc_d, ss=ss_d, qnw=e['qnw'], kvnw=e['kvnw'], mask=maskd)
                emit_mla(kb, pints[0], md, o_d[0][g * 256:(g + 1) * 256, :])
        for g in range(4):
            pint = pints[1 + g]
            with kb.phase():
                kb.set_stream(0)
                cst0 = DeltaConsts(kb, cd)
                emit_gdn(kb, pint[0:768, :], pint[768:1024, 3:3 + S], pint[1024:1028, 3:3 + S],
                         e['g'][g]['gdn'], o_d[1][g * 256:(g + 1) * 256, :], cst0, psum=delta_psum(kb))
                for si, hs_ in ((1, (0, 1)), (2, (2, 3))):
                    kb.set_stream(si)
                    cst1 = DeltaConsts(kb, cd)
                    emit_rwkv(kb, pint[1028:1796, 2:3 + S], pint[1796:2052, 2:3 + S],
                              e['g'][g]['rwkv'], o_d[2][g * 256:(g + 1) * 256, :], cst1, heads=hs_,
                              psum=delta_psum(kb))
                kb.set_stream(0)
        with kb.phase():
            ca = dict(gt1=mc(l, 2), nw2=nw2, sc2=mc(l, 4), sh2=mc(l, 3), gt2=mc(l, 5), fnw=fnw_s[:], nw1=nw1,
                      sc1=mc(l, 1), sh1=mc(l, 0))
            emit_stageC(kb, S, x_src, o_d, ca, [], e['w_g'], e['w_br'], e['w_out'], e['w_gu'], e['w_dn'], x_dst, final)
    return kb.finish()


def fused_inputs(inp, b, consts, nlayers=DEPTH):
    f32 = np.float32
    d = {}
    d['xT'] = np.ascontiguousarray(inp['x'][b].T)
    d['cT'] = fm_vec(inp['c'][b])
    d['wada'] = np.ascontiguousarray(inp['w_ada'].reshape(DEPTH * D, 6 * D))
    d['bada'] = np.ascontiguousarray(np.concatenate(
        [fm_vec(inp['b_ada'][l, j * D:(j + 1) * D]) for l in range(DEPTH) for j in range(6)], axis=1), dtype=f32)
    d['pos'] = np.ascontiguousarray(np.broadcast_to(inp['positions'][b], (64, S)))
    inv = (1.0 / (np.float32(10000.0) ** (np.arange(0, 64, 2, dtype=np.float32) / np.float32(64)))).astype(np.float32)
    cst = np.zeros((64, 2), f32)
    cst[:, 0] = np.tile(inv, 2)
    cst[:32, 1] = -1
    cst[32:, 1] = 1
    d['cst'] = cst
    d['fnw'] = fm_vec(inp['final_norm_w'])
    d['mask'] = consts['mask']
    for nm, _ in DCONST_SHAPES:
        d[nm] = consts[nm]
    dummy_mod = np.zeros((B, DEPTH, 6, D), f32)
    for l in range(nlayers):
        d['lcols_%d' % l] = np.ascontiguousarray(
            np.concatenate([fm_vec(inp['norm1_w'][l]), fm_vec(inp['norm2_w'][l])], axis=1), dtype=f32)
        cols = np.concatenate([np.arange(1344)] + [sel_columns(g)[1344:] for g in range(4)])
        d['wproj_%d' % l] = np.ascontiguousarray(inp['w_in'][l][:, cols])
        d['w_g_%d' % l] = np.ascontiguousarray(inp['w_in'][l][:, OFF_PG:])
        d['w_branch_%d' % l] = inp['w_branch'][l]
        d['w_out_%d' % l] = inp['w_out'][l]
        d['w_gate_up_%d' % l] = inp['w_gate_up'][l]
        d['w_down_%d' % l] = inp['w_down'][l]
        d['qnw_%d' % l] = fm_vec(inp['mla_q_norm_w'][l])
        d['kvnw_%d' % l] = fm_vec(inp['mla_kv_norm_w'][l])
        for g in range(4):
            sfx = "_%d_%d" % (l, g)
            a = ab_inputs(inp, l, b, g, None, dummy_mod, [None] * B, [None] * B, consts)
            for nm in ('wq_n', 'wq_r', 'wq_rs', 'wk', 'wv', 'gd_convw', 'gd_hcols', 'rw_cols', 'rw_mulow',
                       'rw_w_up', 'rw_a_up', 'rw_g_up'):
                d[nm + sfx] = a[nm]
    return d


def kernel_fused(inp, nlayers=DEPTH):
    consts = make_consts()
    ims = [fused_inputs(inp, b, consts, nlayers) for b in range(B)]
    res = launch('fused%d' % nlayers, lambda: build_fused(nlayers), ims)
    out = np.stack([r['xo'].T for r in res], axis=0)
    return np.ascontiguousarray(out, dtype=np.float32)
```
